# Optimizing a Trainium2 kernel written in Bass

```python
import jax, jax.numpy as jnp
from jax import lax
import numpy as np

D_MODEL = 2048
BATCH = 4
SEQ = 2048
DEPTH = 1
DEC_BATCH = 128
DEC_SEQ = 8
PAST_LEN = 16384
PAGE_SIZE = 128

HEAD_DIM = 64
ATTN_WIDTH = D_MODEL // 2
N_Q_HEADS = ATTN_WIDTH // HEAD_DIM
N_KV_HEADS = N_Q_HEADS // 4
Q_PER_KV = N_Q_HEADS // N_KV_HEADS
KV_WIDTH = N_KV_HEADS * HEAD_DIM
WINDOW = 128
BLOCK = WINDOW
SG_WIDTH = D_MODEL - ATTN_WIDTH
N_SG_HEADS = 8
SG_HEAD_DIM = SG_WIDTH // N_SG_HEADS
CHUNK = 128
D_FF = ((8 * D_MODEL // 3 + 255) // 256) * 256
IN_WIDTH = ATTN_WIDTH + 2 * KV_WIDTH + 2 * SG_WIDTH
ROPE_THETA = 10000.0
EPS = 1e-6

kernel_name = "hymba_swa_sink_gmlp_decode_step"


def rmsnorm(x, g):
    xf = x.astype(jnp.float32)
    r = lax.rsqrt(jnp.mean(xf * xf, axis=-1, keepdims=True) + EPS)
    return (xf * r * g.astype(jnp.float32)).astype(x.dtype)


def rope(x, pos):
    half = HEAD_DIM // 2
    inv = ROPE_THETA ** (-jnp.arange(half, dtype=jnp.float32) / half)
    ang = pos.astype(jnp.float32)[:, None] * inv[None, :]
    cos = jnp.cos(ang)[:, None, :]
    sin = jnp.sin(ang)[:, None, :]
    x1 = x[..., :half].astype(jnp.float32)
    x2 = x[..., half:].astype(jnp.float32)
    out = jnp.concatenate([x1 * cos - x2 * sin, x2 * cos + x1 * sin], axis=-1)
    return out.astype(x.dtype)


def project(xn, pos, w_in, q_norm, k_norm, sg_norm):
    b, s, _ = xn.shape
    proj = jnp.einsum('bsd,de->bse', xn, w_in)
    cuts = [ATTN_WIDTH, ATTN_WIDTH + KV_WIDTH, ATTN_WIDTH + 2 * KV_WIDTH,
            ATTN_WIDTH + 2 * KV_WIDTH + SG_WIDTH]
    q, k, v, u, g = jnp.split(proj, cuts, axis=-1)
    q = rope(rmsnorm(q.reshape(b, s, N_Q_HEADS, HEAD_DIM), q_norm), pos)
    k = rope(rmsnorm(k.reshape(b, s, N_KV_HEADS, HEAD_DIM), k_norm), pos)
    v = v.reshape(b, s, N_KV_HEADS, HEAD_DIM)
    u = jax.nn.gelu(u)
    g = rmsnorm(jax.nn.gelu(g), sg_norm)
    return q, k, v, u, g


def sink_attention(q, k, v, mask, sinks):
    scale = HEAD_DIM ** -0.5
    s = jnp.einsum('...qhgd,...khd->...hgqk', q.astype(jnp.float32), k.astype(jnp.float32)) * scale
    s = jnp.where(mask, s, -jnp.inf)
    sink = sinks.astype(jnp.float32).reshape(N_KV_HEADS, Q_PER_KV, 1, 1)
    m = jnp.maximum(jnp.max(s, axis=-1, keepdims=True), sink)
    p = jnp.exp(s - m)
    denom = jnp.sum(p, axis=-1, keepdims=True) + jnp.exp(sink - m)
    o = jnp.einsum('...hgqk,...khd->...qhgd', p / denom, v.astype(jnp.float32))
    return o.astype(v.dtype)


def attention_prompt(q, k, v, sinks):
    b, s = q.shape[:2]
    nb = s // BLOCK
    qb = q.reshape(b, nb, BLOCK, N_KV_HEADS, Q_PER_KV, HEAD_DIM)

    def band(t):
        pad = jnp.zeros((b, BLOCK) + t.shape[2:], t.dtype)
        tb = jnp.concatenate([pad, t], axis=1).reshape((b, nb + 1, BLOCK) + t.shape[2:])
        return jnp.concatenate([tb[:, :-1], tb[:, 1:]], axis=2)

    qi = jnp.arange(BLOCK)[:, None]
    kj = jnp.arange(2 * BLOCK)[None, :]
    diff = BLOCK + qi - kj
    key_pos = (jnp.arange(nb)[:, None, None] - 1) * BLOCK + kj[None]
    mask = (diff >= 0) & (diff < WINDOW) & (key_pos >= 0)
    o = sink_attention(qb, band(k), band(v), mask[:, None, None], sinks)
    return o.reshape(b, s, ATTN_WIDTH)


def attention_sample(q, k, v, ck, cv, sinks):
    b, s = q.shape[:2]
    k_all = jnp.concatenate([ck, k], axis=1)
    v_all = jnp.concatenate([cv, v], axis=1)
    qi = jnp.arange(s)[:, None]
    kj = jnp.arange(WINDOW + s)[None, :]
    diff = WINDOW + qi - kj
    mask = (diff >= 0) & (diff < WINDOW)
    o = sink_attention(q.reshape(b, s, N_KV_HEADS, Q_PER_KV, HEAD_DIM), k_all, v_all, mask, sinks)
    return o.reshape(b, s, ATTN_WIDTH), k_all[:, -WINDOW:], v_all[:, -WINDOW:]


def spatial_gate(u, g, sg_w, sg_b):
    b, s, _ = u.shape
    rows = min(s, CHUNK)
    nc = s // rows
    causal = jnp.tril(jnp.ones((rows, rows), dtype=bool))
    w = jnp.where(causal, sg_w[:, :rows, :rows], 0.0)
    gc = g.reshape(b, nc, rows, N_SG_HEADS, SG_HEAD_DIM)
    mixed = jnp.einsum('hij,bnjhd->bnihd', w, gc) + sg_b[:, :rows].T[:, :, None]
    return u * mixed.reshape(b, s, SG_WIDTH)


def merge_and_ffn(h, a, sgo, attn_out_norm, sg_out_norm, w_o, ffn_norm, w_gate, w_up, w_down):
    mix = jnp.concatenate([rmsnorm(a, attn_out_norm), rmsnorm(sgo, sg_out_norm)], axis=-1)
    h = h + jnp.einsum('bse,ed->bsd', mix, w_o)
    hn = rmsnorm(h, ffn_norm)
    act = jax.nn.silu(jnp.einsum('bsd,df->bsf', hn, w_gate)) * jnp.einsum('bsd,df->bsf', hn, w_up)
    return h + jnp.einsum('bsf,fd->bsd', act, w_down)


def setup_inputs(seed: int = 0) -> dict:
    key = jax.random.key(seed)
    ks = jax.random.split(key, 20)
    f32 = jnp.float32
    L = DEPTH

    def nrm(k, shape, scale=1.0):
        return jax.random.normal(k, shape, f32) * scale

    return {
        "x_prompt": nrm(ks[0], (BATCH, SEQ, D_MODEL)),
        "x_sample": nrm(ks[1], (DEC_BATCH, DEC_SEQ, D_MODEL)),
        "cache_k_win": nrm(ks[2], (L, DEC_BATCH, WINDOW, N_KV_HEADS, HEAD_DIM)),
        "cache_v_win": nrm(ks[3], (L, DEC_BATCH, WINDOW, N_KV_HEADS, HEAD_DIM)),
        "attn_norm": 1.0 + nrm(ks[4], (L, D_MODEL), 0.02),
        "w_in": nrm(ks[5], (L, D_MODEL, IN_WIDTH), D_MODEL ** -0.5),
        "q_norm": 1.0 + nrm(ks[6], (L, HEAD_DIM), 0.02),
        "k_norm": 1.0 + nrm(ks[7], (L, HEAD_DIM), 0.02),
        "sinks": nrm(ks[8], (L, N_Q_HEADS), 0.5),
        "sg_norm": 1.0 + nrm(ks[9], (L, SG_WIDTH), 0.02),
        "sg_w": nrm(ks[10], (L, N_SG_HEADS, CHUNK, CHUNK), CHUNK ** -0.5),
        "sg_b": 1.0 + nrm(ks[11], (L, N_SG_HEADS, CHUNK), 0.1),
        "attn_out_norm": 1.0 + nrm(ks[12], (L, ATTN_WIDTH), 0.02),
        "sg_out_norm": 1.0 + nrm(ks[13], (L, SG_WIDTH), 0.02),
        "w_o": nrm(ks[14], (L, D_MODEL, D_MODEL), D_MODEL ** -0.5),
        "ffn_norm": 1.0 + nrm(ks[15], (L, D_MODEL), 0.02),
        "w_gate": nrm(ks[16], (L, D_MODEL, D_FF), D_MODEL ** -0.5),
        "w_up": nrm(ks[17], (L, D_MODEL, D_FF), D_MODEL ** -0.5),
        "w_down": nrm(ks[18], (L, D_FF, D_MODEL), D_FF ** -0.5),
    }


def reference(x_prompt, x_sample, cache_k_win, cache_v_win, attn_norm, w_in, q_norm, k_norm,
              sinks, sg_norm, sg_w, sg_b, attn_out_norm, sg_out_norm, w_o, ffn_norm,
              w_gate, w_up, w_down):
    pos_p = jnp.arange(x_prompt.shape[1], dtype=jnp.int32)
    pos_s = PAST_LEN + jnp.arange(x_sample.shape[1], dtype=jnp.int32)
    hp, hs = x_prompt, x_sample
    kp_list, vp_list, ks_list, vs_list, gs_list = [], [], [], [], []
    for l in range(DEPTH):
        q, k, v, u, g = project(rmsnorm(hp, attn_norm[l]), pos_p, w_in[l], q_norm[l], k_norm[l], sg_norm[l])
        a = attention_prompt(q, k, v, sinks[l])
        sgo = spatial_gate(u, g, sg_w[l], sg_b[l])
        hp = merge_and_ffn(hp, a, sgo, attn_out_norm[l], sg_out_norm[l], w_o[l], ffn_norm[l],
                           w_gate[l], w_up[l], w_down[l])
        kp_list.append(k[:, -WINDOW:])
        vp_list.append(v[:, -WINDOW:])
        q, k, v, u, g = project(rmsnorm(hs, attn_norm[l]), pos_s, w_in[l], q_norm[l], k_norm[l], sg_norm[l])
        a, k_new, v_new = attention_sample(q, k, v, cache_k_win[l], cache_v_win[l], sinks[l])
        sgo = spatial_gate(u, g, sg_w[l], sg_b[l])
        hs = merge_and_ffn(hs, a, sgo, attn_out_norm[l], sg_out_norm[l], w_o[l], ffn_norm[l],
                           w_gate[l], w_up[l], w_down[l])
        ks_list.append(k_new)
        vs_list.append(v_new)
        gs_list.append(g)
    k_win_prompt = jnp.stack(kp_list)
    v_win_prompt = jnp.stack(vp_list)
    k_win_sample = jnp.stack(ks_list)
    v_win_sample = jnp.stack(vs_list)
    sg_v_sample = jnp.stack(gs_list)
    return (hp, hs, k_win_prompt, v_win_prompt, k_win_sample, v_win_sample, sg_v_sample)
```

```python
import contextlib
import numpy as np
import concourse.bass as bass
import concourse.mybir as mybir
from concourse.bass_utils import run_bass_kernel_spmd

F32 = mybir.dt.float32
BF16 = mybir.dt.bfloat16
AF = mybir.ActivationFunctionType
ALU = mybir.AluOpType
AX = mybir.AxisListType

D = 2048
DC = 16
NT = 10
DFF = 5632
EPS = 1e-6
NEG = -240000.0
GROUPS = [([0, 1, 2, 3, 4], [1, 2, 3, 4]), ([5, 6, 7, 8, 9], [5, 6, 7, 8, 9])]
NG = 5
FFN_PARTS = [(0, 16), (16, 32), (32, 44)]


class Tr:
    def __init__(self):
        self.w = None
        self.r = []


class DSem:
    def __init__(self, sem):
        self.sem = sem
        self.n = 0

    def next(self):
        self.n += 16
        return ("dma", self.sem, self.n)


class Node:
    __slots__ = ("eng", "fns", "deps", "sig", "est", "tbl", "idx", "cnt", "fin", "dma")

    def __init__(self, eng, fns, deps, sig, est, tbl, idx):
        self.eng, self.fns, self.deps, self.sig, self.est, self.tbl, self.idx = eng, fns, deps, sig, est, tbl, idx
        self.cnt = None
        self.fin = None
        self.dma = None


SCHED = True
SCHED_WINDOW = 32


class Prog:
    ENG = ("pe", "act", "dve", "pool", "sp")

    def __init__(self):
        self.nodes = []
        self.bar = {e: [] for e in self.ENG}
        self.last = {e: None for e in self.ENG}
        self.ops = {e: [] for e in self.ENG}

    def _deps(self, reads, writes, extra):
        deps = []
        for t in reads:
            if t.w is not None:
                deps.append(t.w)
        for t in writes:
            if t.w is not None:
                deps.append(t.w)
            deps.extend(t.r)
        deps.extend([d for d in extra if d is not None])
        return deps

    def _finish(self, tok, reads, writes):
        for t in writes:
            t.w = tok
            t.r = []
        for t in reads:
            if t not in writes:
                t.r.append(tok)

    def _node(self, eng, fns, deps, sig):
        est = sum(getattr(f, "est", 0.1) for f in fns)
        tbl = getattr(fns[0], "tbl", None)
        n = Node(eng, fns, deps, sig, est, tbl, len(self.nodes))
        self.nodes.append(n)
        self.ops[eng].extend(fns)
        return n

    def op(self, eng, fn, reads=(), writes=(), extra=()):
        return self.group(eng, [fn], reads, writes, extra)

    def group(self, eng, fns, reads=(), writes=(), extra=()):
        deps = self._deps(reads, writes, extra) + self.bar[eng]
        self.bar[eng] = []
        n = self._node(eng, list(fns), deps, True)
        tok = ("n", n)
        self._finish(tok, reads, writes)
        self.last[eng] = tok
        return tok

    def dma(self, eng, out, in_, ds, reads=(), writes=(), extra=(), nbytes=65536):
        deps = self._deps(reads, writes, extra) + self.bar[eng]
        self.bar[eng] = []
        tok0 = ds.next()
        sem = ds.sem

        def fn(e, out=out, in_=in_, sem=sem):
            return e.dma_start(out=out, in_=in_).then_inc(sem, 16)

        fn.est = 1.4 if eng == "pool" else 0.1
        n = self._node(eng, [fn], deps, False)
        n.dma = nbytes
        tok = ("dma", tok0[1], tok0[2], n)
        self._finish(tok, reads, writes)
        return tok

    def barrier(self, engines=("pe", "act", "dve", "sp"), extra=()):
        toks = [self.last[e] for e in ("pe", "act", "dve") if self.last[e] is not None] + list(extra)
        for e in engines:
            self.bar[e] = list(toks)

    def final_wait(self, eng, fn, deps):
        n = self._node(eng, [fn], list(deps), False)
        return n

    def schedule(self):
        per = {e: [n for n in self.nodes if n.eng == e] for e in self.ENG}
        if not SCHED:
            return per
        pos = {e: 0 for e in self.ENG}
        pend = {e: list(per[e]) for e in self.ENG}
        free = {e: 0.0 for e in self.ENG}
        cur_tbl = [None]
        dma_free = [0.0]
        out = {e: [] for e in self.ENG}
        remaining = len(self.nodes)

        def dep_fin(d):
            nd = d[1] if d[0] == "n" else (d[3] if len(d) > 3 else None)
            if nd is None:
                return 0.0
            return nd.fin

        while remaining:
            best = None
            for e in self.ENG:
                lst = pend[e]
                if not lst:
                    continue
                W = SCHED_WINDOW if e in ("pe", "act", "dve") else 1
                seen = 0
                for n in lst:
                    if seen >= W:
                        break
                    seen += 1
                    ok = True
                    t = free[e]
                    for d in n.deps:
                        f = dep_fin(d)
                        if f is None:
                            ok = False
                            break
                        lat = 0.15 if (d[0] == "n" and d[1].eng != e) else 0.05
                        if f + lat > t:
                            t = f + lat
                    if not ok:
                        continue
                    if e == "act" and n.tbl is not None and n.tbl != cur_tbl[0]:
                        t += 1.3
                    key = (t, n.idx)
                    if best is None or key < best[0]:
                        best = (key, e, n)
            key, e, n = best
            t = key[0]
            if e == "act" and n.tbl is not None:
                cur_tbl[0] = n.tbl
            end = t + n.est + 0.08
            free[e] = end
            if n.dma is not None:
                st = max(end, dma_free[0])
                dma_free[0] = st + n.dma / 300e3
                n.fin = dma_free[0] + 2.0
            else:
                n.fin = end
            pend[e].remove(n)
            out[e].append(n)
            remaining -= 1
        self.makespan = max(free.values())
        return out

    def finalize(self):
        order = self.schedule()
        for e in self.ENG:
            c = 0
            for n in order[e]:
                if n.sig:
                    c += 1
                    n.cnt = c
        self.order = order

    def replay(self, name, e, sems):
        waited = {}
        for n in self.order[name]:
            for d in n.deps:
                if d[0] == "dma":
                    key = ("dma", id(d[1]))
                    if waited.get(key, 0) >= d[2]:
                        continue
                    e.wait_ge(d[1], d[2])
                    waited[key] = d[2]
                else:
                    src = d[1]
                    if waited.get(src.eng, 0) >= src.cnt:
                        continue
                    e.wait_ge(sems[src.eng], src.cnt)
                    waited[src.eng] = src.cnt
            ins = None
            for fn in n.fns:
                ins = fn(e)
            if n.sig:
                ins.then_inc(sems[name], 1)


def _free(ap):
    k = 1
    for d in ap.shape[1:]:
        k *= int(d)
    return k


def _est(fn, v, tbl=None):
    fn.est = v
    fn.tbl = tbl
    return fn


def MM(out, lhsT, rhs, start, stop):
    return _est(lambda e: e.matmul(out, lhsT=lhsT, rhs=rhs, start=start, stop=stop, skip_group_check=True),
                max(_free(rhs), 64) / 2400.0 + 0.015)


def TRP(out, in_, ident):
    return _est(lambda e: e.transpose(out=out, in_=in_, identity=ident), 0.075)


def ACTF(out, in_, func, scale=None, bias=None, accum=None):
    kw = {}
    if scale is not None:
        kw["scale"] = scale
    if bias is not None:
        kw["bias"] = bias
    if accum is not None:
        kw["accum_out"] = accum
    tbl = {AF.Exp: "exp", AF.Ln: "exp", AF.Gelu_apprx_tanh: "gelu", AF.Silu: "silu"}.get(func)
    return _est(lambda e: e.activation(out=out, in_=in_, func=func, **kw), 0.2 + _free(out) / 1050.0, tbl)


def TT(out, in0, in1, op):
    return _est(lambda e: e.tensor_tensor(out=out, in0=in0, in1=in1, op=op), 0.12 + _free(out) / 930.0)


def TS(out, in0, s1, op0, s2=None, op1=None):
    if op1 is None:
        return lambda e: e.tensor_scalar(out=out, in0=in0, scalar1=s1, scalar2=None, op0=op0)
    return lambda e: e.tensor_scalar(out=out, in0=in0, scalar1=s1, scalar2=s2, op0=op0, op1=op1)


def STT(out, in0, scalar, in1, op0, op1):
    return _est(lambda e: e.scalar_tensor_tensor(out=out, in0=in0, scalar=scalar, in1=in1, op0=op0, op1=op1), 0.12 + _free(out) / 930.0)


def RSUM(out, in_):
    return _est(lambda e: e.reduce_sum(out=out, in_=in_, axis=AX.X), 0.12 + _free(in_) / 930.0)


def RCP(out, in_):
    return _est(lambda e: e.reciprocal(out=out, in_=in_), 0.2)


def CP(out, in_):
    return _est(lambda e: e.tensor_copy(out=out, in_=in_), 0.12 + _free(out) / 1800.0)


def MSET(ap, v):
    return _est(lambda e: e.memset(ap, v), 0.05 + _free(ap) / 4000.0)


class _Stop(Exception):
    pass


MARKS = []
LAST_PROG = []


def build_program(debug=False, stop_after=None):
    nc = bass.Bass("TRN2", target_bir_lowering=False)
    dt_in = lambda name, shape: nc.dram_tensor(name, shape, F32, kind="ExternalInput").ap()
    dt_out = lambda name, shape: nc.dram_tensor(name, shape, F32, kind="ExternalOutput").ap()

    xs = dt_in("xs", [NT, 128, D])
    ck = dt_in("ck", [16, 128, 256])
    cv = dt_in("cv", [16, 128, 256])
    w_in = dt_in("w_in", [D, 3584])
    w_o = dt_in("w_o", [D, D])
    w_gate = dt_in("w_gate", [D, DFF])
    w_up = dt_in("w_up", [D, DFF])
    w_down = dt_in("w_down", [DFF, D])
    cols_d = dt_in("cols", [128, 64])
    bcv_d = dt_in("bcv", [1168])
    cs_d = dt_in("cs", [128, NT, 128])
    mk_d = dt_in("mk", [128, 2304])
    wt_d = dt_in("wt", [128, 2048])
    cm_d = dt_in("cm", [128, 256])

    y_o = dt_out("y", [9, 128, D])
    kwp_o = dt_out("kwp", [128, 256])
    vwp_o = dt_out("vwp", [128, 256])
    kws_o = dt_out("kws", [16, 120, 256])
    vws_o = dt_out("vws", [16, 120, 256])
    knew_o = dt_out("knew", [128, 256])
    vnew_o = dt_out("vnew", [128, 256])
    sgv_o = dt_out("sgv", [128, 1024])
    if debug:
        dbg_mix = nc.dram_tensor("dbg_mix", [128, 16, NG * 128], BF16, kind="ExternalOutput").ap()
        dbg_h = dt_out("dbg_h", [128, NG, D])

    P = Prog()
    es = contextlib.ExitStack()
    with es:
        def sb(name, shape, dt):
            return es.enter_context(nc.sbuf_tensor("sb_" + name, shape, dt))

        def ps(name, shape, dt):
            return es.enter_context(nc.psum_tensor("ps_" + name, shape, dt))

        def sem(name):
            return es.enter_context(nc.semaphore(name))

        NSLOT = 4
        ring = [sb(f"ring{i}", [128, 16, 512], BF16) for i in range(NSLOT)]
        xnT = sb("xnT", [128, 16, NG * 128], BF16)
        mixT = sb("mixT", [128, 16, NG * 128], BF16)
        Hreg = sb("Hreg", [128, NG * D], F32)
        hbuf = Hreg[:, :].rearrange("p (t d) -> p t d", d=D)
        Hb = Hreg[:, :].bitcast(BF16)
        kT2 = sb("kT2", [64, 4, 6, 128], BF16)
        Vaug = sb("Vaug", [128, 6, 4, 72], BF16)
        g_n = Hb[:, 0:NG * 1024].rearrange("p (t e) -> p t e", e=1024)
        X = sb("X", [128, 4096], F32)
        xsb = sb("xsb", [128, D], BF16)
        o0 = NG * 1024
        qb2 = [Hb[:, o0 + 1024 * i:o0 + 1024 * (i + 1)] for i in range(2)]
        qT2 = [Hb[0:64, o0 + 2048 + 2048 * i:o0 + 2048 + 2048 * (i + 1)].rearrange("p (h t) -> p h t", t=128) for i in range(2)]
        PT2 = [Hb[:, o0 + 6144 + 1024 * i:o0 + 6144 + 1024 * (i + 1)].rearrange("p (b e) -> p b e", e=512) for i in range(2)]
        kbd = Hb[:, o0 + 8192:o0 + 8448]
        assert (o0 + 8448) // 2 <= 7168
        S = sb("S", [128, 3328], F32)
        Sb = S[:, :].bitcast(BF16)
        PTz = Sb[:, 0:2048].rearrange("p (h t) -> p h t", t=128)
        ckd = Sb[:, 2048:2560].rearrange("p (s e) -> p s e", e=256)
        cva = Sb[:, 2560:3136].rearrange("p (s h e) -> p s h e", h=4, e=72)
        kcT2 = Sb[0:64, 3136:3648].rearrange("p (h t) -> p h t", t=128)
        kcT2b = [kcT2, Sb[0:64, 5696:6208].rearrange("p (h t) -> p h t", t=128)]
        ckf = S[:, 1824:2336].rearrange("p (s e) -> p s e", e=256)
        cvf = S[:, 2336:2848].rearrange("p (s e) -> p s e", e=256)
        cols = sb("cols", [128, 64], F32)
        bc = sb("bc", [128, 144], F32)
        cs = sb("cs", [128, NT, 128], F32)
        mk = sb("mk", [128, 2304], BF16)
        wTm = sb("wTm", [128, 2, 8, 128], BF16)
        st8 = sb("st8", [128, 64], F32)
        rall = sb("rall", [128, 64], F32)
        sinkexp = sb("sinkexp", [128, 16], F32)
        kfo = sb("kfo", [128, 2, 256], F32)
        vfo = sb("vfo", [128, 2, 256], F32)
        kfw = sb("kfw", [128, 256], F32)

        mm = [ps(f"mm{i}", [128, 512], F32) for i in range(2)]
        tp = ps("tp", [128, 1024], BF16)
        Bk = [ps(f"bk{i}", [128, 512], F32) for i in range(5)]
        stp = [Bk[0], Bk[1]]

        esem = {e: sem(f"s_{e}") for e in Prog.ENG}
        ds_ring = [DSem(sem(f"d_ring{i}")) for i in range(NSLOT)]
        ds_xt = [DSem(sem(f"d_xt{i}")) for i in range(2)]
        ds_xr = [DSem(sem(f"d_xr{i}")) for i in range(2)]
        ds_yb = [DSem(sem(f"d_yb{i}")) for i in range(2)]
        ds_ck = DSem(sem("d_ck"))
        ds_cv = DSem(sem("d_cv"))
        ds_setup = DSem(sem("d_setup"))
        ds_setup2 = DSem(sem("d_setup2"))
        t_const2 = Tr()
        ds_out = DSem(sem("d_out"))
        ds_gout = DSem(sem("d_gout"))
        ds_gg = DSem(sem("d_gg"))
        t_gg = Tr()
        ds_dbg = DSem(sem("d_dbg"))
        ds_dbg2 = DSem(sem("d_dbg2"))

        xt = [X[:, 0:2048], X[:, 2048:4096]]
        T_sq = X[:, 0:512]
        T_qh = X[:, 512:1024]
        T_m1 = X[:, 1024:1536]
        T_m2 = X[:, 1536:2048]
        T_a = X[:, 2048:3072]
        T_g = X[:, 3072:4096]
        xr = [S[:, 0:512], S[:, 512:1024]]
        yb = [S[:, 1024:1536], S[:, 1536:2048]]
        sgt = [S[:, 2048 + 320 * i:2048 + 320 * (i + 1)] for i in range(4)]
        tpF = tp[:, :].bitcast(F32)
        ga_col = cols[:, 0:16]
        gf_col = cols[:, 16:32]
        gao_col = cols[:, 32:40]
        gso_col = cols[:, 40:48]
        bcol = [cols[:, 48:56], cols[:, 56:64]]
        gq_bc = bc[:, 0:64]
        gk_bc = bc[:, 64:128]
        sinks_bc = bc[:, 128:144]
        gg_bc = Hreg[:, 8192:9216]
        m_cur = mk[:, 0:512]
        m_prev = mk[:, 512:1024]
        m_first = mk[:, 1024:1536]
        m_news = mk[:, 1536:2048]
        m_cache = mk[:, 2048:2176]
        ident = mk[:, 2176:2304]

        t_stq, t_rq, t_stk, t_rk = Tr(), Tr(), Tr(), Tr()
        t_ksq, t_kqh, t_km1 = Tr(), Tr(), Tr()
        t_ring = [Tr() for _ in range(NSLOT)]
        t_xt = [Tr(), Tr()]
        t_xsb = Tr()
        t_xsb2 = Tr()
        t_tp = Tr()
        t_mm = [Tr(), Tr()]
        t_B = [Tr() for _ in range(5)]
        t_st = [t_B[0], t_B[1]]
        t_xnT = [Tr() for _ in range(NG)]
        t_mixA = [Tr() for _ in range(NG)]
        t_mixS = [Tr() for _ in range(NG)]
        t_act = Tr()
        t_h = [[Tr() for _ in range(4)] for _ in range(NG)]
        t_kT2 = [Tr() for _ in range(6)]
        t_V = [Tr() for _ in range(6)]
        t_gn = [Tr() for _ in range(NG)]
        t_sq, t_qh, t_m1, t_m2, t_a, t_g = Tr(), Tr(), Tr(), Tr(), Tr(), Tr()
        t_kbd, t_PTz = Tr(), Tr()
        t_qb2, t_qT2 = [Tr(), Tr()], [Tr(), Tr()]
        t_PT2 = [[Tr(), Tr()], [Tr(), Tr()]]
        t_ckf, t_cvf, t_ckd, t_cva, t_kcT2 = Tr(), Tr(), Tr(), Tr(), Tr()
        t_kcT2b = [t_kcT2, Tr()]
        t_xr, t_yb, t_sgt = [Tr(), Tr()], [Tr(), Tr()], [Tr() for _ in range(4)]
        t_ovb = [Tr(), Tr(), Tr()]
        t_const = Tr()
        t_wTm = Tr()
        t_st8 = Tr()
        t_st8b = Tr()
        t_rallb = Tr()
        t_rall = Tr()
        t_kfw = Tr()
        t_kfo, t_vfo = [Tr(), Tr()], [Tr(), Tr()]

        slab_list = []
        w_in_v = w_in.rearrange("(c p) e -> p c e", p=128)
        w_o_v = w_o.rearrange("(c p) e -> p c e", p=128)
        w_g_v = w_gate.rearrange("(c p) e -> p c e", p=128)
        w_u_v = w_up.rearrange("(c p) e -> p c e", p=128)
        w_d_v = w_down.rearrange("(c p) e -> p c e", p=128)

        def full_slab(src_v, c0):
            return [(lambda r, h=h: r[:, 8 * h:8 * h + 8, :], src_v[:, 8 * h:8 * h + 8, c0:c0 + 512]) for h in range(2)]

        for _g in range(len(GROUPS)):
            for c0 in (1024, 0, 512, 2560, 3072, 1536, 2048):
                slab_list.append(full_slab(w_in_v, c0))
            for s in range(4):
                slab_list.append(full_slab(w_o_v, s * 512))
            for (pc0, pc1) in FFN_PARTS:
                for j in range(pc0 // 4, pc1 // 4):
                    slab_list.append(full_slab(w_g_v, 512 * j))
                    slab_list.append(full_slab(w_u_v, 512 * j))
                nch = pc1 - pc0
                for s in range(4):
                    hh = nch // 2
                    slab_list.append([
                        (lambda r, hh=hh: r[:, 0:hh, :], w_d_v[:, pc0:pc0 + hh, s * 512:(s + 1) * 512]),
                        (lambda r, hh=hh, nch=nch: r[:, hh:nch, :], w_d_v[:, pc0 + hh:pc1, s * 512:(s + 1) * 512]),
                    ])
        slab_state = {"loaded": 0, "used": 0}
        MARKS.clear()

        def load_next_slab():
            n = slab_state["loaded"]
            if n >= len(slab_list):
                return
            slot = n % NSLOT
            for (dst_fn, src) in slab_list[n]:
                P.dma("pool", dst_fn(ring[slot]), src, ds_ring[slot], writes=[t_ring[slot]], nbytes=2 * 1024 * 1024)
            slab_state["loaded"] = n + 1

        def next_slab():
            n = slab_state["used"]
            slab_state["used"] = n + 1
            assert n < slab_state["loaded"]
            return ring[n % NSLOT], t_ring[n % NSLOT]

        def release_slab():
            load_next_slab()

        P.dma("sp", cols[:, :], cols_d, ds_setup, writes=[t_const])
        P.dma("sp", bc[:, :], bcv_d[0:144].partition_broadcast(128), ds_setup, writes=[t_const])
        P.dma("sp", cs[:, :, :], cs_d, ds_setup, writes=[t_const])
        P.dma("sp", X[:, 0:2048], wt_d, ds_setup, writes=[t_const])
        P.dma("sp", X[:, 2048:2304], cm_d, ds_setup, writes=[t_const])
        P.dma("pool", mk[:, :], mk_d, ds_setup2, writes=[t_const2])
        P.dma("sp", kws_o, ck[:, 8:128, :], ds_out)
        P.dma("sp", vws_o, cv[:, 8:128, :], ds_out)
        for _ in range(NSLOT):
            load_next_slab()

        P.op("dve", MSET(Vaug[:, :, :, :], 1.0), writes=t_V)
        for k in range(2):
            P.op("dve", TT(wTm[:, k, :, :], X[:, k * 1024:(k + 1) * 1024].rearrange("p (h i) -> p h i", i=128),
                           X[:, 2048 + 128 * k:2048 + 128 * (k + 1)].unsqueeze(1).to_broadcast([128, 8, 128]), ALU.mult),
                 reads=[t_const], writes=[t_wTm] + ([t_xt[0], t_xt[1]] if k == 1 else []))
        P.op("act", ACTF(sinkexp[:, :], sinks_bc, AF.Exp), reads=[t_const, t_const2])

        def rstd_from_ssq(ssq_ap, out_ap, n, width, st_tr=None, r_tr=None):
            st_tr = st_tr or t_st8
            r_tr = r_tr or t_rall
            P.op("act", ACTF(ssq_ap, ssq_ap, AF.Ln, scale=1.0 / n, bias=EPS), reads=[st_tr], writes=[st_tr])
            return P.op("act", ACTF(out_ap, ssq_ap, AF.Exp, scale=-0.5), reads=[st_tr], writes=[r_tr])

        tp_pool = [(tp, t_tp), (Bk[2][:, :].bitcast(BF16), t_B[2]), (Bk[3][:, :].bitcast(BF16), t_B[3]), (Bk[4][:, :].bitcast(BF16), t_B[4])]
        tp_rr = [0]

        def next_tp(rotate):
            if not rotate:
                return tp, t_tp
            tp_rr[0] = (tp_rr[0] + 1) % len(tp_pool)
            return tp_pool[tp_rr[0]]

        def norm_transpose(src_ap, src_tr, nchunks, dstT, dst_col0, dst_tr, gcol, c_off=0, rotate=False):
            W = nchunks * 128
            ssq = st8[:, 0:1]
            P.op("dve", MSET(ssq, 0.0), writes=[t_st8])
            P.op("act", ACTF(xsb[:, 0:W], src_ap, AF.Square, accum=ssq), reads=[src_tr], writes=[t_xsb, t_st8])
            rstd_from_ssq(ssq, rall[:, 0:1], float(W), 1)
            P.op("act", ACTF(xsb[:, 0:W], src_ap, AF.Copy, scale=rall[:, 0:1]), reads=[src_tr, t_rall], writes=[t_xsb])
            for h0 in range(0, nchunks, 8):
                tpb, tpt = next_tp(rotate)
                fns = [TRP(tpb[:, (c - h0) * 128:(c - h0 + 1) * 128], xsb[:, c * 128:(c + 1) * 128], ident)
                       for c in range(h0, h0 + 8)]
                P.group("pe", fns, reads=[t_xsb, t_const, t_const2], writes=[tpt])
                P.op("dve", TT(dstT[:, c_off + h0:c_off + h0 + 8, dst_col0:dst_col0 + 128],
                               tpb[:, :].rearrange("p (c t) -> p c t", t=128),
                               gcol[:, h0:h0 + 8].unsqueeze(2).to_broadcast([128, 8, 128]), ALU.mult),
                     reads=[tpt, t_const, t_const2], writes=[dst_tr])

        def dense_B(srcT, col0, src_tr, slab, slab_tr, nch, bank, bank_tr, ch0=0, ncols=512, sc0=0):
            fns = [MM(bank[:, 0:ncols], srcT[:, ch0 + c, col0:col0 + 128], slab[:, c, sc0:sc0 + ncols], c == 0, c == nch - 1)
                   for c in range(nch)]
            return P.group("pe", fns, reads=[src_tr, slab_tr], writes=[bank_tr])

        def qk_norm_rope(src_bank, src_tr, nh, gbc, t, out_ap, out_tr, TS_):
            W = nh * 64
            v3 = lambda ap: ap.rearrange("p (h d) -> p h d", d=64)
            sq, qh, m1 = TS_["sq"][:, 0:W], TS_["qh"][:, 0:W], TS_["m1"][:, 0:W]
            m2 = sq
            tsq, tqh, tm1, tst, tr_ = TS_["tsq"], TS_["tqh"], TS_["tm1"], TS_["tst"], TS_["tr"]
            stv, rv = TS_["st"][:, 0:nh], TS_["r"][:, 0:nh]
            P.op("act", ACTF(sq, src_bank[:, 0:W], AF.Square), reads=[src_tr], writes=[tsq])
            P.op("dve", RSUM(stv, v3(sq)), reads=[tsq], writes=[tst])
            rstd_from_ssq(stv, rv, 64.0, nh, tst, tr_)
            P.op("dve", TT(v3(qh), v3(src_bank[:, 0:W]), rv.unsqueeze(2).to_broadcast([128, nh, 64]), ALU.mult),
                 reads=[src_tr, tr_], writes=[tqh])
            P.op("dve", TT(v3(qh), v3(qh), gbc.unsqueeze(1).to_broadcast([128, nh, 64]), ALU.mult),
                 reads=[t_const, t_const2], writes=[tqh])
            csA = cs[:, t, 0:64].unsqueeze(1).to_broadcast([128, nh, 64])
            P.op("dve", TT(v3(m1), v3(qh), csA, ALU.mult), reads=[tqh, t_const, t_const2], writes=[tm1])
            P.op("dve", TT(v3(m2)[:, :, 0:32], v3(qh)[:, :, 32:64],
                           cs[:, t, 64:96].unsqueeze(1).to_broadcast([128, nh, 32]), ALU.mult),
                 reads=[tqh, t_const, t_const2], writes=[tsq])
            P.op("dve", TT(v3(m2)[:, :, 32:64], v3(qh)[:, :, 0:32],
                           cs[:, t, 96:128].unsqueeze(1).to_broadcast([128, nh, 32]), ALU.mult),
                 reads=[tqh], writes=[tsq])
            P.op("dve", TT(out_ap, m1, m2, ALU.add), reads=[tm1, tsq], writes=[out_tr])

        def head_slot(h):
            return Bk[2 + h // 7], t_B[2 + h // 7], (h % 7) * 72

        def S1_tile(t, i):
            b = i % 2
            P.dma("sp", xt[b], xs[t], ds_xt[b], writes=[t_xt[b]] + ([t_xsb2] if b == 0 else []), nbytes=1024 * 1024)
            norm_transpose(xt[b], t_xt[b], 16, xnT, i * 128, t_xnT[i], ga_col, rotate=True)

        def run_group(gi, kv_tiles, ctiles, next_kv=None, s1_done=0):
            nct = len(ctiles)
            xcol = {t: (i * 128) for i, t in enumerate(kv_tiles)}
            if kv_tiles[0] != ctiles[0]:
                xblk = {t: i for i, t in enumerate(kv_tiles)}
            else:
                xblk = {t: i for i, t in enumerate(kv_tiles)}
            li_of = {t: i for i, t in enumerate(ctiles)}

            QS = dict(sq=T_sq, qh=T_qh, m1=T_m1, tsq=t_sq, tqh=t_qh, tm1=t_m1,
                      st=st8[:, 32:40], r=rall[:, 32:40], tst=t_stq, tr=t_rq)
            KS = dict(sq=Hreg[:, 7168:7424], qh=Hreg[:, 7424:7680], m1=Hreg[:, 7680:7936], tsq=t_ksq, tqh=t_kqh, tm1=t_km1,
                      st=st8[:, 48:52], r=rall[:, 48:52], tst=t_stk, tr=t_rk)

            def S1(i):
                if i >= s1_done:
                    S1_tile(kv_tiles[i], i)

            slab, slab_tr = next_slab()

            def S2mm(i):
                t = kv_tiles[i]
                dense_B(xnT, xcol[t], t_xnT[xblk[t]], slab, slab_tr, 16, mm[i % 2], t_mm[i % 2])

            def S2post(i):
                t = kv_tiles[i]
                bk = i % 2
                if t == 8:
                    kf, kf_tr, vf, vf_tr = kfo[:, 0, :], t_kfo[0], vfo[:, 0, :], t_vfo[0]
                elif t == 9:
                    kf, kf_tr, vf, vf_tr = kfo[:, 1, :], t_kfo[1], vfo[:, 1, :], t_vfo[1]
                else:
                    kf, kf_tr, vf, vf_tr = kfw[:, :], t_kfw, None, None
                qk_norm_rope(mm[bk], t_mm[bk], 4, gk_bc, t, kf, kf_tr, KS)
                P.op("act", ACTF(kbd, kf, AF.Copy), reads=[kf_tr], writes=[t_kbd])
                P.op("act", ACTF(Vaug[:, t % 6, :, 0:64], mm[bk][:, 256:512].rearrange("p (h d) -> p h d", d=64), AF.Copy),
                     reads=[t_mm[bk]], writes=[t_V[t % 6]])
                if vf is not None:
                    P.op("act", ACTF(vf, mm[bk][:, 256:512], AF.Copy), reads=[t_mm[bk]], writes=[vf_tr])
                    if t == 8:
                        P.dma("sp", kwp_o, kf, ds_out, reads=[kf_tr])
                        P.dma("sp", vwp_o, vf, ds_out, reads=[vf_tr])
                    else:
                        P.dma("sp", knew_o, kf, ds_out, reads=[kf_tr])
                        P.dma("sp", vnew_o, vf, ds_out, reads=[vf_tr])
                fns = [TRP(tp[0:64, h * 128:(h + 1) * 128], kbd[:, h * 64:(h + 1) * 64], ident) for h in range(4)]
                P.group("pe", fns, reads=[t_kbd, t_const, t_const2], writes=[t_tp])
                P.op("dve", CP(kT2[:, :, t % 6, :], tp[0:64, 0:512].rearrange("p (h t) -> p h t", t=128)),
                     reads=[t_tp], writes=[t_kT2[t % 6]])

            nkv = len(kv_tiles)
            S1(0)
            if nkv > 1:
                S1(1)
            S2mm(0)
            for i in range(nkv):
                if i + 2 < nkv:
                    S1(i + 2)
                if i + 1 < nkv:
                    S2mm(i + 1)
                S2post(i)
            release_slab()

            MARKS.append((gi, 1, len(P.ops['pe'])))
            MARKS.append((gi, 2, len(P.ops['pe'])))
            if stop_after in (1, 2):
                raise _Stop()
            s0, s0_tr = next_slab()
            s1, s1_tr = next_slab()

            def A3(i):
                t = ctiles[i]
                for hq, (sl, sl_tr) in enumerate(((s0, s0_tr), (s1, s1_tr))):
                    dense_B(xnT, xcol[t], t_xnT[xblk[t]], sl, sl_tr, 16, mm[hq], t_mm[hq])

            def B3_chain(i, hq):
                t = ctiles[i]
                par = i % 2
                qk_norm_rope(mm[hq], t_mm[hq], 8, gq_bc, t, qb2[par][:, hq * 512:(hq + 1) * 512], t_qb2[par], QS)

            def B3_tr(i, hq):
                par = i % 2
                fns = [TRP(tp[0:64, c * 128:(c + 1) * 128], qb2[par][:, hq * 512 + c * 64:hq * 512 + (c + 1) * 64], ident) for c in range(8)]
                P.group("pe", fns, reads=[t_qb2[par], t_const, t_const2], writes=[t_tp])
                P.op("dve", CP(qT2[par][:, hq * 8:(hq + 1) * 8, :], tp[0:64, 0:1024].rearrange("p (c t) -> p c t", t=128)),
                     reads=[t_tp], writes=[t_qT2[par]])

            def blocks_of(t):
                if t == 9:
                    return [(9 % 6, m_news)]
                if t == 1:
                    return [(0, m_first), (1, m_cur)]
                return [((t - 1) % 6, m_prev), (t % 6, m_cur)]

            def C3_ST(i, kvh):
                t = ctiles[i]
                par = i % 2
                qT_, tqT_ = qT2[par], t_qT2[par]
                for bi, (slot, mask) in enumerate(blocks_of(t)):
                    bidx = (kvh % 2) * 2 + bi if t != 9 else 0
                    bank, btr = Bk[bidx], t_B[bidx]
                    fns = [MM(bank[:, :], ident, mask, True, False)]
                    for g in range(4):
                        h = 4 * kvh + g
                        fns.append(MM(bank[:, g * 128:(g + 1) * 128], kT2[:, kvh, slot, :], qT_[:, h, :], False, g == 3))
                    P.group("pe", fns, reads=[t_kT2[slot], tqT_, t_const, t_const2], writes=[btr])
                    pp = kvh % 2
                    P.op("act", ACTF(PT2[pp][:, bi, :], bank[:, :], AF.Exp, scale=0.125), reads=[btr], writes=[t_PT2[pp][bi]])

            def C3_PV(i, kvh):
                t = ctiles[i]
                blks = blocks_of(t)
                pp = kvh % 2
                fns = []
                wr = set()
                for g in range(4):
                    h = 4 * kvh + g
                    if t == 9:
                        bank, btr, c0 = head_slot(h)
                        first = (h % 7 == 0)
                    else:
                        bank, btr, c0 = Bk[4], t_B[4], g * 72
                        first = (g == 0)
                    wr.add(btr)
                    for bi, (slot, mask) in enumerate(blks):
                        last = (bi == len(blks) - 1) and (t != 9)
                        fns.append(MM(bank[:, c0:c0 + 72], PT2[pp][:, bi, g * 128:(g + 1) * 128], Vaug[:, slot, kvh, :],
                                      bi == 0 and first, last))
                P.group("pe", fns, reads=[t_PT2[pp][0], t_PT2[pp][1]] + [t_V[s_] for s_, _ in blks], writes=list(wr))
                if t != 9:
                    v = Bk[4][:, 0:288].rearrange("p (h e) -> p h e", e=72)
                    P.op("act", ACTF(T_g[:, kvh * 256:(kvh + 1) * 256].rearrange("p (h d) -> p h d", d=64), v[:, :, 0:64], AF.Copy),
                         reads=[t_B[4]], writes=[t_g])
                    P.op("act", ACTF(st8[:, 16 + 4 * kvh:20 + 4 * kvh], v[:, :, 64], AF.Copy), reads=[t_B[4]], writes=[t_st8b])
                    if kvh == 3:
                        P.op("dve", TT(st8[:, 16:32], st8[:, 16:32], sinkexp[:, 0:16], ALU.add), reads=[t_st8b], writes=[t_st8b])
                        P.op("dve", RCP(rall[:, 16:32], st8[:, 16:32]), reads=[t_st8b], writes=[t_rallb])
                        P.op("dve", TT(T_a.rearrange("p (h d) -> p h d", d=64), T_g.rearrange("p (h d) -> p h d", d=64),
                                       rall[:, 16:32].unsqueeze(2).to_broadcast([128, 16, 64]), ALU.mult),
                             reads=[t_g, t_rallb], writes=[t_a])

            def evac(bank, btr, nh, h0):
                v = bank[:, 0:nh * 72].rearrange("p (h e) -> p h e", e=72)
                P.op("dve", TT(st8[:, 16:16 + nh], v[:, :, 64], sinkexp[:, h0:h0 + nh], ALU.add), reads=[btr], writes=[t_st8b])
                P.op("dve", RCP(rall[:, 16:16 + nh], st8[:, 16:16 + nh]), reads=[t_st8b], writes=[t_rallb])
                P.op("dve", TT(T_a[:, h0 * 64:(h0 + nh) * 64].rearrange("p (h d) -> p h d", d=64), v[:, :, 0:64],
                               rall[:, 16:16 + nh].unsqueeze(2).to_broadcast([128, nh, 64]), ALU.mult),
                     reads=[btr, t_rallb], writes=[t_a])

            def C3_sample_cache(i):
                par = i % 2
                qT_, tqT_ = qT2[par], t_qT2[par]
                P.op("dve", MSET(cva, 1.0), writes=[t_cva])
                P.op("dve", MSET(PTz, 0.0), writes=[t_PTz])

                def load(sg):
                    P.dma("sp", ckf, ck[2 * sg:2 * sg + 2].rearrange("s k e -> k s e"), ds_ck, writes=[t_ckf])
                    P.dma("sp", cvf, cv[2 * sg:2 * sg + 2].rearrange("s k e -> k s e"), ds_cv, writes=[t_cvf])

                def cast(sg):
                    P.op("act", ACTF(ckd, ckf, AF.Copy), reads=[t_ckf], writes=[t_ckd])

                def castv(sg):
                    P.op("dve", CP(cva[:, :, :, 0:64], cvf.rearrange("p s (h d) -> p s h d", d=64)),
                         reads=[t_cvf], writes=[t_cva])

                def TR(b):
                    s_ = b % 2
                    fns = [TRP(tp[0:64, h * 128:(h + 1) * 128], ckd[:, s_, h * 64:(h + 1) * 64], ident) for h in range(4)]
                    P.group("pe", fns, reads=[t_ckd, t_const, t_const2], writes=[t_tp])
                    P.op("dve", CP(kcT2b[b % 2], tp[0:64, 0:512].rearrange("p (h t) -> p h t", t=128)), reads=[t_tp], writes=[t_kcT2b[b % 2]])

                def ST(b):
                    kc, kct = kcT2b[b % 2], t_kcT2b[b % 2]
                    fns = [MM(Bk[1][:, 0:128], ident, m_cache, True, False)]
                    for h in range(16):
                        fns.append(MM(Bk[1][:, h * 8:(h + 1) * 8], kc[:, h // 4, :], qT_[:, h, b * 8:(b + 1) * 8], False, h == 15))
                    P.group("pe", fns, reads=[kct, tqT_, t_const, t_const2], writes=[t_B[1]])
                    P.op("act", ACTF(PTz[:, :, b * 8:(b + 1) * 8], Bk[1][:, 0:128].rearrange("p (h i) -> p h i", i=8), AF.Exp, scale=0.125),
                         reads=[t_B[1]], writes=[t_PTz])

                def PV(b):
                    s_ = b % 2
                    fns = []
                    for h in range(16):
                        bank, btr, c0 = head_slot(h)
                        fns.append(MM(bank[:, c0:c0 + 72], PTz[:, h, :], cva[:, s_, h // 4, :], False, b == 15))
                    P.group("pe", fns, reads=[t_PTz, t_cva], writes=[t_B[2], t_B[3], t_B[4]])
                    P.op("act", MSET_ACT(PTz[:, :, b * 8:(b + 1) * 8]), reads=[], writes=[t_PTz])

                load(0); cast(0); castv(0)
                TR(0)
                for b in range(16):
                    ST(b)
                    if b % 2 == 0:
                        TR(b + 1)
                    PV(b)
                    if b % 2 == 1 and b + 1 < 16:
                        load((b + 1) // 2); cast((b + 1) // 2); castv((b + 1) // 2)
                        TR(b + 1)
                for bnk in range(3):
                    nh = 7 if bnk < 2 else 2
                    evac(Bk[2 + bnk], t_B[2 + bnk], nh, 7 * bnk)

            def C3_fin(i):
                li = i
                norm_transpose(T_a, t_a, 8, mixT, li * 128, t_mixA[li], gao_col, c_off=0)

            A3(0)
            B3_chain(0, 0); B3_tr(0, 0); B3_chain(0, 1); B3_tr(0, 1)
            if nct > 1:
                A3(1)
            for i in range(nct):
                nxt = i + 1 < nct
                t = ctiles[i]
                C3_ST(i, 0)
                C3_ST(i, 1)
                if nxt:
                    B3_chain(i + 1, 0)
                C3_PV(i, 0)
                C3_ST(i, 2)
                C3_PV(i, 1)
                C3_ST(i, 3)
                if nxt:
                    B3_tr(i + 1, 0)
                    B3_chain(i + 1, 1)
                C3_PV(i, 2)
                C3_PV(i, 3)
                if nxt:
                    B3_tr(i + 1, 1)
                    if i + 2 < nct:
                        A3(i + 2)
                if t == 9:
                    C3_sample_cache(i)
                C3_fin(i)
            release_slab()
            release_slab()

            MARKS.append((gi, 3, len(P.ops['pe'])))
            if stop_after == 3:
                raise _Stop()
            P.dma("sp", gg_bc, bcv_d[144:1168].partition_broadcast(128), ds_gg, writes=[t_gg])
            s0, s0_tr = next_slab()
            s1, s1_tr = next_slab()
            for t in ctiles:
                li = li_of[t]
                for hq, (sl, sl_tr) in enumerate(((s0, s0_tr), (s1, s1_tr))):
                    dense_B(xnT, xcol[t], t_xnT[xblk[t]], sl, sl_tr, 16, mm[hq], t_mm[hq])
                for hq in range(2):
                    P.op("act", ACTF(T_g[:, hq * 512:(hq + 1) * 512], mm[hq][:, :], AF.Gelu_apprx_tanh), reads=[t_mm[hq]], writes=[t_g])
                ssq = st8[:, 0:1]
                P.op("dve", MSET(ssq, 0.0), writes=[t_st8])
                P.op("act", ACTF(xsb[:, 0:1024], T_g, AF.Square, accum=ssq), reads=[t_g], writes=[t_xsb, t_st8])
                rstd_from_ssq(ssq, rall[:, 0:1], 1024.0, 1)
                P.op("dve", STT(g_n[:, li, :], T_g, rall[:, 0:1], gg_bc, ALU.mult, ALU.mult), reads=[t_g, t_rall, t_gg], writes=[t_gn[li]])
                if t == 9:
                    P.op("dve", STT(T_a, T_g, rall[:, 0:1], gg_bc, ALU.mult, ALU.mult), reads=[t_g, t_rall, t_gg], writes=[t_a])
                    P.dma("sp", sgv_o, T_a, ds_gout, reads=[t_a])
            release_slab()
            release_slab()

            MARKS.append((gi, 4, len(P.ops['pe'])))
            if stop_after == 4:
                raise _Stop()
            s0, s0_tr = next_slab()
            s1, s1_tr = next_slab()

            def A5(i):
                t = ctiles[i]
                for hq, (sl, sl_tr) in enumerate(((s0, s0_tr), (s1, s1_tr))):
                    dense_B(xnT, xcol[t], t_xnT[xblk[t]], sl, sl_tr, 16, mm[hq], t_mm[hq])

            A5(0)
            for i, t in enumerate(ctiles):
                li = li_of[t]
                kk = 1 if t == 9 else 0
                for hq in range(2):
                    P.op("act", ACTF(T_g[:, hq * 512:(hq + 1) * 512], mm[hq][:, :], AF.Gelu_apprx_tanh), reads=[t_mm[hq]], writes=[t_g])
                for hq in range(2):
                    fns = [MM(stp[hq][:, j * 128:(j + 1) * 128], wTm[:, kk, hq * 4 + j, :],
                              g_n[:, li, (hq * 4 + j) * 128:(hq * 4 + j + 1) * 128], True, True) for j in range(4)]
                    P.group("pe", fns, reads=[t_gn[li], t_wTm], writes=[t_st[hq]])
                if i + 1 < nct:
                    A5(i + 1)
                for hd in range(8):
                    P.op("dve", STT(T_a[:, hd * 128:(hd + 1) * 128], stp[hd // 4][:, (hd % 4) * 128:(hd % 4 + 1) * 128],
                                    bcol[kk][:, hd:hd + 1], T_g[:, hd * 128:(hd + 1) * 128], ALU.add, ALU.mult),
                         reads=[t_st[hd // 4], t_g, t_const, t_const2], writes=[t_a])
                norm_transpose(T_a, t_a, 8, mixT, li * 128, t_mixS[li], gso_col, c_off=8)
            release_slab()
            release_slab()

            MARKS.append((gi, 5, len(P.ops['pe'])))
            if stop_after == 5:
                raise _Stop()
            if debug and gi == 0:
                P.dma("sp", dbg_mix, mixT[:, :, :], ds_dbg, reads=t_mixA[0:nct] + t_mixS[0:nct])
            P.barrier(engines=("pe", "act", "dve", "sp"))
            for s in range(4):
                slab, slab_tr = next_slab()
                for t in ctiles:
                    li = li_of[t]
                    bk = li % 2
                    fns = [MM(mm[bk][:, :], mixT[:, c, li * 128:(li + 1) * 128], slab[:, c, :], c == 0, c == 15) for c in range(16)]
                    P.group("pe", fns, reads=[t_mixA[li], t_mixS[li], slab_tr], writes=[t_mm[bk]])
                    P.dma("sp", xr[bk], xs[t][:, s * 512:(s + 1) * 512], ds_xr[bk], writes=[t_xr[bk]])
                    P.op("dve", TT(hbuf[:, li, s * 512:(s + 1) * 512], mm[bk][:, :], xr[bk], ALU.add),
                         reads=[t_mm[bk], t_xr[bk]], writes=[t_h[li][s]])
                    if s == 3 and not debug:
                        hn_part1(li)
                        if li > 0:
                            hn_part2(li - 1)
                if s == 3 and not debug:
                    hn_part2(nct - 1)
                release_slab()
            if debug and gi == 0:
                P.dma("sp", dbg_h, hbuf[:, :, :], ds_dbg2, reads=[x for l in t_h for x in l])
            if debug:
                for t in ctiles:
                    norm_transpose_h(li_of[t])

            MARKS.append((gi, 6, len(P.ops['pe'])))
            if stop_after == 6:
                raise _Stop()
            ntok = nct * 128
            t_act.w = None
            t_act.r = [P.last["pe"]] + [tok for tr in (t_mixA + t_mixS) for tok in tr.r]
            banks = [mm[0], mm[1], Bk[0], Bk[1], Bk[2], Bk[3], Bk[4], tpF]
            t_banks = [t_mm[0], t_mm[1]] + t_B + [t_tp]
            chunks = [(0, ntok)] if ntok <= 512 else [(0, ntok // 2), (ntok // 2, ntok)]
            nck = len(chunks)
            wd_cnt = 0
            for pi, (pc0, pc1) in enumerate(FFN_PARTS):
                nch = pc1 - pc0
                for j in range(nch // 4):
                    slabG, slabG_tr = next_slab()
                    slabU, slabU_tr = next_slab()
                    for fc in range(4):
                        fcl = 4 * j + fc
                        st_ = fcl % 2
                        for ci, (ca, cb) in enumerate(chunks):
                            bg = (st_ * nck + ci) * 2
                            bu_ = bg + 1
                            n = cb - ca
                            fg = [MM(banks[bg][:, 0:n], slabG[:, c, fc * 128:(fc + 1) * 128], xnT[:, c, ca:cb], c == 0, c == 15) for c in range(16)]
                            P.group("pe", fg, reads=[slabG_tr] + t_xnT[0:nct], writes=[t_banks[bg]])
                            fu = [MM(banks[bu_][:, 0:n], slabU[:, c, fc * 128:(fc + 1) * 128], xnT[:, c, ca:cb], c == 0, c == 15) for c in range(16)]
                            P.group("pe", fu, reads=[slabU_tr] + t_xnT[0:nct], writes=[t_banks[bu_]])
                            si = st_ * 2 + ci
                            sg_ap = S[:, 2048 + 640 * st_:2048 + 640 * st_ + n] if nck == 1 else sgt[si][:, 0:n]
                            P.op("act", ACTF(sg_ap, banks[bg][:, 0:n], AF.Silu), reads=[t_banks[bg]], writes=[t_sgt[si]])
                            P.op("dve", TT(mixT[:, fcl, ca:cb], sg_ap, banks[bu_][:, 0:n], ALU.mult),
                                 reads=[t_sgt[si], t_banks[bu_]], writes=[t_act])
                    release_slab()
                    release_slab()
                last_part = pi == len(FFN_PARTS) - 1
                if last_part and next_kv is not None and not debug:
                    S1_tile(next_kv[0], 0)
                    S1_tile(next_kv[1], 1)
                for s in range(4):
                    slab, slab_tr = next_slab()
                    for t in ctiles:
                        li = li_of[t]
                        bi_ = wd_cnt % 8
                        wd_cnt += 1
                        bk = li % 2
                        fns = [MM(banks[bi_][:, :], mixT[:, c, li * 128:(li + 1) * 128], slab[:, c, :], c == 0, c == nch - 1) for c in range(nch)]
                        P.group("pe", fns, reads=[t_act, slab_tr], writes=[t_banks[bi_]])
                        if not last_part:
                            P.op("dve", TT(hbuf[:, li, s * 512:(s + 1) * 512], banks[bi_][:, :], hbuf[:, li, s * 512:(s + 1) * 512], ALU.add),
                                 reads=[t_banks[bi_]], writes=[t_h[li][s]])
                        else:
                            P.op("dve", TT(yb[bk], banks[bi_][:, :], hbuf[:, li, s * 512:(s + 1) * 512], ALU.add),
                                 reads=[t_banks[bi_], t_h[li][s]], writes=[t_yb[bk]])
                            P.dma("sp", y_o[t - 1][:, s * 512:(s + 1) * 512], yb[bk], ds_yb[bk], reads=[t_yb[bk]])
                    release_slab()

        xsbs = [xsb, X[:, 0:1024].bitcast(BF16)]
        t_xsbs = [t_xsb, t_xsb2]

        def hn_part1(li):
            b = li % 2
            ssq = st8[:, 0:1]
            src = hbuf[:, li, :]
            P.op("dve", MSET(ssq, 0.0), writes=[t_st8])
            P.op("act", ACTF(xsbs[b][:, :], src, AF.Square, accum=ssq), reads=t_h[li], writes=[t_xsbs[b], t_st8])
            rstd_from_ssq(ssq, rall[:, 0:1], float(D), 1)
            P.op("act", ACTF(xsbs[b][:, :], src, AF.Copy, scale=rall[:, 0:1]), reads=t_h[li] + [t_rall], writes=[t_xsbs[b]])

        def hn_part2(li):
            b = li % 2
            for h0 in (0, 8):
                tpb, tpt = next_tp(True)
                fns = [TRP(tpb[:, (c - h0) * 128:(c - h0 + 1) * 128], xsbs[b][:, c * 128:(c + 1) * 128], ident) for c in range(h0, h0 + 8)]
                P.group("pe", fns, reads=[t_xsbs[b], t_const, t_const2], writes=[tpt])
                P.op("dve", TT(xnT[:, h0:h0 + 8, li * 128:(li + 1) * 128], tpb[:, :].rearrange("p (c t) -> p c t", t=128),
                               gf_col[:, h0:h0 + 8].unsqueeze(2).to_broadcast([128, 8, 128]), ALU.mult),
                     reads=[tpt, t_const, t_const2], writes=[t_xnT[li]])

        def norm_transpose_h(li):
            hn_part1(li)
            hn_part2(li)

        def MSET_ACT(ap):
            return _est(lambda e: e.activation(out=ap, in_=ap, func=AF.Copy, scale=0.0), 0.25)


        try:
            if stop_after == 0:
                raise _Stop()
            for gi, (kv_tiles, ctiles) in enumerate(GROUPS):
                if gi > 0:
                    P.barrier(extra=[("dma", d.sem, d.n) for d in (ds_yb + ds_xr + ds_xt) if d.n > 0])
                nxt = GROUPS[gi + 1][0] if gi + 1 < len(GROUPS) else None
                run_group(gi, kv_tiles, ctiles, next_kv=nxt, s1_done=(2 if (gi > 0 and not debug) else 0))
                if stop_after == 7:
                    raise _Stop()
        except _Stop:
            pass

        final_deps = [("dma", d.sem, d.n) for d in (ds_out, ds_gout, ds_yb[0], ds_yb[1], ds_dbg, ds_dbg2) if d.n > 0]
        P.final_wait("sp", (lambda e: e.nop()), final_deps)
        P.finalize()
        LAST_PROG.clear()
        LAST_PROG.append(P)

        with nc.Block() as block:
            @block.tensor
            def _(e):
                P.replay("pe", e, esem)

            @block.scalar
            def _(e):
                P.replay("act", e, esem)

            @block.vector
            def _(e):
                P.replay("dve", e, esem)

            @block.gpsimd
            def _(e):
                P.replay("pool", e, esem)

            @block.sync
            def _(e):
                P.replay("sp", e, esem)
    return nc


def _rope_tables():
    half = 32
    inv = 10000.0 ** (-np.arange(half, dtype=np.float64) / float(half))
    out = np.zeros((8, 128, NT, 128), np.float32)
    for c in range(8):
        hf = c % 2
        for t in range(NT):
            if t == 9:
                pos = (16384 + (np.arange(128) % 8)).astype(np.float64)
            else:
                pos = (hf * 1024 + (t - 1) * 128 + np.arange(128)).astype(np.float64)
            ang = pos[:, None] * inv[None, :]
            cos = np.cos(ang).astype(np.float32)
            sin = np.sin(ang).astype(np.float32)
            out[c, :, t, 0:32] = cos
            out[c, :, t, 32:64] = cos
            out[c, :, t, 64:96] = -sin
            out[c, :, t, 96:128] = sin
    return out


def _masks(first_valid):
    j = np.arange(128)[:, None]
    i = np.arange(128)[None, :]
    m_cur = np.where(j <= i, 0.0, NEG).astype(np.float32)
    m_prev = np.where(j > i, 0.0, NEG).astype(np.float32)
    m_first = m_prev if first_valid else np.full((128, 128), NEG, np.float32)
    bj, jj = j // 8, j % 8
    bi, ii = i // 8, i % 8
    m_news = np.where((bj == bi) & (jj <= ii), 0.0, NEG).astype(np.float32)
    icol = (np.arange(128) % 8)[None, :]
    m_cache = np.where(j > icol, 0.0, NEG).astype(np.float32)
    ident = np.eye(128, dtype=np.float32)
    return np.concatenate([np.tile(m_cur, (1, 4)), np.tile(m_prev, (1, 4)), np.tile(m_first, (1, 4)),
                           np.tile(m_news, (1, 4)), m_cache, ident], axis=1)


_NC_CACHE = {}


def kernel(x_prompt, x_sample, cache_k_win, cache_v_win, attn_norm, w_in, q_norm, k_norm,
           sinks, sg_norm, sg_w, sg_b, attn_out_norm, sg_out_norm, w_o, ffn_norm,
           w_gate, w_up, w_down):
    f = lambda a: np.ascontiguousarray(np.asarray(a, dtype=np.float32))
    x_prompt, x_sample = f(x_prompt), f(x_sample)
    ck_all = f(cache_k_win)[0].reshape(128, 128, 256)
    cv_all = f(cache_v_win)[0].reshape(128, 128, 256)
    w_in_, w_o_, w_g_, w_u_, w_d_ = f(w_in)[0], f(w_o)[0], f(w_gate)[0], f(w_up)[0], f(w_down)[0]
    colf = lambda v, n: f(v)[0].reshape(n, 128).T
    sgb = f(sg_b)[0]
    bcol = sgb.T
    bcol_s = np.tile(sgb[:, :8].T, (16, 1))
    cols = np.ascontiguousarray(np.concatenate(
        [colf(attn_norm, 16), colf(ffn_norm, 16), colf(attn_out_norm, 8), colf(sg_out_norm, 8), bcol, bcol_s], axis=1))
    bcv = np.ascontiguousarray(np.concatenate([f(q_norm)[0], f(k_norm)[0], f(sinks)[0], f(sg_norm)[0]]))
    sgw = f(sg_w)[0]
    wT = np.ascontiguousarray(sgw.transpose(2, 0, 1)).reshape(128, 1024)
    wT_s = np.zeros((128, 8, 128), np.float32)
    blk = sgw[:, :8, :8].transpose(2, 0, 1)
    for b in range(16):
        wT_s[b * 8:(b + 1) * 8, :, b * 8:(b + 1) * 8] = blk
    wt = np.ascontiguousarray(np.concatenate([wT, wT_s.reshape(128, 1024)], axis=1))
    jj = np.arange(128)[:, None]
    ii = np.arange(128)[None, :]
    cmask = (jj <= ii).astype(np.float32)
    cmask_s = ((jj // 8 == ii // 8) & (jj % 8 <= ii % 8)).astype(np.float32)
    cm = np.ascontiguousarray(np.concatenate([cmask, cmask_s], axis=1))
    rope = _rope_tables()

    in_maps = []
    for c in range(8):
        b, hf = c // 2, c % 2
        xs = np.zeros((NT, 128, D), np.float32)
        if hf == 1:
            xs[0] = x_prompt[b, 896:1024]
        xs[1:9] = x_prompt[b, hf * 1024:(hf + 1) * 1024].reshape(8, 128, D)
        xs[9] = x_sample[16 * c:16 * c + 16].reshape(128, D)
        in_maps.append({
            "xs": xs, "ck": np.ascontiguousarray(ck_all[16 * c:16 * c + 16]), "cv": np.ascontiguousarray(cv_all[16 * c:16 * c + 16]),
            "w_in": w_in_, "w_o": w_o_, "w_gate": w_g_, "w_up": w_u_, "w_down": w_d_,
            "cols": cols, "bcv": bcv, "cs": np.ascontiguousarray(rope[c]), "mk": np.ascontiguousarray(_masks(hf == 1)),
            "wt": wt, "cm": cm,
        })
    if "nc" not in _NC_CACHE:
        _NC_CACHE["nc"] = build_program()
    res = run_bass_kernel_spmd(_NC_CACHE["nc"], in_maps, core_ids=list(range(8)))
    R = res.results
    y_prompt = np.zeros((4, 2048, D), np.float32)
    y_sample = np.zeros((128, 8, D), np.float32)
    kwp = np.zeros((1, 4, 128, 4, 64), np.float32)
    vwp = np.zeros((1, 4, 128, 4, 64), np.float32)
    kws = np.zeros((1, 128, 128, 4, 64), np.float32)
    vws = np.zeros((1, 128, 128, 4, 64), np.float32)
    sgv = np.zeros((1, 128, 8, 1024), np.float32)
    for c in range(8):
        b, hf = c // 2, c % 2
        y = np.asarray(R[c]["y"])
        y_prompt[b, hf * 1024:(hf + 1) * 1024] = y[0:8].reshape(1024, D)
        y_sample[16 * c:16 * c + 16] = y[8].reshape(16, 8, D)
        if hf == 1:
            kwp[0, b] = np.asarray(R[c]["kwp"]).reshape(128, 4, 64)
            vwp[0, b] = np.asarray(R[c]["vwp"]).reshape(128, 4, 64)
        kws[0, 16 * c:16 * c + 16, 0:120] = np.asarray(R[c]["kws"]).reshape(16, 120, 4, 64)
        vws[0, 16 * c:16 * c + 16, 0:120] = np.asarray(R[c]["vws"]).reshape(16, 120, 4, 64)
        kws[0, 16 * c:16 * c + 16, 120:128] = np.asarray(R[c]["knew"]).reshape(16, 8, 4, 64)
        vws[0, 16 * c:16 * c + 16, 120:128] = np.asarray(R[c]["vnew"]).reshape(16, 8, 4, 64)
        sgv[0, 16 * c:16 * c + 16] = np.asarray(R[c]["sgv"]).reshape(16, 8, 1024)
    return (y_prompt, y_sample, kwp, vwp, kws, vws, sgv)
```

```python
import contextlib
import numpy as np
import concourse.bass as bass
import concourse.mybir as mybir
from concourse.bass_utils import run_bass_kernel_spmd

F32 = mybir.dt.float32
BF16 = mybir.dt.bfloat16
AF = mybir.ActivationFunctionType
ALU = mybir.AluOpType
AX = mybir.AxisListType

D = 2048
DC = 16
NT = 10
DFF = 5632
EPS = 1e-6
NEG = -240000.0
GROUPS = [([0, 1, 2, 3, 4], [1, 2, 3, 4]), ([5, 6, 7, 8, 9], [5, 6, 7, 8, 9])]
NG = 5
FFN_PARTS = [(0, 16), (16, 32), (32, 44)]


class Tr:
    def __init__(self):
        self.w = None
        self.r = []


class DSem:
    def __init__(self, sem):
        self.sem = sem
        self.n = 0
        self.last = None

    def next(self):
        self.n += 16
        return ("dma", self.sem, self.n)


class Node:
    __slots__ = ("eng", "fns", "deps", "sig", "est", "tbl", "idx", "cnt", "fin", "dma", "tail")

    def __init__(self, eng, fns, deps, sig, est, tbl, idx):
        self.eng, self.fns, self.deps, self.sig, self.est, self.tbl, self.idx = eng, fns, deps, sig, est, tbl, idx
        self.cnt = None
        self.fin = None
        self.dma = None


SCHED = True
SCHED_WINDOW = 32
SCHED_SLACK = 0.3


class Prog:
    ENG = ("pe", "act", "dve", "pool", "sp")

    def __init__(self):
        self.nodes = []
        self.bar = {e: [] for e in self.ENG}
        self.last = {e: None for e in self.ENG}
        self.ops = {e: [] for e in self.ENG}

    def _deps(self, reads, writes, extra):
        deps = []
        for t in reads:
            if t.w is not None:
                deps.append(t.w)
        for t in writes:
            if t.w is not None:
                deps.append(t.w)
            deps.extend(t.r)
        deps.extend([d for d in extra if d is not None])
        return deps

    def _finish(self, tok, reads, writes):
        for t in writes:
            t.w = tok
            t.r = []
        for t in reads:
            if t not in writes:
                t.r.append(tok)

    def _node(self, eng, fns, deps, sig):
        est = sum(getattr(f, "est", 0.1) for f in fns)
        tbl = getattr(fns[0], "tbl", None)
        n = Node(eng, fns, deps, sig, est, tbl, len(self.nodes))
        self.nodes.append(n)
        self.ops[eng].extend(fns)
        return n

    def op(self, eng, fn, reads=(), writes=(), extra=()):
        return self.group(eng, [fn], reads, writes, extra)

    def group(self, eng, fns, reads=(), writes=(), extra=()):
        deps = self._deps(reads, writes, extra) + self.bar[eng]
        self.bar[eng] = []
        n = self._node(eng, list(fns), deps, True)
        tok = ("n", n)
        self._finish(tok, reads, writes)
        self.last[eng] = tok
        return tok

    def dma(self, eng, out, in_, ds, reads=(), writes=(), extra=(), nbytes=65536):
        deps = self._deps(reads, writes, extra) + self.bar[eng]
        self.bar[eng] = []
        tok0 = ds.next()
        sem = ds.sem

        def fn(e, out=out, in_=in_, sem=sem):
            return e.dma_start(out=out, in_=in_).then_inc(sem, 16)

        fn.est = 1.4 if eng == "pool" else 0.1
        n = self._node(eng, [fn], deps, False)
        n.dma = nbytes
        tok = ("dma", tok0[1], tok0[2], n)
        ds.last = tok
        self._finish(tok, reads, writes)
        return tok

    def barrier(self, engines=("pe", "act", "dve", "sp"), extra=()):
        toks = [self.last[e] for e in ("pe", "act", "dve") if self.last[e] is not None] + list(extra)
        for e in engines:
            self.bar[e] = list(toks)

    def final_wait(self, eng, fn, deps):
        n = self._node(eng, [fn], list(deps), False)
        return n

    def schedule(self):
        per = {e: [n for n in self.nodes if n.eng == e] for e in self.ENG}
        if not SCHED:
            return per
        for n in self.nodes:
            n.tail = 0.0
        for n in reversed(self.nodes):
            w = n.est + n.tail + (n.dma / 300e3 + 2.0 if n.dma is not None else 0.0)
            for d in n.deps:
                nd = d[1] if d[0] == "n" else (d[3] if len(d) > 3 else None)
                if nd is not None and w > nd.tail:
                    nd.tail = w
        pos = {e: 0 for e in self.ENG}
        pend = {e: list(per[e]) for e in self.ENG}
        free = {e: 0.0 for e in self.ENG}
        cur_tbl = [None]
        dma_free = [0.0]
        out = {e: [] for e in self.ENG}
        remaining = len(self.nodes)

        def dep_fin(d):
            nd = d[1] if d[0] == "n" else (d[3] if len(d) > 3 else None)
            if nd is None:
                return 0.0
            return nd.fin

        while remaining:
            best = None
            for e in self.ENG:
                lst = pend[e]
                if not lst:
                    continue
                W = SCHED_WINDOW if e in ("pe", "act", "dve") else 1
                seen = 0
                cands = []
                for n in lst:
                    if seen >= W:
                        break
                    seen += 1
                    ok = True
                    t = free[e]
                    for d in n.deps:
                        f = dep_fin(d)
                        if f is None:
                            ok = False
                            break
                        lat = 0.15 if (d[0] == "n" and d[1].eng != e) else 0.05
                        if f + lat > t:
                            t = f + lat
                    if not ok:
                        continue
                    if e == "act" and n.tbl is not None and n.tbl != cur_tbl[0]:
                        t += 1.3
                    cands.append((t, n))
                if not cands:
                    continue
                tmin = min(c[0] for c in cands)
                t, n = max((c for c in cands if c[0] <= tmin + SCHED_SLACK), key=lambda c: (c[1].tail, -c[1].idx))
                key = (t, n.idx)
                if best is None or key < best[0]:
                    best = (key, e, n)
            key, e, n = best
            t = key[0]
            if e == "act" and n.tbl is not None:
                cur_tbl[0] = n.tbl
            end = t + n.est + 0.08
            free[e] = end
            if n.dma is not None:
                st = max(end, dma_free[0])
                dma_free[0] = st + n.dma / 300e3
                n.fin = dma_free[0] + 2.0
            else:
                n.fin = end
            pend[e].remove(n)
            out[e].append(n)
            remaining -= 1
        self.makespan = max(free.values())
        return out

    def finalize(self):
        order = self.schedule()
        for e in self.ENG:
            c = 0
            for n in order[e]:
                if n.sig:
                    c += 1
                    n.cnt = c
        self.order = order

    def replay(self, name, e, sems):
        waited = {}
        for n in self.order[name]:
            for d in n.deps:
                if d[0] == "dma":
                    key = ("dma", id(d[1]))
                    if waited.get(key, 0) >= d[2]:
                        continue
                    e.wait_ge(d[1], d[2])
                    waited[key] = d[2]
                else:
                    src = d[1]
                    if waited.get(src.eng, 0) >= src.cnt:
                        continue
                    e.wait_ge(sems[src.eng], src.cnt)
                    waited[src.eng] = src.cnt
            ins = None
            for fn in n.fns:
                ins = fn(e)
            if n.sig:
                ins.then_inc(sems[name], 1)


def _free(ap):
    k = 1
    for d in ap.shape[1:]:
        k *= int(d)
    return k


def _est(fn, v, tbl=None):
    fn.est = v
    fn.tbl = tbl
    return fn


def MM(out, lhsT, rhs, start, stop):
    return _est(lambda e: e.matmul(out, lhsT=lhsT, rhs=rhs, start=start, stop=stop, skip_group_check=True),
                max(_free(rhs), 64) / 2400.0 + 0.015)


def TRP(out, in_, ident):
    return _est(lambda e: e.transpose(out=out, in_=in_, identity=ident), 0.075)


def ACTF(out, in_, func, scale=None, bias=None, accum=None):
    kw = {}
    if scale is not None:
        kw["scale"] = scale
    if bias is not None:
        kw["bias"] = bias
    if accum is not None:
        kw["accum_out"] = accum
    tbl = {AF.Exp: "exp", AF.Ln: "exp", AF.Gelu_apprx_tanh: "gelu", AF.Silu: "silu"}.get(func)
    return _est(lambda e: e.activation(out=out, in_=in_, func=func, **kw), 0.2 + _free(out) / 1050.0, tbl)


def TT(out, in0, in1, op):
    return _est(lambda e: e.tensor_tensor(out=out, in0=in0, in1=in1, op=op), 0.12 + _free(out) / 930.0)


def TS(out, in0, s1, op0, s2=None, op1=None):
    if op1 is None:
        return lambda e: e.tensor_scalar(out=out, in0=in0, scalar1=s1, scalar2=None, op0=op0)
    return lambda e: e.tensor_scalar(out=out, in0=in0, scalar1=s1, scalar2=s2, op0=op0, op1=op1)


def STT(out, in0, scalar, in1, op0, op1):
    return _est(lambda e: e.scalar_tensor_tensor(out=out, in0=in0, scalar=scalar, in1=in1, op0=op0, op1=op1), 0.12 + _free(out) / 930.0)


def RSUM(out, in_):
    return _est(lambda e: e.reduce_sum(out=out, in_=in_, axis=AX.X), 0.12 + _free(in_) / 930.0)


def RCP(out, in_):
    return _est(lambda e: e.reciprocal(out=out, in_=in_), 0.2)


def CP(out, in_):
    return _est(lambda e: e.tensor_copy(out=out, in_=in_), 0.12 + _free(out) / 1800.0)


def MSET(ap, v):
    return _est(lambda e: e.memset(ap, v), 0.05 + _free(ap) / 4000.0)


class _Stop(Exception):
    pass


MARKS = []
LAST_PROG = []


def build_program(debug=False, stop_after=None):
    nc = bass.Bass("TRN2", target_bir_lowering=False)
    dt_in = lambda name, shape: nc.dram_tensor(name, shape, F32, kind="ExternalInput").ap()
    dt_out = lambda name, shape: nc.dram_tensor(name, shape, F32, kind="ExternalOutput").ap()

    xs = dt_in("xs", [NT, 128, D])
    ck = dt_in("ck", [16, 128, 256])
    cv = dt_in("cv", [16, 128, 256])
    w_in = dt_in("w_in", [D, 3584])
    w_o = dt_in("w_o", [D, D])
    w_gate = dt_in("w_gate", [D, DFF])
    w_up = dt_in("w_up", [D, DFF])
    w_down = dt_in("w_down", [DFF, D])
    cols_d = dt_in("cols", [128, 64])
    bcv_d = dt_in("bcv", [1168])
    cs_d = dt_in("cs", [128, NT, 128])
    mk_d = dt_in("mk", [128, 2304])
    wt_d = dt_in("wt", [128, 2048])
    cm_d = dt_in("cm", [128, 256])

    y_o = dt_out("y", [9, 128, D])
    kwp_o = dt_out("kwp", [128, 256])
    vwp_o = dt_out("vwp", [128, 256])
    kws_o = dt_out("kws", [16, 120, 256])
    vws_o = dt_out("vws", [16, 120, 256])
    knew_o = dt_out("knew", [128, 256])
    vnew_o = dt_out("vnew", [128, 256])
    sgv_o = dt_out("sgv", [128, 1024])
    if debug:
        dbg_mix = nc.dram_tensor("dbg_mix", [128, 16, NG * 128], BF16, kind="ExternalOutput").ap()
        dbg_h = dt_out("dbg_h", [128, NG, D])

    P = Prog()
    es = contextlib.ExitStack()
    with es:
        def sb(name, shape, dt):
            return es.enter_context(nc.sbuf_tensor("sb_" + name, shape, dt))

        def ps(name, shape, dt):
            return es.enter_context(nc.psum_tensor("ps_" + name, shape, dt))

        def sem(name):
            return es.enter_context(nc.semaphore(name))

        NSLOT = 4
        ring = [sb(f"ring{i}", [128, 16, 512], BF16) for i in range(NSLOT)]
        xnT = sb("xnT", [128, 16, NG * 128], BF16)
        mixT = sb("mixT", [128, 16, NG * 128], BF16)
        Hreg = sb("Hreg", [128, NG * D], F32)
        hbuf = Hreg[:, :].rearrange("p (t d) -> p t d", d=D)
        Hb = Hreg[:, :].bitcast(BF16)
        kT2 = sb("kT2", [64, 4, 6, 128], BF16)
        Vaug = sb("Vaug", [128, 6, 4, 72], BF16)
        g_n = Hb[:, 0:NG * 1024].rearrange("p (t e) -> p t e", e=1024)
        X = sb("X", [128, 4096], F32)
        xsb = sb("xsb", [128, D], BF16)
        o0 = NG * 1024
        qb2 = [Hb[:, o0 + 1024 * i:o0 + 1024 * (i + 1)] for i in range(2)]
        qT2 = [Hb[0:64, o0 + 2048 + 2048 * i:o0 + 2048 + 2048 * (i + 1)].rearrange("p (h t) -> p h t", t=128) for i in range(2)]
        PT2 = [Hb[:, o0 + 6144 + 1024 * i:o0 + 6144 + 1024 * (i + 1)].rearrange("p (b e) -> p b e", e=512) for i in range(2)]
        kbd = Hb[:, o0 + 8192:o0 + 8448]
        assert (o0 + 8448) // 2 <= 7168
        S = sb("S", [128, 3328], F32)
        Sb = S[:, :].bitcast(BF16)
        PTz = Sb[:, 0:2048].rearrange("p (h t) -> p h t", t=128)
        ckd = Sb[:, 2048:2560].rearrange("p (s e) -> p s e", e=256)
        cva = Sb[:, 2560:3136].rearrange("p (s h e) -> p s h e", h=4, e=72)
        kcT2 = Sb[0:64, 3136:3648].rearrange("p (h t) -> p h t", t=128)
        kcT2b = [kcT2, Sb[0:64, 5696:6208].rearrange("p (h t) -> p h t", t=128)]
        ckf = S[:, 1824:2336].rearrange("p (s e) -> p s e", e=256)
        cvf = S[:, 2336:2848].rearrange("p (s e) -> p s e", e=256)
        cols = sb("cols", [128, 64], F32)
        bc = sb("bc", [128, 144], F32)
        cs = sb("cs", [128, NT, 128], F32)
        mk = sb("mk", [128, 2304], BF16)
        wTm = sb("wTm", [128, 2, 8, 128], BF16)
        st8 = sb("st8", [128, 64], F32)
        rall = sb("rall", [128, 64], F32)
        sinkexp = sb("sinkexp", [128, 16], F32)
        kfo = sb("kfo", [128, 2, 256], F32)
        vfo = sb("vfo", [128, 2, 256], F32)
        kfw = sb("kfw", [128, 256], F32)

        mm = [ps(f"mm{i}", [128, 512], F32) for i in range(2)]
        tp = ps("tp", [128, 1024], BF16)
        Bk = [ps(f"bk{i}", [128, 512], F32) for i in range(5)]
        stp = [Bk[0], Bk[1]]

        esem = {e: sem(f"s_{e}") for e in Prog.ENG}
        ds_ring = [DSem(sem(f"d_ring{i}")) for i in range(NSLOT)]
        ds_xt = [DSem(sem(f"d_xt{i}")) for i in range(2)]
        ds_xr = [DSem(sem(f"d_xr{i}")) for i in range(2)]
        ds_yb = [DSem(sem(f"d_yb{i}")) for i in range(2)]
        ds_ck = DSem(sem("d_ck"))
        ds_cv = DSem(sem("d_cv"))
        ds_setup = DSem(sem("d_setup"))
        ds_setup2 = DSem(sem("d_setup2"))
        t_const2 = Tr()
        ds_out = DSem(sem("d_out"))
        ds_gout = DSem(sem("d_gout"))
        ds_gg = DSem(sem("d_gg"))
        t_gg = Tr()
        ds_dbg = DSem(sem("d_dbg"))
        ds_dbg2 = DSem(sem("d_dbg2"))

        xt = [X[:, 0:2048], X[:, 2048:4096]]
        T_sq = X[:, 0:512]
        T_qh = X[:, 512:1024]
        T_m1 = X[:, 1024:1536]
        T_m2 = X[:, 1536:2048]
        T_a = X[:, 2048:3072]
        T_g = X[:, 3072:4096]
        xr = [S[:, 0:512], S[:, 512:1024]]
        yb = [S[:, 1024:1536], S[:, 1536:2048]]
        sgt = [S[:, 2048 + 320 * i:2048 + 320 * (i + 1)] for i in range(4)]
        tpF = tp[:, :].bitcast(F32)
        ga_col = cols[:, 0:16]
        gf_col = cols[:, 16:32]
        gao_col = cols[:, 32:40]
        gso_col = cols[:, 40:48]
        bcol = [cols[:, 48:56], cols[:, 56:64]]
        gq_bc = bc[:, 0:64]
        gk_bc = bc[:, 64:128]
        sinks_bc = bc[:, 128:144]
        gg_bc = Hreg[:, 8192:9216]
        m_cur = mk[:, 0:512]
        m_prev = mk[:, 512:1024]
        m_first = mk[:, 1024:1536]
        m_news = mk[:, 1536:2048]
        m_cache = mk[:, 2048:2176]
        ident = mk[:, 2176:2304]

        t_stq, t_rq, t_stk, t_rk = Tr(), Tr(), Tr(), Tr()
        t_ksq, t_kqh, t_km1 = Tr(), Tr(), Tr()
        t_ring = [Tr() for _ in range(NSLOT)]
        t_xt = [Tr(), Tr()]
        t_xsb = Tr()
        t_xsb2 = Tr()
        t_tp = Tr()
        t_mm = [Tr(), Tr()]
        t_B = [Tr() for _ in range(5)]
        t_st = [t_B[0], t_B[1]]
        t_xnT = [Tr() for _ in range(NG)]
        t_mixA = [Tr() for _ in range(NG)]
        t_mixS = [Tr() for _ in range(NG)]
        t_act = Tr()
        t_h = [[Tr() for _ in range(4)] for _ in range(NG)]
        t_kT2 = [Tr() for _ in range(6)]
        t_V = [Tr() for _ in range(6)]
        t_gn = [Tr() for _ in range(NG)]
        t_sq, t_qh, t_m1, t_m2, t_a, t_g = Tr(), Tr(), Tr(), Tr(), Tr(), Tr()
        t_kbd, t_PTz = Tr(), Tr()
        t_qb2, t_qT2 = [Tr(), Tr()], [Tr(), Tr()]
        t_PT2 = [[Tr(), Tr()], [Tr(), Tr()]]
        t_ckf, t_cvf, t_ckd, t_cva, t_kcT2 = Tr(), Tr(), Tr(), Tr(), Tr()
        t_kcT2b = [t_kcT2, Tr()]
        t_xr, t_yb, t_sgt = [Tr(), Tr()], [Tr(), Tr()], [Tr() for _ in range(4)]
        t_ovb = [Tr(), Tr(), Tr()]
        t_const = Tr()
        t_wTm = Tr()
        t_st8 = Tr()
        t_st8b = Tr()
        t_rallb = Tr()
        t_rall = Tr()
        t_kfw = Tr()
        t_kfo, t_vfo = [Tr(), Tr()], [Tr(), Tr()]

        slab_list = []
        w_in_v = w_in.rearrange("(c p) e -> p c e", p=128)
        w_o_v = w_o.rearrange("(c p) e -> p c e", p=128)
        w_g_v = w_gate.rearrange("(c p) e -> p c e", p=128)
        w_u_v = w_up.rearrange("(c p) e -> p c e", p=128)
        w_d_v = w_down.rearrange("(c p) e -> p c e", p=128)

        def full_slab(src_v, c0):
            return [(lambda r, h=h: r[:, 8 * h:8 * h + 8, :], src_v[:, 8 * h:8 * h + 8, c0:c0 + 512]) for h in range(2)]

        for _g in range(len(GROUPS)):
            for c0 in (1024, 0, 512, 2560, 3072, 1536, 2048):
                slab_list.append(full_slab(w_in_v, c0))
            for s in range(4):
                slab_list.append(full_slab(w_o_v, s * 512))
            for (pc0, pc1) in FFN_PARTS:
                for j in range(pc0 // 4, pc1 // 4):
                    slab_list.append(full_slab(w_g_v, 512 * j))
                    slab_list.append(full_slab(w_u_v, 512 * j))
                nch = pc1 - pc0
                for s in range(4):
                    hh = nch // 2
                    slab_list.append([
                        (lambda r, hh=hh: r[:, 0:hh, :], w_d_v[:, pc0:pc0 + hh, s * 512:(s + 1) * 512]),
                        (lambda r, hh=hh, nch=nch: r[:, hh:nch, :], w_d_v[:, pc0 + hh:pc1, s * 512:(s + 1) * 512]),
                    ])
        slab_state = {"loaded": 0, "used": 0}
        MARKS.clear()

        def load_next_slab():
            n = slab_state["loaded"]
            if n >= len(slab_list):
                return
            slot = n % NSLOT
            for (dst_fn, src) in slab_list[n]:
                P.dma("pool", dst_fn(ring[slot]), src, ds_ring[slot], writes=[t_ring[slot]], nbytes=2 * 1024 * 1024)
            slab_state["loaded"] = n + 1

        def next_slab():
            n = slab_state["used"]
            slab_state["used"] = n + 1
            assert n < slab_state["loaded"]
            return ring[n % NSLOT], t_ring[n % NSLOT]

        def release_slab():
            load_next_slab()

        P.dma("sp", cols[:, :], cols_d, ds_setup, writes=[t_const])
        P.dma("sp", bc[:, :], bcv_d[0:144].partition_broadcast(128), ds_setup, writes=[t_const])
        P.dma("sp", cs[:, :, :], cs_d, ds_setup, writes=[t_const])
        P.dma("sp", X[:, 0:2048], wt_d, ds_setup, writes=[t_const])
        P.dma("sp", X[:, 2048:2304], cm_d, ds_setup, writes=[t_const])
        P.dma("pool", mk[:, :], mk_d, ds_setup2, writes=[t_const2])
        P.dma("sp", kws_o, ck[:, 8:128, :], ds_out)
        P.dma("sp", vws_o, cv[:, 8:128, :], ds_out)
        for _ in range(NSLOT):
            load_next_slab()

        P.op("dve", MSET(Vaug[:, :, :, :], 1.0), writes=t_V)
        for k in range(2):
            P.op("dve", TT(wTm[:, k, :, :], X[:, k * 1024:(k + 1) * 1024].rearrange("p (h i) -> p h i", i=128),
                           X[:, 2048 + 128 * k:2048 + 128 * (k + 1)].unsqueeze(1).to_broadcast([128, 8, 128]), ALU.mult),
                 reads=[t_const], writes=[t_wTm] + ([t_xt[0], t_xt[1]] if k == 1 else []))
        P.op("act", ACTF(sinkexp[:, :], sinks_bc, AF.Exp), reads=[t_const, t_const2])

        def rstd_from_ssq(ssq_ap, out_ap, n, width, st_tr=None, r_tr=None):
            st_tr = st_tr or t_st8
            r_tr = r_tr or t_rall
            P.op("act", ACTF(ssq_ap, ssq_ap, AF.Ln, scale=1.0 / n, bias=EPS), reads=[st_tr], writes=[st_tr])
            return P.op("act", ACTF(out_ap, ssq_ap, AF.Exp, scale=-0.5), reads=[st_tr], writes=[r_tr])

        tp_pool = [(tp, t_tp), (Bk[2][:, :].bitcast(BF16), t_B[2]), (Bk[3][:, :].bitcast(BF16), t_B[3]), (Bk[4][:, :].bitcast(BF16), t_B[4])]
        tp_rr = [0]

        def next_tp(rotate):
            if not rotate:
                return tp, t_tp
            tp_rr[0] = (tp_rr[0] + 1) % len(tp_pool)
            return tp_pool[tp_rr[0]]

        def norm_transpose(src_ap, src_tr, nchunks, dstT, dst_col0, dst_tr, gcol, c_off=0, rotate=False):
            W = nchunks * 128
            ssq = st8[:, 0:1]
            P.op("dve", MSET(ssq, 0.0), writes=[t_st8])
            P.op("act", ACTF(xsb[:, 0:W], src_ap, AF.Square, accum=ssq), reads=[src_tr], writes=[t_xsb, t_st8])
            rstd_from_ssq(ssq, rall[:, 0:1], float(W), 1)
            P.op("act", ACTF(xsb[:, 0:W], src_ap, AF.Copy, scale=rall[:, 0:1]), reads=[src_tr, t_rall], writes=[t_xsb])
            for h0 in range(0, nchunks, 8):
                tpb, tpt = next_tp(rotate)
                fns = [TRP(tpb[:, (c - h0) * 128:(c - h0 + 1) * 128], xsb[:, c * 128:(c + 1) * 128], ident)
                       for c in range(h0, h0 + 8)]
                P.group("pe", fns, reads=[t_xsb, t_const, t_const2], writes=[tpt])
                P.op("dve", TT(dstT[:, c_off + h0:c_off + h0 + 8, dst_col0:dst_col0 + 128],
                               tpb[:, :].rearrange("p (c t) -> p c t", t=128),
                               gcol[:, h0:h0 + 8].unsqueeze(2).to_broadcast([128, 8, 128]), ALU.mult),
                     reads=[tpt, t_const, t_const2], writes=[dst_tr])

        def dense_B(srcT, col0, src_tr, slab, slab_tr, nch, bank, bank_tr, ch0=0, ncols=512, sc0=0):
            fns = [MM(bank[:, 0:ncols], srcT[:, ch0 + c, col0:col0 + 128], slab[:, c, sc0:sc0 + ncols], c == 0, c == nch - 1)
                   for c in range(nch)]
            return P.group("pe", fns, reads=[src_tr, slab_tr], writes=[bank_tr])

        def qk_norm_rope(src_bank, src_tr, nh, gbc, t, out_ap, out_tr, TS_):
            W = nh * 64
            v3 = lambda ap: ap.rearrange("p (h d) -> p h d", d=64)
            sq, qh, m1 = TS_["sq"][:, 0:W], TS_["qh"][:, 0:W], TS_["m1"][:, 0:W]
            m2 = sq
            tsq, tqh, tm1, tst, tr_ = TS_["tsq"], TS_["tqh"], TS_["tm1"], TS_["tst"], TS_["tr"]
            stv, rv = TS_["st"][:, 0:nh], TS_["r"][:, 0:nh]
            P.op("act", ACTF(sq, src_bank[:, 0:W], AF.Square), reads=[src_tr], writes=[tsq])
            P.op("dve", RSUM(stv, v3(sq)), reads=[tsq], writes=[tst])
            rstd_from_ssq(stv, rv, 64.0, nh, tst, tr_)
            P.op("dve", TT(v3(qh), v3(src_bank[:, 0:W]), rv.unsqueeze(2).to_broadcast([128, nh, 64]), ALU.mult),
                 reads=[src_tr, tr_], writes=[tqh])
            P.op("dve", TT(v3(qh), v3(qh), gbc.unsqueeze(1).to_broadcast([128, nh, 64]), ALU.mult),
                 reads=[t_const, t_const2], writes=[tqh])
            csA = cs[:, t, 0:64].unsqueeze(1).to_broadcast([128, nh, 64])
            P.op("dve", TT(v3(m1), v3(qh), csA, ALU.mult), reads=[tqh, t_const, t_const2], writes=[tm1])
            P.op("dve", TT(v3(m2)[:, :, 0:32], v3(qh)[:, :, 32:64],
                           cs[:, t, 64:96].unsqueeze(1).to_broadcast([128, nh, 32]), ALU.mult),
                 reads=[tqh, t_const, t_const2], writes=[tsq])
            P.op("dve", TT(v3(m2)[:, :, 32:64], v3(qh)[:, :, 0:32],
                           cs[:, t, 96:128].unsqueeze(1).to_broadcast([128, nh, 32]), ALU.mult),
                 reads=[tqh], writes=[tsq])
            P.op("dve", TT(out_ap, m1, m2, ALU.add), reads=[tm1, tsq], writes=[out_tr])

        def head_slot(h):
            return Bk[2 + h // 7], t_B[2 + h // 7], (h % 7) * 72

        def S1_tile(t, i):
            b = i % 2
            P.dma("sp", xt[b], xs[t], ds_xt[b], writes=[t_xt[b]] + ([t_xsb2] if b == 0 else []), nbytes=1024 * 1024)
            norm_transpose(xt[b], t_xt[b], 16, xnT, i * 128, t_xnT[i], ga_col, rotate=True)

        def run_group(gi, kv_tiles, ctiles, next_kv=None, s1_done=0):
            nct = len(ctiles)
            xcol = {t: (i * 128) for i, t in enumerate(kv_tiles)}
            if kv_tiles[0] != ctiles[0]:
                xblk = {t: i for i, t in enumerate(kv_tiles)}
            else:
                xblk = {t: i for i, t in enumerate(kv_tiles)}
            li_of = {t: i for i, t in enumerate(ctiles)}

            QS = dict(sq=T_sq, qh=T_qh, m1=T_m1, tsq=t_sq, tqh=t_qh, tm1=t_m1,
                      st=st8[:, 32:40], r=rall[:, 32:40], tst=t_stq, tr=t_rq)
            KS = dict(sq=Hreg[:, 7168:7424], qh=Hreg[:, 7424:7680], m1=Hreg[:, 7680:7936], tsq=t_ksq, tqh=t_kqh, tm1=t_km1,
                      st=st8[:, 48:52], r=rall[:, 48:52], tst=t_stk, tr=t_rk)

            def S1(i):
                if i >= s1_done:
                    S1_tile(kv_tiles[i], i)

            slab, slab_tr = next_slab()

            def S2mm(i):
                t = kv_tiles[i]
                dense_B(xnT, xcol[t], t_xnT[xblk[t]], slab, slab_tr, 16, mm[i % 2], t_mm[i % 2])

            def S2post(i):
                t = kv_tiles[i]
                bk = i % 2
                if t == 8:
                    kf, kf_tr, vf, vf_tr = kfo[:, 0, :], t_kfo[0], vfo[:, 0, :], t_vfo[0]
                elif t == 9:
                    kf, kf_tr, vf, vf_tr = kfo[:, 1, :], t_kfo[1], vfo[:, 1, :], t_vfo[1]
                else:
                    kf, kf_tr, vf, vf_tr = kfw[:, :], t_kfw, None, None
                qk_norm_rope(mm[bk], t_mm[bk], 4, gk_bc, t, kf, kf_tr, KS)
                P.op("act", ACTF(kbd, kf, AF.Copy), reads=[kf_tr], writes=[t_kbd])
                P.op("act", ACTF(Vaug[:, t % 6, :, 0:64], mm[bk][:, 256:512].rearrange("p (h d) -> p h d", d=64), AF.Copy),
                     reads=[t_mm[bk]], writes=[t_V[t % 6]])
                if vf is not None:
                    P.op("act", ACTF(vf, mm[bk][:, 256:512], AF.Copy), reads=[t_mm[bk]], writes=[vf_tr])
                    if t == 8:
                        P.dma("sp", kwp_o, kf, ds_out, reads=[kf_tr])
                        P.dma("sp", vwp_o, vf, ds_out, reads=[vf_tr])
                    else:
                        P.dma("sp", knew_o, kf, ds_out, reads=[kf_tr])
                        P.dma("sp", vnew_o, vf, ds_out, reads=[vf_tr])
                fns = [TRP(tp[0:64, h * 128:(h + 1) * 128], kbd[:, h * 64:(h + 1) * 64], ident) for h in range(4)]
                P.group("pe", fns, reads=[t_kbd, t_const, t_const2], writes=[t_tp])
                P.op("dve", CP(kT2[:, :, t % 6, :], tp[0:64, 0:512].rearrange("p (h t) -> p h t", t=128)),
                     reads=[t_tp], writes=[t_kT2[t % 6]])

            nkv = len(kv_tiles)
            S1(0)
            if nkv > 1:
                S1(1)
            S2mm(0)
            for i in range(nkv):
                if i + 2 < nkv:
                    S1(i + 2)
                if i + 1 < nkv:
                    S2mm(i + 1)
                S2post(i)
            release_slab()

            MARKS.append((gi, 1, len(P.ops['pe'])))
            MARKS.append((gi, 2, len(P.ops['pe'])))
            if stop_after in (1, 2):
                raise _Stop()
            s0, s0_tr = next_slab()
            s1, s1_tr = next_slab()

            def A3(i):
                t = ctiles[i]
                for hq, (sl, sl_tr) in enumerate(((s0, s0_tr), (s1, s1_tr))):
                    dense_B(xnT, xcol[t], t_xnT[xblk[t]], sl, sl_tr, 16, mm[hq], t_mm[hq])

            def B3_chain(i, hq):
                t = ctiles[i]
                par = i % 2
                qk_norm_rope(mm[hq], t_mm[hq], 8, gq_bc, t, qb2[par][:, hq * 512:(hq + 1) * 512], t_qb2[par], QS)

            def B3_tr(i, hq):
                par = i % 2
                fns = [TRP(tp[0:64, c * 128:(c + 1) * 128], qb2[par][:, hq * 512 + c * 64:hq * 512 + (c + 1) * 64], ident) for c in range(8)]
                P.group("pe", fns, reads=[t_qb2[par], t_const, t_const2], writes=[t_tp])
                P.op("dve", CP(qT2[par][:, hq * 8:(hq + 1) * 8, :], tp[0:64, 0:1024].rearrange("p (c t) -> p c t", t=128)),
                     reads=[t_tp], writes=[t_qT2[par]])

            def blocks_of(t):
                if t == 9:
                    return [(9 % 6, m_news)]
                if t == 1:
                    return [(0, m_first), (1, m_cur)]
                return [((t - 1) % 6, m_prev), (t % 6, m_cur)]

            def C3_ST(i, kvh):
                t = ctiles[i]
                par = i % 2
                qT_, tqT_ = qT2[par], t_qT2[par]
                for bi, (slot, mask) in enumerate(blocks_of(t)):
                    bidx = (kvh % 2) * 2 + bi if t != 9 else 0
                    bank, btr = Bk[bidx], t_B[bidx]
                    fns = [MM(bank[:, :], ident, mask, True, False)]
                    for g in range(4):
                        h = 4 * kvh + g
                        fns.append(MM(bank[:, g * 128:(g + 1) * 128], kT2[:, kvh, slot, :], qT_[:, h, :], False, g == 3))
                    P.group("pe", fns, reads=[t_kT2[slot], tqT_, t_const, t_const2], writes=[btr])
                    pp = kvh % 2
                    P.op("act", ACTF(PT2[pp][:, bi, :], bank[:, :], AF.Exp, scale=0.125), reads=[btr], writes=[t_PT2[pp][bi]])

            def C3_PV(i, kvh):
                t = ctiles[i]
                blks = blocks_of(t)
                pp = kvh % 2
                fns = []
                wr = set()
                for g in range(4):
                    h = 4 * kvh + g
                    if t == 9:
                        bank, btr, c0 = head_slot(h)
                        first = (h % 7 == 0)
                    else:
                        bank, btr, c0 = Bk[4], t_B[4], g * 72
                        first = (g == 0)
                    wr.add(btr)
                    for bi, (slot, mask) in enumerate(blks):
                        last = (bi == len(blks) - 1) and (t != 9)
                        fns.append(MM(bank[:, c0:c0 + 72], PT2[pp][:, bi, g * 128:(g + 1) * 128], Vaug[:, slot, kvh, :],
                                      bi == 0 and first, last))
                P.group("pe", fns, reads=[t_PT2[pp][0], t_PT2[pp][1]] + [t_V[s_] for s_, _ in blks], writes=list(wr))
                if t != 9:
                    v = Bk[4][:, 0:288].rearrange("p (h e) -> p h e", e=72)
                    P.op("act", ACTF(T_g[:, kvh * 256:(kvh + 1) * 256].rearrange("p (h d) -> p h d", d=64), v[:, :, 0:64], AF.Copy),
                         reads=[t_B[4]], writes=[t_g])
                    P.op("act", ACTF(st8[:, 16 + 4 * kvh:20 + 4 * kvh], v[:, :, 64], AF.Copy), reads=[t_B[4]], writes=[t_st8b])
                    if kvh == 3:
                        P.op("dve", TT(st8[:, 16:32], st8[:, 16:32], sinkexp[:, 0:16], ALU.add), reads=[t_st8b], writes=[t_st8b])
                        P.op("dve", RCP(rall[:, 16:32], st8[:, 16:32]), reads=[t_st8b], writes=[t_rallb])
                        P.op("dve", TT(T_a.rearrange("p (h d) -> p h d", d=64), T_g.rearrange("p (h d) -> p h d", d=64),
                                       rall[:, 16:32].unsqueeze(2).to_broadcast([128, 16, 64]), ALU.mult),
                             reads=[t_g, t_rallb], writes=[t_a])

            def evac(bank, btr, nh, h0):
                v = bank[:, 0:nh * 72].rearrange("p (h e) -> p h e", e=72)
                P.op("dve", TT(st8[:, 16:16 + nh], v[:, :, 64], sinkexp[:, h0:h0 + nh], ALU.add), reads=[btr], writes=[t_st8b])
                P.op("dve", RCP(rall[:, 16:16 + nh], st8[:, 16:16 + nh]), reads=[t_st8b], writes=[t_rallb])
                P.op("dve", TT(T_a[:, h0 * 64:(h0 + nh) * 64].rearrange("p (h d) -> p h d", d=64), v[:, :, 0:64],
                               rall[:, 16:16 + nh].unsqueeze(2).to_broadcast([128, nh, 64]), ALU.mult),
                     reads=[btr, t_rallb], writes=[t_a])

            def C3_sample_cache(i):
                par = i % 2
                qT_, tqT_ = qT2[par], t_qT2[par]
                P.op("dve", MSET(cva, 1.0), writes=[t_cva])
                P.op("dve", MSET(PTz, 0.0), writes=[t_PTz])

                def load(sg):
                    P.dma("sp", ckf, ck[2 * sg:2 * sg + 2].rearrange("s k e -> k s e"), ds_ck, writes=[t_ckf])
                    P.dma("sp", cvf, cv[2 * sg:2 * sg + 2].rearrange("s k e -> k s e"), ds_cv, writes=[t_cvf])

                def cast(sg):
                    P.op("act", ACTF(ckd, ckf, AF.Copy), reads=[t_ckf], writes=[t_ckd])

                def castv(sg):
                    P.op("dve", CP(cva[:, :, :, 0:64], cvf.rearrange("p s (h d) -> p s h d", d=64)),
                         reads=[t_cvf], writes=[t_cva])

                def TR(b):
                    s_ = b % 2
                    fns = [TRP(tp[0:64, h * 128:(h + 1) * 128], ckd[:, s_, h * 64:(h + 1) * 64], ident) for h in range(4)]
                    P.group("pe", fns, reads=[t_ckd, t_const, t_const2], writes=[t_tp])
                    P.op("dve", CP(kcT2b[b % 2], tp[0:64, 0:512].rearrange("p (h t) -> p h t", t=128)), reads=[t_tp], writes=[t_kcT2b[b % 2]])

                def ST(b):
                    kc, kct = kcT2b[b % 2], t_kcT2b[b % 2]
                    fns = [MM(Bk[1][:, 0:128], ident, m_cache, True, False)]
                    for h in range(16):
                        fns.append(MM(Bk[1][:, h * 8:(h + 1) * 8], kc[:, h // 4, :], qT_[:, h, b * 8:(b + 1) * 8], False, h == 15))
                    P.group("pe", fns, reads=[kct, tqT_, t_const, t_const2], writes=[t_B[1]])
                    P.op("act", ACTF(PTz[:, :, b * 8:(b + 1) * 8], Bk[1][:, 0:128].rearrange("p (h i) -> p h i", i=8), AF.Exp, scale=0.125),
                         reads=[t_B[1]], writes=[t_PTz])

                def PV(b):
                    s_ = b % 2
                    fns = []
                    for h in range(16):
                        bank, btr, c0 = head_slot(h)
                        fns.append(MM(bank[:, c0:c0 + 72], PTz[:, h, :], cva[:, s_, h // 4, :], False, b == 15))
                    P.group("pe", fns, reads=[t_PTz, t_cva], writes=[t_B[2], t_B[3], t_B[4]])
                    P.op("act", MSET_ACT(PTz[:, :, b * 8:(b + 1) * 8]), reads=[], writes=[t_PTz])

                load(0); cast(0); castv(0)
                TR(0)
                for b in range(16):
                    ST(b)
                    if b % 2 == 0:
                        TR(b + 1)
                    PV(b)
                    if b % 2 == 1 and b + 1 < 16:
                        load((b + 1) // 2); cast((b + 1) // 2); castv((b + 1) // 2)
                        TR(b + 1)
                for bnk in range(3):
                    nh = 7 if bnk < 2 else 2
                    evac(Bk[2 + bnk], t_B[2 + bnk], nh, 7 * bnk)

            def C3_fin(i):
                li = i
                norm_transpose(T_a, t_a, 8, mixT, li * 128, t_mixA[li], gao_col, c_off=0)

            A3(0)
            B3_chain(0, 0); B3_tr(0, 0); B3_chain(0, 1); B3_tr(0, 1)
            if nct > 1:
                A3(1)
            for i in range(nct):
                nxt = i + 1 < nct
                t = ctiles[i]
                C3_ST(i, 0)
                C3_ST(i, 1)
                if nxt:
                    B3_chain(i + 1, 0)
                C3_PV(i, 0)
                C3_ST(i, 2)
                C3_PV(i, 1)
                C3_ST(i, 3)
                if nxt:
                    B3_tr(i + 1, 0)
                    B3_chain(i + 1, 1)
                C3_PV(i, 2)
                C3_PV(i, 3)
                if nxt:
                    B3_tr(i + 1, 1)
                    if i + 2 < nct:
                        A3(i + 2)
                if t == 9:
                    C3_sample_cache(i)
                C3_fin(i)
            release_slab()
            release_slab()

            MARKS.append((gi, 3, len(P.ops['pe'])))
            if stop_after == 3:
                raise _Stop()
            P.dma("sp", gg_bc, bcv_d[144:1168].partition_broadcast(128), ds_gg, writes=[t_gg])
            s0, s0_tr = next_slab()
            s1, s1_tr = next_slab()
            for t in ctiles:
                li = li_of[t]
                for hq, (sl, sl_tr) in enumerate(((s0, s0_tr), (s1, s1_tr))):
                    dense_B(xnT, xcol[t], t_xnT[xblk[t]], sl, sl_tr, 16, mm[hq], t_mm[hq])
                for hq in range(2):
                    P.op("act", ACTF(T_g[:, hq * 512:(hq + 1) * 512], mm[hq][:, :], AF.Gelu_apprx_tanh), reads=[t_mm[hq]], writes=[t_g])
                ssq = st8[:, 0:1]
                P.op("dve", MSET(ssq, 0.0), writes=[t_st8])
                P.op("act", ACTF(xsb[:, 0:1024], T_g, AF.Square, accum=ssq), reads=[t_g], writes=[t_xsb, t_st8])
                rstd_from_ssq(ssq, rall[:, 0:1], 1024.0, 1)
                P.op("dve", STT(g_n[:, li, :], T_g, rall[:, 0:1], gg_bc, ALU.mult, ALU.mult), reads=[t_g, t_rall, t_gg], writes=[t_gn[li]])
                if t == 9:
                    P.op("dve", STT(T_a, T_g, rall[:, 0:1], gg_bc, ALU.mult, ALU.mult), reads=[t_g, t_rall, t_gg], writes=[t_a])
                    P.dma("sp", sgv_o, T_a, ds_gout, reads=[t_a])
            release_slab()
            release_slab()

            MARKS.append((gi, 4, len(P.ops['pe'])))
            if stop_after == 4:
                raise _Stop()
            s0, s0_tr = next_slab()
            s1, s1_tr = next_slab()

            def A5(i):
                t = ctiles[i]
                for hq, (sl, sl_tr) in enumerate(((s0, s0_tr), (s1, s1_tr))):
                    dense_B(xnT, xcol[t], t_xnT[xblk[t]], sl, sl_tr, 16, mm[hq], t_mm[hq])

            A5(0)
            for i, t in enumerate(ctiles):
                li = li_of[t]
                kk = 1 if t == 9 else 0
                for hq in range(2):
                    P.op("act", ACTF(T_g[:, hq * 512:(hq + 1) * 512], mm[hq][:, :], AF.Gelu_apprx_tanh), reads=[t_mm[hq]], writes=[t_g])
                for hq in range(2):
                    fns = [MM(stp[hq][:, j * 128:(j + 1) * 128], wTm[:, kk, hq * 4 + j, :],
                              g_n[:, li, (hq * 4 + j) * 128:(hq * 4 + j + 1) * 128], True, True) for j in range(4)]
                    P.group("pe", fns, reads=[t_gn[li], t_wTm], writes=[t_st[hq]])
                if i + 1 < nct:
                    A5(i + 1)
                for hd in range(8):
                    P.op("dve", STT(T_a[:, hd * 128:(hd + 1) * 128], stp[hd // 4][:, (hd % 4) * 128:(hd % 4 + 1) * 128],
                                    bcol[kk][:, hd:hd + 1], T_g[:, hd * 128:(hd + 1) * 128], ALU.add, ALU.mult),
                         reads=[t_st[hd // 4], t_g, t_const, t_const2], writes=[t_a])
                norm_transpose(T_a, t_a, 8, mixT, li * 128, t_mixS[li], gso_col, c_off=8)
            release_slab()
            release_slab()

            MARKS.append((gi, 5, len(P.ops['pe'])))
            if stop_after == 5:
                raise _Stop()
            if debug and gi == 0:
                P.dma("sp", dbg_mix, mixT[:, :, :], ds_dbg, reads=t_mixA[0:nct] + t_mixS[0:nct])
            P.barrier(engines=("pe", "act", "dve", "sp"))
            for s in range(4):
                slab, slab_tr = next_slab()
                for t in ctiles:
                    li = li_of[t]
                    bk = li % 2
                    fns = [MM(mm[bk][:, :], mixT[:, c, li * 128:(li + 1) * 128], slab[:, c, :], c == 0, c == 15) for c in range(16)]
                    P.group("pe", fns, reads=[t_mixA[li], t_mixS[li], slab_tr], writes=[t_mm[bk]])
                    P.dma("sp", xr[bk], xs[t][:, s * 512:(s + 1) * 512], ds_xr[bk], writes=[t_xr[bk]])
                    P.op("dve", TT(hbuf[:, li, s * 512:(s + 1) * 512], mm[bk][:, :], xr[bk], ALU.add),
                         reads=[t_mm[bk], t_xr[bk]], writes=[t_h[li][s]])
                    if s == 3 and not debug:
                        hn_part1(li)
                        if li > 0:
                            hn_part2(li - 1)
                if s == 3 and not debug:
                    hn_part2(nct - 1)
                release_slab()
            if debug and gi == 0:
                P.dma("sp", dbg_h, hbuf[:, :, :], ds_dbg2, reads=[x for l in t_h for x in l])
            if debug:
                for t in ctiles:
                    norm_transpose_h(li_of[t])

            MARKS.append((gi, 6, len(P.ops['pe'])))
            if stop_after == 6:
                raise _Stop()
            ntok = nct * 128
            t_act.w = None
            t_act.r = [P.last["pe"]] + [tok for tr in (t_mixA + t_mixS) for tok in tr.r]
            banks = [mm[0], mm[1], Bk[0], Bk[1], Bk[2], Bk[3], Bk[4], tpF]
            t_banks = [t_mm[0], t_mm[1]] + t_B + [t_tp]
            chunks = [(0, ntok)] if ntok <= 512 else [(0, ntok // 2), (ntok // 2, ntok)]
            nck = len(chunks)
            wd_cnt = 0
            for pi, (pc0, pc1) in enumerate(FFN_PARTS):
                nch = pc1 - pc0
                for j in range(nch // 4):
                    slabG, slabG_tr = next_slab()
                    slabU, slabU_tr = next_slab()
                    for fc in range(4):
                        fcl = 4 * j + fc
                        st_ = fcl % 2
                        for ci, (ca, cb) in enumerate(chunks):
                            bg = (st_ * nck + ci) * 2
                            bu_ = bg + 1
                            n = cb - ca
                            fg = [MM(banks[bg][:, 0:n], slabG[:, c, fc * 128:(fc + 1) * 128], xnT[:, c, ca:cb], c == 0, c == 15) for c in range(16)]
                            P.group("pe", fg, reads=[slabG_tr] + t_xnT[0:nct], writes=[t_banks[bg]])
                            fu = [MM(banks[bu_][:, 0:n], slabU[:, c, fc * 128:(fc + 1) * 128], xnT[:, c, ca:cb], c == 0, c == 15) for c in range(16)]
                            P.group("pe", fu, reads=[slabU_tr] + t_xnT[0:nct], writes=[t_banks[bu_]])
                            si = st_ * 2 + ci
                            sg_ap = S[:, 2048 + 640 * st_:2048 + 640 * st_ + n] if nck == 1 else sgt[si][:, 0:n]
                            P.op("act", ACTF(sg_ap, banks[bg][:, 0:n], AF.Silu), reads=[t_banks[bg]], writes=[t_sgt[si]])
                            P.op("dve", TT(mixT[:, fcl, ca:cb], sg_ap, banks[bu_][:, 0:n], ALU.mult),
                                 reads=[t_sgt[si], t_banks[bu_]], writes=[t_act])
                    release_slab()
                    release_slab()
                last_part = pi == len(FFN_PARTS) - 1
                if last_part and next_kv is not None and not debug:
                    S1_tile(next_kv[0], 0)
                    S1_tile(next_kv[1], 1)
                for s in range(4):
                    slab, slab_tr = next_slab()
                    for t in ctiles:
                        li = li_of[t]
                        bi_ = wd_cnt % 8
                        wd_cnt += 1
                        bk = li % 2
                        fns = [MM(banks[bi_][:, :], mixT[:, c, li * 128:(li + 1) * 128], slab[:, c, :], c == 0, c == nch - 1) for c in range(nch)]
                        P.group("pe", fns, reads=[t_act, slab_tr], writes=[t_banks[bi_]])
                        if not last_part:
                            P.op("dve", TT(hbuf[:, li, s * 512:(s + 1) * 512], banks[bi_][:, :], hbuf[:, li, s * 512:(s + 1) * 512], ALU.add),
                                 reads=[t_banks[bi_]], writes=[t_h[li][s]])
                        else:
                            P.op("dve", TT(yb[bk], banks[bi_][:, :], hbuf[:, li, s * 512:(s + 1) * 512], ALU.add),
                                 reads=[t_banks[bi_], t_h[li][s]], writes=[t_yb[bk]])
                            P.dma("sp", y_o[t - 1][:, s * 512:(s + 1) * 512], yb[bk], ds_yb[bk], reads=[t_yb[bk]])
                    release_slab()

        xsbs = [xsb, X[:, 0:1024].bitcast(BF16)]
        t_xsbs = [t_xsb, t_xsb2]

        def hn_part1(li):
            b = li % 2
            ssq = st8[:, 0:1]
            src = hbuf[:, li, :]
            P.op("dve", MSET(ssq, 0.0), writes=[t_st8])
            P.op("act", ACTF(xsbs[b][:, :], src, AF.Square, accum=ssq), reads=t_h[li], writes=[t_xsbs[b], t_st8])
            rstd_from_ssq(ssq, rall[:, 0:1], float(D), 1)
            P.op("act", ACTF(xsbs[b][:, :], src, AF.Copy, scale=rall[:, 0:1]), reads=t_h[li] + [t_rall], writes=[t_xsbs[b]])

        def hn_part2(li):
            b = li % 2
            for h0 in (0, 8):
                tpb, tpt = next_tp(True)
                fns = [TRP(tpb[:, (c - h0) * 128:(c - h0 + 1) * 128], xsbs[b][:, c * 128:(c + 1) * 128], ident) for c in range(h0, h0 + 8)]
                P.group("pe", fns, reads=[t_xsbs[b], t_const, t_const2], writes=[tpt])
                P.op("dve", TT(xnT[:, h0:h0 + 8, li * 128:(li + 1) * 128], tpb[:, :].rearrange("p (c t) -> p c t", t=128),
                               gf_col[:, h0:h0 + 8].unsqueeze(2).to_broadcast([128, 8, 128]), ALU.mult),
                     reads=[tpt, t_const, t_const2], writes=[t_xnT[li]])

        def norm_transpose_h(li):
            hn_part1(li)
            hn_part2(li)

        def MSET_ACT(ap):
            return _est(lambda e: e.activation(out=ap, in_=ap, func=AF.Copy, scale=0.0), 0.25)


        try:
            if stop_after == 0:
                raise _Stop()
            for gi, (kv_tiles, ctiles) in enumerate(GROUPS):
                if gi > 0:
                    P.barrier(extra=[d.last for d in (ds_yb + ds_xr + ds_xt) if d.last is not None])
                nxt = GROUPS[gi + 1][0] if gi + 1 < len(GROUPS) else None
                run_group(gi, kv_tiles, ctiles, next_kv=nxt, s1_done=(2 if (gi > 0 and not debug) else 0))
                if stop_after == 7:
                    raise _Stop()
        except _Stop:
            pass

        final_deps = [d.last for d in (ds_out, ds_gout, ds_yb[0], ds_yb[1], ds_dbg, ds_dbg2) if d.last is not None]
        P.final_wait("sp", (lambda e: e.nop()), final_deps)
        P.finalize()
        LAST_PROG.clear()
        LAST_PROG.append(P)

        with nc.Block() as block:
            @block.tensor
            def _(e):
                P.replay("pe", e, esem)

            @block.scalar
            def _(e):
                P.replay("act", e, esem)

            @block.vector
            def _(e):
                P.replay("dve", e, esem)

            @block.gpsimd
            def _(e):
                P.replay("pool", e, esem)

            @block.sync
            def _(e):
                P.replay("sp", e, esem)
    return nc


def _rope_tables():
    half = 32
    inv = 10000.0 ** (-np.arange(half, dtype=np.float64) / float(half))
    out = np.zeros((8, 128, NT, 128), np.float32)
    for c in range(8):
        hf = c % 2
        for t in range(NT):
            if t == 9:
                pos = (16384 + (np.arange(128) % 8)).astype(np.float64)
            else:
                pos = (hf * 1024 + (t - 1) * 128 + np.arange(128)).astype(np.float64)
            ang = pos[:, None] * inv[None, :]
            cos = np.cos(ang).astype(np.float32)
            sin = np.sin(ang).astype(np.float32)
            out[c, :, t, 0:32] = cos
            out[c, :, t, 32:64] = cos
            out[c, :, t, 64:96] = -sin
            out[c, :, t, 96:128] = sin
    return out


def _masks(first_valid):
    j = np.arange(128)[:, None]
    i = np.arange(128)[None, :]
    m_cur = np.where(j <= i, 0.0, NEG).astype(np.float32)
    m_prev = np.where(j > i, 0.0, NEG).astype(np.float32)
    m_first = m_prev if first_valid else np.full((128, 128), NEG, np.float32)
    bj, jj = j // 8, j % 8
    bi, ii = i // 8, i % 8
    m_news = np.where((bj == bi) & (jj <= ii), 0.0, NEG).astype(np.float32)
    icol = (np.arange(128) % 8)[None, :]
    m_cache = np.where(j > icol, 0.0, NEG).astype(np.float32)
    ident = np.eye(128, dtype=np.float32)
    return np.concatenate([np.tile(m_cur, (1, 4)), np.tile(m_prev, (1, 4)), np.tile(m_first, (1, 4)),
                           np.tile(m_news, (1, 4)), m_cache, ident], axis=1)


_NC_CACHE = {}


def kernel(x_prompt, x_sample, cache_k_win, cache_v_win, attn_norm, w_in, q_norm, k_norm,
           sinks, sg_norm, sg_w, sg_b, attn_out_norm, sg_out_norm, w_o, ffn_norm,
           w_gate, w_up, w_down):
    f = lambda a: np.ascontiguousarray(np.asarray(a, dtype=np.float32))
    x_prompt, x_sample = f(x_prompt), f(x_sample)
    ck_all = f(cache_k_win)[0].reshape(128, 128, 256)
    cv_all = f(cache_v_win)[0].reshape(128, 128, 256)
    w_in_, w_o_, w_g_, w_u_, w_d_ = f(w_in)[0], f(w_o)[0], f(w_gate)[0], f(w_up)[0], f(w_down)[0]
    colf = lambda v, n: f(v)[0].reshape(n, 128).T
    sgb = f(sg_b)[0]
    bcol = sgb.T
    bcol_s = np.tile(sgb[:, :8].T, (16, 1))
    cols = np.ascontiguousarray(np.concatenate(
        [colf(attn_norm, 16), colf(ffn_norm, 16), colf(attn_out_norm, 8), colf(sg_out_norm, 8), bcol, bcol_s], axis=1))
    bcv = np.ascontiguousarray(np.concatenate([f(q_norm)[0], f(k_norm)[0], f(sinks)[0], f(sg_norm)[0]]))
    sgw = f(sg_w)[0]
    wT = np.ascontiguousarray(sgw.transpose(2, 0, 1)).reshape(128, 1024)
    wT_s = np.zeros((128, 8, 128), np.float32)
    blk = sgw[:, :8, :8].transpose(2, 0, 1)
    for b in range(16):
        wT_s[b * 8:(b + 1) * 8, :, b * 8:(b + 1) * 8] = blk
    wt = np.ascontiguousarray(np.concatenate([wT, wT_s.reshape(128, 1024)], axis=1))
    jj = np.arange(128)[:, None]
    ii = np.arange(128)[None, :]
    cmask = (jj <= ii).astype(np.float32)
    cmask_s = ((jj // 8 == ii // 8) & (jj % 8 <= ii % 8)).astype(np.float32)
    cm = np.ascontiguousarray(np.concatenate([cmask, cmask_s], axis=1))
    rope = _rope_tables()

    in_maps = []
    for c in range(8):
        b, hf = c // 2, c % 2
        xs = np.zeros((NT, 128, D), np.float32)
        if hf == 1:
            xs[0] = x_prompt[b, 896:1024]
        xs[1:9] = x_prompt[b, hf * 1024:(hf + 1) * 1024].reshape(8, 128, D)
        xs[9] = x_sample[16 * c:16 * c + 16].reshape(128, D)
        in_maps.append({
            "xs": xs, "ck": np.ascontiguousarray(ck_all[16 * c:16 * c + 16]), "cv": np.ascontiguousarray(cv_all[16 * c:16 * c + 16]),
            "w_in": w_in_, "w_o": w_o_, "w_gate": w_g_, "w_up": w_u_, "w_down": w_d_,
            "cols": cols, "bcv": bcv, "cs": np.ascontiguousarray(rope[c]), "mk": np.ascontiguousarray(_masks(hf == 1)),
            "wt": wt, "cm": cm,
        })
    if "nc" not in _NC_CACHE:
        _NC_CACHE["nc"] = build_program()
    res = run_bass_kernel_spmd(_NC_CACHE["nc"], in_maps, core_ids=list(range(8)))
    R = res.results
    y_prompt = np.zeros((4, 2048, D), np.float32)
    y_sample = np.zeros((128, 8, D), np.float32)
    kwp = np.zeros((1, 4, 128, 4, 64), np.float32)
    vwp = np.zeros((1, 4, 128, 4, 64), np.float32)
    kws = np.zeros((1, 128, 128, 4, 64), np.float32)
    vws = np.zeros((1, 128, 128, 4, 64), np.float32)
    sgv = np.zeros((1, 128, 8, 1024), np.float32)
    for c in range(8):
        b, hf = c // 2, c % 2
        y = np.asarray(R[c]["y"])
        y_prompt[b, hf * 1024:(hf + 1) * 1024] = y[0:8].reshape(1024, D)
        y_sample[16 * c:16 * c + 16] = y[8].reshape(16, 8, D)
        if hf == 1:
            kwp[0, b] = np.asarray(R[c]["kwp"]).reshape(128, 4, 64)
            vwp[0, b] = np.asarray(R[c]["vwp"]).reshape(128, 4, 64)
        kws[0, 16 * c:16 * c + 16, 0:120] = np.asarray(R[c]["kws"]).reshape(16, 120, 4, 64)
        vws[0, 16 * c:16 * c + 16, 0:120] = np.asarray(R[c]["vws"]).reshape(16, 120, 4, 64)
        kws[0, 16 * c:16 * c + 16, 120:128] = np.asarray(R[c]["knew"]).reshape(16, 8, 4, 64)
        vws[0, 16 * c:16 * c + 16, 120:128] = np.asarray(R[c]["vnew"]).reshape(16, 8, 4, 64)
        sgv[0, 16 * c:16 * c + 16] = np.asarray(R[c]["sgv"]).reshape(16, 8, 1024)
    return (y_prompt, y_sample, kwp, vwp, kws, vws, sgv)
```

```python
import contextlib
import numpy as np
import concourse.bass as bass
import concourse.mybir as mybir
from concourse.bass_utils import run_bass_kernel_spmd

F32 = mybir.dt.float32
BF16 = mybir.dt.bfloat16
AF = mybir.ActivationFunctionType
ALU = mybir.AluOpType
AX = mybir.AxisListType

D = 2048
DC = 16
NT = 10
DFF = 5632
EPS = 1e-6
NEG = -240000.0
GROUPS = [([0, 1, 2, 3, 4], [1, 2, 3, 4]), ([5, 6, 7, 8, 9], [5, 6, 7, 8, 9])]
NG = 5
FFN_PARTS = [(0, 16), (16, 32), (32, 44)]


class Tr:
    def __init__(self):
        self.w = None
        self.r = []


class DSem:
    def __init__(self, sem):
        self.sem = sem
        self.n = 0
        self.last = None

    def next(self):
        self.n += 16
        return ("dma", self.sem, self.n)


class Node:
    __slots__ = ("eng", "fns", "deps", "sig", "est", "tbl", "idx", "cnt", "fin", "dma", "tail")

    def __init__(self, eng, fns, deps, sig, est, tbl, idx):
        self.eng, self.fns, self.deps, self.sig, self.est, self.tbl, self.idx = eng, fns, deps, sig, est, tbl, idx
        self.cnt = None
        self.fin = None
        self.dma = None


SCHED = True
SCHED_WINDOW = 32
SCHED_SLACK = 1.0


class Prog:
    ENG = ("pe", "act", "dve", "pool", "sp")

    def __init__(self):
        self.nodes = []
        self.bar = {e: [] for e in self.ENG}
        self.last = {e: None for e in self.ENG}
        self.ops = {e: [] for e in self.ENG}

    def _deps(self, reads, writes, extra):
        deps = []
        for t in reads:
            if t.w is not None:
                deps.append(t.w)
        for t in writes:
            if t.w is not None:
                deps.append(t.w)
            deps.extend(t.r)
        deps.extend([d for d in extra if d is not None])
        return deps

    def _finish(self, tok, reads, writes):
        for t in writes:
            t.w = tok
            t.r = []
        for t in reads:
            if t not in writes:
                t.r.append(tok)

    def _node(self, eng, fns, deps, sig):
        est = sum(getattr(f, "est", 0.1) for f in fns)
        tbl = getattr(fns[0], "tbl", None)
        n = Node(eng, fns, deps, sig, est, tbl, len(self.nodes))
        self.nodes.append(n)
        self.ops[eng].extend(fns)
        return n

    def op(self, eng, fn, reads=(), writes=(), extra=()):
        return self.group(eng, [fn], reads, writes, extra)

    def group(self, eng, fns, reads=(), writes=(), extra=()):
        deps = self._deps(reads, writes, extra) + self.bar[eng]
        self.bar[eng] = []
        n = self._node(eng, list(fns), deps, True)
        tok = ("n", n)
        self._finish(tok, reads, writes)
        self.last[eng] = tok
        return tok

    def dma(self, eng, out, in_, ds, reads=(), writes=(), extra=(), nbytes=65536):
        deps = self._deps(reads, writes, extra) + self.bar[eng]
        self.bar[eng] = []
        tok0 = ds.next()
        sem = ds.sem

        def fn(e, out=out, in_=in_, sem=sem):
            return e.dma_start(out=out, in_=in_).then_inc(sem, 16)

        fn.est = 1.4 if eng == "pool" else 0.1
        n = self._node(eng, [fn], deps, False)
        n.dma = nbytes
        tok = ("dma", tok0[1], tok0[2], n)
        ds.last = tok
        self._finish(tok, reads, writes)
        return tok

    def barrier(self, engines=("pe", "act", "dve", "sp"), extra=()):
        toks = [self.last[e] for e in ("pe", "act", "dve") if self.last[e] is not None] + list(extra)
        for e in engines:
            self.bar[e] = list(toks)

    def final_wait(self, eng, fn, deps):
        n = self._node(eng, [fn], list(deps), False)
        return n

    def schedule(self):
        per = {e: [n for n in self.nodes if n.eng == e] for e in self.ENG}
        if not SCHED:
            return per
        for n in self.nodes:
            n.tail = 0.0
        for n in reversed(self.nodes):
            w = n.est + n.tail + (n.dma / 300e3 + 2.0 if n.dma is not None else 0.0)
            for d in n.deps:
                nd = d[1] if d[0] == "n" else (d[3] if len(d) > 3 else None)
                if nd is not None and w > nd.tail:
                    nd.tail = w
        pos = {e: 0 for e in self.ENG}
        pend = {e: list(per[e]) for e in self.ENG}
        free = {e: 0.0 for e in self.ENG}
        cur_tbl = [None]
        dma_free = [0.0]
        out = {e: [] for e in self.ENG}
        remaining = len(self.nodes)

        def dep_fin(d):
            nd = d[1] if d[0] == "n" else (d[3] if len(d) > 3 else None)
            if nd is None:
                return 0.0
            return nd.fin

        while remaining:
            best = None
            for e in self.ENG:
                lst = pend[e]
                if not lst:
                    continue
                W = SCHED_WINDOW if e in ("pe", "act", "dve") else 1
                seen = 0
                cands = []
                for n in lst:
                    if seen >= W:
                        break
                    seen += 1
                    ok = True
                    t = free[e]
                    for d in n.deps:
                        f = dep_fin(d)
                        if f is None:
                            ok = False
                            break
                        lat = 0.15 if (d[0] == "n" and d[1].eng != e) else 0.05
                        if f + lat > t:
                            t = f + lat
                    if not ok:
                        continue
                    if e == "act" and n.tbl is not None and n.tbl != cur_tbl[0]:
                        t += 1.3
                    cands.append((t, n))
                if not cands:
                    continue
                tmin = min(c[0] for c in cands)
                t, n = max((c for c in cands if c[0] <= tmin + SCHED_SLACK), key=lambda c: (c[1].tail, -c[1].idx))
                key = (t, n.idx)
                if best is None or key < best[0]:
                    best = (key, e, n)
            key, e, n = best
            t = key[0]
            if e == "act" and n.tbl is not None:
                cur_tbl[0] = n.tbl
            end = t + n.est + 0.08
            free[e] = end
            if n.dma is not None:
                st = max(end, dma_free[0])
                dma_free[0] = st + n.dma / 300e3
                n.fin = dma_free[0] + 2.0
            else:
                n.fin = end
            pend[e].remove(n)
            out[e].append(n)
            remaining -= 1
        self.makespan = max(free.values())
        return out

    def finalize(self):
        order = self.schedule()
        for e in self.ENG:
            c = 0
            for n in order[e]:
                if n.sig:
                    c += 1
                    n.cnt = c
        self.order = order

    def replay(self, name, e, sems):
        waited = {}
        for n in self.order[name]:
            for d in n.deps:
                if d[0] == "dma":
                    key = ("dma", id(d[1]))
                    if waited.get(key, 0) >= d[2]:
                        continue
                    e.wait_ge(d[1], d[2])
                    waited[key] = d[2]
                else:
                    src = d[1]
                    if waited.get(src.eng, 0) >= src.cnt:
                        continue
                    e.wait_ge(sems[src.eng], src.cnt)
                    waited[src.eng] = src.cnt
            ins = None
            for fn in n.fns:
                ins = fn(e)
            if n.sig:
                ins.then_inc(sems[name], 1)


def _free(ap):
    k = 1
    for d in ap.shape[1:]:
        k *= int(d)
    return k


def _est(fn, v, tbl=None):
    fn.est = v
    fn.tbl = tbl
    return fn


def MM(out, lhsT, rhs, start, stop):
    return _est(lambda e: e.matmul(out, lhsT=lhsT, rhs=rhs, start=start, stop=stop, skip_group_check=True),
                max(_free(rhs), 64) / 2400.0 + 0.015)


def TRP(out, in_, ident):
    return _est(lambda e: e.transpose(out=out, in_=in_, identity=ident), 0.075)


def ACTF(out, in_, func, scale=None, bias=None, accum=None):
    kw = {}
    if scale is not None:
        kw["scale"] = scale
    if bias is not None:
        kw["bias"] = bias
    if accum is not None:
        kw["accum_out"] = accum
    tbl = {AF.Exp: "exp", AF.Ln: "exp", AF.Gelu_apprx_tanh: "gelu", AF.Silu: "silu"}.get(func)
    return _est(lambda e: e.activation(out=out, in_=in_, func=func, **kw), 0.2 + _free(out) / 1050.0, tbl)


def TT(out, in0, in1, op):
    return _est(lambda e: e.tensor_tensor(out=out, in0=in0, in1=in1, op=op), 0.12 + _free(out) / 930.0)


def TS(out, in0, s1, op0, s2=None, op1=None):
    if op1 is None:
        return lambda e: e.tensor_scalar(out=out, in0=in0, scalar1=s1, scalar2=None, op0=op0)
    return lambda e: e.tensor_scalar(out=out, in0=in0, scalar1=s1, scalar2=s2, op0=op0, op1=op1)


def STT(out, in0, scalar, in1, op0, op1):
    return _est(lambda e: e.scalar_tensor_tensor(out=out, in0=in0, scalar=scalar, in1=in1, op0=op0, op1=op1), 0.12 + _free(out) / 930.0)


def RSUM(out, in_):
    return _est(lambda e: e.reduce_sum(out=out, in_=in_, axis=AX.X), 0.12 + _free(in_) / 930.0)


def RCP(out, in_):
    return _est(lambda e: e.reciprocal(out=out, in_=in_), 0.2)


def CP(out, in_):
    return _est(lambda e: e.tensor_copy(out=out, in_=in_), 0.12 + _free(out) / 1800.0)


def MSET(ap, v):
    return _est(lambda e: e.memset(ap, v), 0.05 + _free(ap) / 4000.0)


class _Stop(Exception):
    pass


MARKS = []
LAST_PROG = []


def build_program(debug=False, stop_after=None):
    nc = bass.Bass("TRN2", target_bir_lowering=False)
    dt_in = lambda name, shape: nc.dram_tensor(name, shape, F32, kind="ExternalInput").ap()
    dt_out = lambda name, shape: nc.dram_tensor(name, shape, F32, kind="ExternalOutput").ap()

    xs = dt_in("xs", [NT, 128, D])
    ck = dt_in("ck", [16, 128, 256])
    cv = dt_in("cv", [16, 128, 256])
    w_in = dt_in("w_in", [D, 3584])
    w_o = dt_in("w_o", [D, D])
    w_gate = dt_in("w_gate", [D, DFF])
    w_up = dt_in("w_up", [D, DFF])
    w_down = dt_in("w_down", [DFF, D])
    cols_d = dt_in("cols", [128, 64])
    bcv_d = dt_in("bcv", [1168])
    cs_d = dt_in("cs", [128, NT, 128])
    mk_d = dt_in("mk", [128, 2304])
    wt_d = dt_in("wt", [128, 2048])
    cm_d = dt_in("cm", [128, 256])

    y_o = dt_out("y", [9, 128, D])
    kwp_o = dt_out("kwp", [128, 256])
    vwp_o = dt_out("vwp", [128, 256])
    kws_o = dt_out("kws", [16, 120, 256])
    vws_o = dt_out("vws", [16, 120, 256])
    knew_o = dt_out("knew", [128, 256])
    vnew_o = dt_out("vnew", [128, 256])
    sgv_o = dt_out("sgv", [128, 1024])
    if debug:
        dbg_mix = nc.dram_tensor("dbg_mix", [128, 16, NG * 128], BF16, kind="ExternalOutput").ap()
        dbg_h = dt_out("dbg_h", [128, NG, D])

    P = Prog()
    es = contextlib.ExitStack()
    with es:
        def sb(name, shape, dt):
            return es.enter_context(nc.sbuf_tensor("sb_" + name, shape, dt))

        def ps(name, shape, dt):
            return es.enter_context(nc.psum_tensor("ps_" + name, shape, dt))

        def sem(name):
            return es.enter_context(nc.semaphore(name))

        NSLOT = 4
        ring = [sb(f"ring{i}", [128, 16, 512], BF16) for i in range(NSLOT)]
        xnT = sb("xnT", [128, 16, NG * 128], BF16)
        mixT = sb("mixT", [128, 16, NG * 128], BF16)
        Hreg = sb("Hreg", [128, NG * D], F32)
        hbuf = Hreg[:, :].rearrange("p (t d) -> p t d", d=D)
        Hb = Hreg[:, :].bitcast(BF16)
        kT2 = sb("kT2", [64, 4, 6, 128], BF16)
        Vaug = sb("Vaug", [128, 6, 4, 72], BF16)
        g_n = Hb[:, 0:NG * 1024].rearrange("p (t e) -> p t e", e=1024)
        X = sb("X", [128, 4096], F32)
        xsb = sb("xsb", [128, D], BF16)
        o0 = NG * 1024
        qb2 = [Hb[:, o0 + 1024 * i:o0 + 1024 * (i + 1)] for i in range(2)]
        qT2 = [Hb[0:64, o0 + 2048 + 2048 * i:o0 + 2048 + 2048 * (i + 1)].rearrange("p (h t) -> p h t", t=128) for i in range(2)]
        PT2 = [Hb[:, o0 + 6144 + 1024 * i:o0 + 6144 + 1024 * (i + 1)].rearrange("p (b e) -> p b e", e=512) for i in range(2)]
        kbd = Hb[:, o0 + 8192:o0 + 8448]
        assert (o0 + 8448) // 2 <= 7168
        S = sb("S", [128, 3328], F32)
        Sb = S[:, :].bitcast(BF16)
        PTz = Sb[:, 0:2048].rearrange("p (h t) -> p h t", t=128)
        ckd = Sb[:, 2048:2560].rearrange("p (s e) -> p s e", e=256)
        cva = Sb[:, 2560:3136].rearrange("p (s h e) -> p s h e", h=4, e=72)
        kcT2 = Sb[0:64, 3136:3648].rearrange("p (h t) -> p h t", t=128)
        kcT2b = [kcT2, Sb[0:64, 5696:6208].rearrange("p (h t) -> p h t", t=128)]
        ckf = S[:, 1824:2336].rearrange("p (s e) -> p s e", e=256)
        cvf = S[:, 2336:2848].rearrange("p (s e) -> p s e", e=256)
        cols = sb("cols", [128, 64], F32)
        bc = sb("bc", [128, 144], F32)
        cs = sb("cs", [128, NT, 128], F32)
        mk = sb("mk", [128, 2304], BF16)
        wTm = sb("wTm", [128, 2, 8, 128], BF16)
        st8 = sb("st8", [128, 64], F32)
        rall = sb("rall", [128, 64], F32)
        sinkexp = sb("sinkexp", [128, 16], F32)
        kfo = sb("kfo", [128, 2, 256], F32)
        vfo = sb("vfo", [128, 2, 256], F32)
        kfw = sb("kfw", [128, 256], F32)

        mm = [ps(f"mm{i}", [128, 512], F32) for i in range(2)]
        tp = ps("tp", [128, 1024], BF16)
        Bk = [ps(f"bk{i}", [128, 512], F32) for i in range(5)]
        stp = [Bk[0], Bk[1]]

        esem = {e: sem(f"s_{e}") for e in Prog.ENG}
        ds_ring = [DSem(sem(f"d_ring{i}")) for i in range(NSLOT)]
        ds_xt = [DSem(sem(f"d_xt{i}")) for i in range(2)]
        ds_xr = [DSem(sem(f"d_xr{i}")) for i in range(2)]
        ds_yb = [DSem(sem(f"d_yb{i}")) for i in range(2)]
        ds_ck = DSem(sem("d_ck"))
        ds_cv = DSem(sem("d_cv"))
        ds_setup = DSem(sem("d_setup"))
        ds_setup2 = DSem(sem("d_setup2"))
        t_const2 = Tr()
        ds_out = DSem(sem("d_out"))
        ds_gout = DSem(sem("d_gout"))
        ds_gg = DSem(sem("d_gg"))
        t_gg = Tr()
        ds_dbg = DSem(sem("d_dbg"))
        ds_dbg2 = DSem(sem("d_dbg2"))

        xt = [X[:, 0:2048], X[:, 2048:4096]]
        T_sq = X[:, 0:512]
        T_qh = X[:, 512:1024]
        T_m1 = X[:, 1024:1536]
        T_m2 = X[:, 1536:2048]
        T_a = X[:, 2048:3072]
        T_g = X[:, 3072:4096]
        xr = [S[:, 0:512], S[:, 512:1024]]
        yb = [S[:, 1024:1536], S[:, 1536:2048]]
        sgt = [S[:, 2048 + 320 * i:2048 + 320 * (i + 1)] for i in range(4)]
        tpF = tp[:, :].bitcast(F32)
        ga_col = cols[:, 0:16]
        gf_col = cols[:, 16:32]
        gao_col = cols[:, 32:40]
        gso_col = cols[:, 40:48]
        bcol = [cols[:, 48:56], cols[:, 56:64]]
        gq_bc = bc[:, 0:64]
        gk_bc = bc[:, 64:128]
        sinks_bc = bc[:, 128:144]
        gg_bc = Hreg[:, 8192:9216]
        m_cur = mk[:, 0:512]
        m_prev = mk[:, 512:1024]
        m_first = mk[:, 1024:1536]
        m_news = mk[:, 1536:2048]
        m_cache = mk[:, 2048:2176]
        ident = mk[:, 2176:2304]

        t_stq, t_rq, t_stk, t_rk = Tr(), Tr(), Tr(), Tr()
        t_ksq, t_kqh, t_km1 = Tr(), Tr(), Tr()
        t_ring = [Tr() for _ in range(NSLOT)]
        t_xt = [Tr(), Tr()]
        t_xsb = Tr()
        t_xsb2 = Tr()
        t_tp = Tr()
        t_mm = [Tr(), Tr()]
        t_B = [Tr() for _ in range(5)]
        t_st = [t_B[0], t_B[1]]
        t_xnT = [Tr() for _ in range(NG)]
        t_mixA = [Tr() for _ in range(NG)]
        t_mixS = [Tr() for _ in range(NG)]
        t_act = Tr()
        t_h = [[Tr() for _ in range(4)] for _ in range(NG)]
        t_kT2 = [Tr() for _ in range(6)]
        t_V = [Tr() for _ in range(6)]
        t_gn = [Tr() for _ in range(NG)]
        t_sq, t_qh, t_m1, t_m2, t_a, t_g = Tr(), Tr(), Tr(), Tr(), Tr(), Tr()
        t_kbd, t_PTz = Tr(), Tr()
        t_qb2, t_qT2 = [Tr(), Tr()], [Tr(), Tr()]
        t_PT2 = [[Tr(), Tr()], [Tr(), Tr()]]
        t_ckf, t_cvf, t_ckd, t_cva, t_kcT2 = Tr(), Tr(), Tr(), Tr(), Tr()
        t_kcT2b = [t_kcT2, Tr()]
        t_xr, t_yb, t_sgt = [Tr(), Tr()], [Tr(), Tr()], [Tr() for _ in range(4)]
        t_ovb = [Tr(), Tr(), Tr()]
        t_const = Tr()
        t_wTm = Tr()
        t_st8 = Tr()
        t_st8b = Tr()
        t_rallb = Tr()
        t_rall = Tr()
        t_kfw = Tr()
        t_kfo, t_vfo = [Tr(), Tr()], [Tr(), Tr()]

        slab_list = []
        w_in_v = w_in.rearrange("(c p) e -> p c e", p=128)
        w_o_v = w_o.rearrange("(c p) e -> p c e", p=128)
        w_g_v = w_gate.rearrange("(c p) e -> p c e", p=128)
        w_u_v = w_up.rearrange("(c p) e -> p c e", p=128)
        w_d_v = w_down.rearrange("(c p) e -> p c e", p=128)

        def full_slab(src_v, c0):
            return [(lambda r, h=h: r[:, 8 * h:8 * h + 8, :], src_v[:, 8 * h:8 * h + 8, c0:c0 + 512]) for h in range(2)]

        for _g in range(len(GROUPS)):
            for c0 in (1024, 0, 512, 2560, 3072, 1536, 2048):
                slab_list.append(full_slab(w_in_v, c0))
            for s in range(4):
                slab_list.append(full_slab(w_o_v, s * 512))
            for (pc0, pc1) in FFN_PARTS:
                for j in range(pc0 // 4, pc1 // 4):
                    slab_list.append(full_slab(w_g_v, 512 * j))
                    slab_list.append(full_slab(w_u_v, 512 * j))
                nch = pc1 - pc0
                for s in range(4):
                    hh = nch // 2
                    slab_list.append([
                        (lambda r, hh=hh: r[:, 0:hh, :], w_d_v[:, pc0:pc0 + hh, s * 512:(s + 1) * 512]),
                        (lambda r, hh=hh, nch=nch: r[:, hh:nch, :], w_d_v[:, pc0 + hh:pc1, s * 512:(s + 1) * 512]),
                    ])
        slab_state = {"loaded": 0, "used": 0}
        MARKS.clear()

        def load_next_slab():
            n = slab_state["loaded"]
            if n >= len(slab_list):
                return
            slot = n % NSLOT
            for (dst_fn, src) in slab_list[n]:
                P.dma("pool", dst_fn(ring[slot]), src, ds_ring[slot], writes=[t_ring[slot]], nbytes=2 * 1024 * 1024)
            slab_state["loaded"] = n + 1

        def next_slab():
            n = slab_state["used"]
            slab_state["used"] = n + 1
            assert n < slab_state["loaded"]
            return ring[n % NSLOT], t_ring[n % NSLOT]

        def release_slab():
            load_next_slab()

        P.dma("sp", cols[:, :], cols_d, ds_setup, writes=[t_const])
        P.dma("sp", bc[:, :], bcv_d[0:144].partition_broadcast(128), ds_setup, writes=[t_const])
        P.dma("sp", cs[:, :, :], cs_d, ds_setup, writes=[t_const])
        P.dma("sp", X[:, 0:2048], wt_d, ds_setup, writes=[t_const])
        P.dma("sp", X[:, 2048:2304], cm_d, ds_setup, writes=[t_const])
        P.dma("pool", mk[:, :], mk_d, ds_setup2, writes=[t_const2])
        P.dma("sp", kws_o, ck[:, 8:128, :], ds_out)
        P.dma("sp", vws_o, cv[:, 8:128, :], ds_out)
        for _ in range(NSLOT):
            load_next_slab()

        P.op("dve", MSET(Vaug[:, :, :, :], 1.0), writes=t_V)
        for k in range(2):
            P.op("dve", TT(wTm[:, k, :, :], X[:, k * 1024:(k + 1) * 1024].rearrange("p (h i) -> p h i", i=128),
                           X[:, 2048 + 128 * k:2048 + 128 * (k + 1)].unsqueeze(1).to_broadcast([128, 8, 128]), ALU.mult),
                 reads=[t_const], writes=[t_wTm] + ([t_xt[0], t_xt[1]] if k == 1 else []))
        P.op("act", ACTF(sinkexp[:, :], sinks_bc, AF.Exp), reads=[t_const, t_const2])

        def rstd_from_ssq(ssq_ap, out_ap, n, width, st_tr=None, r_tr=None):
            st_tr = st_tr or t_st8
            r_tr = r_tr or t_rall
            P.op("act", ACTF(ssq_ap, ssq_ap, AF.Ln, scale=1.0 / n, bias=EPS), reads=[st_tr], writes=[st_tr])
            return P.op("act", ACTF(out_ap, ssq_ap, AF.Exp, scale=-0.5), reads=[st_tr], writes=[r_tr])

        tp_pool = [(tp, t_tp), (Bk[2][:, :].bitcast(BF16), t_B[2]), (Bk[3][:, :].bitcast(BF16), t_B[3]), (Bk[4][:, :].bitcast(BF16), t_B[4])]
        tp_rr = [0]

        def next_tp(rotate):
            if not rotate:
                return tp, t_tp
            tp_rr[0] = (tp_rr[0] + 1) % len(tp_pool)
            return tp_pool[tp_rr[0]]

        def norm_transpose(src_ap, src_tr, nchunks, dstT, dst_col0, dst_tr, gcol, c_off=0, rotate=False):
            W = nchunks * 128
            ssq = st8[:, 0:1]
            P.op("dve", MSET(ssq, 0.0), writes=[t_st8])
            P.op("act", ACTF(xsb[:, 0:W], src_ap, AF.Square, accum=ssq), reads=[src_tr], writes=[t_xsb, t_st8])
            rstd_from_ssq(ssq, rall[:, 0:1], float(W), 1)
            P.op("act", ACTF(xsb[:, 0:W], src_ap, AF.Copy, scale=rall[:, 0:1]), reads=[src_tr, t_rall], writes=[t_xsb])
            for h0 in range(0, nchunks, 8):
                tpb, tpt = next_tp(rotate)
                fns = [TRP(tpb[:, (c - h0) * 128:(c - h0 + 1) * 128], xsb[:, c * 128:(c + 1) * 128], ident)
                       for c in range(h0, h0 + 8)]
                P.group("pe", fns, reads=[t_xsb, t_const, t_const2], writes=[tpt])
                P.op("dve", TT(dstT[:, c_off + h0:c_off + h0 + 8, dst_col0:dst_col0 + 128],
                               tpb[:, :].rearrange("p (c t) -> p c t", t=128),
                               gcol[:, h0:h0 + 8].unsqueeze(2).to_broadcast([128, 8, 128]), ALU.mult),
                     reads=[tpt, t_const, t_const2], writes=[dst_tr])

        def dense_B(srcT, col0, src_tr, slab, slab_tr, nch, bank, bank_tr, ch0=0, ncols=512, sc0=0):
            fns = [MM(bank[:, 0:ncols], srcT[:, ch0 + c, col0:col0 + 128], slab[:, c, sc0:sc0 + ncols], c == 0, c == nch - 1)
                   for c in range(nch)]
            return P.group("pe", fns, reads=[src_tr, slab_tr], writes=[bank_tr])

        def qk_norm_rope(src_bank, src_tr, nh, gbc, t, out_ap, out_tr, TS_):
            W = nh * 64
            v3 = lambda ap: ap.rearrange("p (h d) -> p h d", d=64)
            sq, qh, m1 = TS_["sq"][:, 0:W], TS_["qh"][:, 0:W], TS_["m1"][:, 0:W]
            m2 = sq
            tsq, tqh, tm1, tst, tr_ = TS_["tsq"], TS_["tqh"], TS_["tm1"], TS_["tst"], TS_["tr"]
            stv, rv = TS_["st"][:, 0:nh], TS_["r"][:, 0:nh]
            P.op("act", ACTF(sq, src_bank[:, 0:W], AF.Square), reads=[src_tr], writes=[tsq])
            P.op("dve", RSUM(stv, v3(sq)), reads=[tsq], writes=[tst])
            rstd_from_ssq(stv, rv, 64.0, nh, tst, tr_)
            P.op("dve", TT(v3(qh), v3(src_bank[:, 0:W]), rv.unsqueeze(2).to_broadcast([128, nh, 64]), ALU.mult),
                 reads=[src_tr, tr_], writes=[tqh])
            P.op("dve", TT(v3(qh), v3(qh), gbc.unsqueeze(1).to_broadcast([128, nh, 64]), ALU.mult),
                 reads=[t_const, t_const2], writes=[tqh])
            csA = cs[:, t, 0:64].unsqueeze(1).to_broadcast([128, nh, 64])
            P.op("dve", TT(v3(m1), v3(qh), csA, ALU.mult), reads=[tqh, t_const, t_const2], writes=[tm1])
            P.op("dve", TT(v3(m2)[:, :, 0:32], v3(qh)[:, :, 32:64],
                           cs[:, t, 64:96].unsqueeze(1).to_broadcast([128, nh, 32]), ALU.mult),
                 reads=[tqh, t_const, t_const2], writes=[tsq])
            P.op("dve", TT(v3(m2)[:, :, 32:64], v3(qh)[:, :, 0:32],
                           cs[:, t, 96:128].unsqueeze(1).to_broadcast([128, nh, 32]), ALU.mult),
                 reads=[tqh], writes=[tsq])
            P.op("dve", TT(out_ap, m1, m2, ALU.add), reads=[tm1, tsq], writes=[out_tr])

        def head_slot(h):
            return Bk[2 + h // 7], t_B[2 + h // 7], (h % 7) * 72

        def S1_tile(t, i):
            b = i % 2
            P.dma("sp", xt[b], xs[t], ds_xt[b], writes=[t_xt[b]] + ([t_xsb2] if b == 0 else []), nbytes=1024 * 1024)
            norm_transpose(xt[b], t_xt[b], 16, xnT, i * 128, t_xnT[i], ga_col, rotate=True)

        def run_group(gi, kv_tiles, ctiles, next_kv=None, s1_done=0):
            nct = len(ctiles)
            xcol = {t: (i * 128) for i, t in enumerate(kv_tiles)}
            if kv_tiles[0] != ctiles[0]:
                xblk = {t: i for i, t in enumerate(kv_tiles)}
            else:
                xblk = {t: i for i, t in enumerate(kv_tiles)}
            li_of = {t: i for i, t in enumerate(ctiles)}

            QS = dict(sq=T_sq, qh=T_qh, m1=T_m1, tsq=t_sq, tqh=t_qh, tm1=t_m1,
                      st=st8[:, 32:40], r=rall[:, 32:40], tst=t_stq, tr=t_rq)
            KS = dict(sq=Hreg[:, 7168:7424], qh=Hreg[:, 7424:7680], m1=Hreg[:, 7680:7936], tsq=t_ksq, tqh=t_kqh, tm1=t_km1,
                      st=st8[:, 48:52], r=rall[:, 48:52], tst=t_stk, tr=t_rk)

            def S1(i):
                if i >= s1_done:
                    S1_tile(kv_tiles[i], i)

            slab, slab_tr = next_slab()

            def S2mm(i):
                t = kv_tiles[i]
                dense_B(xnT, xcol[t], t_xnT[xblk[t]], slab, slab_tr, 16, mm[i % 2], t_mm[i % 2])

            def S2post(i):
                t = kv_tiles[i]
                bk = i % 2
                if t == 8:
                    kf, kf_tr, vf, vf_tr = kfo[:, 0, :], t_kfo[0], vfo[:, 0, :], t_vfo[0]
                elif t == 9:
                    kf, kf_tr, vf, vf_tr = kfo[:, 1, :], t_kfo[1], vfo[:, 1, :], t_vfo[1]
                else:
                    kf, kf_tr, vf, vf_tr = kfw[:, :], t_kfw, None, None
                qk_norm_rope(mm[bk], t_mm[bk], 4, gk_bc, t, kf, kf_tr, KS)
                P.op("act", ACTF(kbd, kf, AF.Copy), reads=[kf_tr], writes=[t_kbd])
                P.op("act", ACTF(Vaug[:, t % 6, :, 0:64], mm[bk][:, 256:512].rearrange("p (h d) -> p h d", d=64), AF.Copy),
                     reads=[t_mm[bk]], writes=[t_V[t % 6]])
                if vf is not None:
                    P.op("act", ACTF(vf, mm[bk][:, 256:512], AF.Copy), reads=[t_mm[bk]], writes=[vf_tr])
                    if t == 8:
                        P.dma("sp", kwp_o, kf, ds_out, reads=[kf_tr])
                        P.dma("sp", vwp_o, vf, ds_out, reads=[vf_tr])
                    else:
                        P.dma("sp", knew_o, kf, ds_out, reads=[kf_tr])
                        P.dma("sp", vnew_o, vf, ds_out, reads=[vf_tr])
                fns = [TRP(tp[0:64, h * 128:(h + 1) * 128], kbd[:, h * 64:(h + 1) * 64], ident) for h in range(4)]
                P.group("pe", fns, reads=[t_kbd, t_const, t_const2], writes=[t_tp])
                P.op("dve", CP(kT2[:, :, t % 6, :], tp[0:64, 0:512].rearrange("p (h t) -> p h t", t=128)),
                     reads=[t_tp], writes=[t_kT2[t % 6]])

            nkv = len(kv_tiles)
            S1(0)
            if nkv > 1:
                S1(1)
            S2mm(0)
            for i in range(nkv):
                if i + 2 < nkv:
                    S1(i + 2)
                if i + 1 < nkv:
                    S2mm(i + 1)
                S2post(i)
            release_slab()

            MARKS.append((gi, 1, len(P.ops['pe'])))
            MARKS.append((gi, 2, len(P.ops['pe'])))
            if stop_after in (1, 2):
                raise _Stop()
            s0, s0_tr = next_slab()
            s1, s1_tr = next_slab()

            def A3(i):
                t = ctiles[i]
                for hq, (sl, sl_tr) in enumerate(((s0, s0_tr), (s1, s1_tr))):
                    dense_B(xnT, xcol[t], t_xnT[xblk[t]], sl, sl_tr, 16, mm[hq], t_mm[hq])

            def B3_chain(i, hq):
                t = ctiles[i]
                par = i % 2
                qk_norm_rope(mm[hq], t_mm[hq], 8, gq_bc, t, qb2[par][:, hq * 512:(hq + 1) * 512], t_qb2[par], QS)

            def B3_tr(i, hq):
                par = i % 2
                fns = [TRP(tp[0:64, c * 128:(c + 1) * 128], qb2[par][:, hq * 512 + c * 64:hq * 512 + (c + 1) * 64], ident) for c in range(8)]
                P.group("pe", fns, reads=[t_qb2[par], t_const, t_const2], writes=[t_tp])
                P.op("dve", CP(qT2[par][:, hq * 8:(hq + 1) * 8, :], tp[0:64, 0:1024].rearrange("p (c t) -> p c t", t=128)),
                     reads=[t_tp], writes=[t_qT2[par]])

            def blocks_of(t):
                if t == 9:
                    return [(9 % 6, m_news)]
                if t == 1:
                    return [(0, m_first), (1, m_cur)]
                return [((t - 1) % 6, m_prev), (t % 6, m_cur)]

            def C3_ST(i, kvh):
                t = ctiles[i]
                par = i % 2
                qT_, tqT_ = qT2[par], t_qT2[par]
                for bi, (slot, mask) in enumerate(blocks_of(t)):
                    bidx = (kvh % 2) * 2 + bi if t != 9 else 0
                    bank, btr = Bk[bidx], t_B[bidx]
                    fns = [MM(bank[:, :], ident, mask, True, False)]
                    for g in range(4):
                        h = 4 * kvh + g
                        fns.append(MM(bank[:, g * 128:(g + 1) * 128], kT2[:, kvh, slot, :], qT_[:, h, :], False, g == 3))
                    P.group("pe", fns, reads=[t_kT2[slot], tqT_, t_const, t_const2], writes=[btr])
                    pp = kvh % 2
                    P.op("act", ACTF(PT2[pp][:, bi, :], bank[:, :], AF.Exp, scale=0.125), reads=[btr], writes=[t_PT2[pp][bi]])

            def C3_PV(i, kvh):
                t = ctiles[i]
                blks = blocks_of(t)
                pp = kvh % 2
                fns = []
                wr = set()
                for g in range(4):
                    h = 4 * kvh + g
                    if t == 9:
                        bank, btr, c0 = head_slot(h)
                        first = (h % 7 == 0)
                    else:
                        bank, btr, c0 = Bk[4], t_B[4], g * 72
                        first = (g == 0)
                    wr.add(btr)
                    for bi, (slot, mask) in enumerate(blks):
                        last = (bi == len(blks) - 1) and (t != 9)
                        fns.append(MM(bank[:, c0:c0 + 72], PT2[pp][:, bi, g * 128:(g + 1) * 128], Vaug[:, slot, kvh, :],
                                      bi == 0 and first, last))
                P.group("pe", fns, reads=[t_PT2[pp][0], t_PT2[pp][1]] + [t_V[s_] for s_, _ in blks], writes=list(wr))
                if t != 9:
                    v = Bk[4][:, 0:288].rearrange("p (h e) -> p h e", e=72)
                    P.op("act", ACTF(T_g[:, kvh * 256:(kvh + 1) * 256].rearrange("p (h d) -> p h d", d=64), v[:, :, 0:64], AF.Copy),
                         reads=[t_B[4]], writes=[t_g])
                    P.op("act", ACTF(st8[:, 16 + 4 * kvh:20 + 4 * kvh], v[:, :, 64], AF.Copy), reads=[t_B[4]], writes=[t_st8b])
                    if kvh == 3:
                        P.op("dve", TT(st8[:, 16:32], st8[:, 16:32], sinkexp[:, 0:16], ALU.add), reads=[t_st8b], writes=[t_st8b])
                        P.op("dve", RCP(rall[:, 16:32], st8[:, 16:32]), reads=[t_st8b], writes=[t_rallb])
                        P.op("dve", TT(T_a.rearrange("p (h d) -> p h d", d=64), T_g.rearrange("p (h d) -> p h d", d=64),
                                       rall[:, 16:32].unsqueeze(2).to_broadcast([128, 16, 64]), ALU.mult),
                             reads=[t_g, t_rallb], writes=[t_a])

            def evac(bank, btr, nh, h0):
                v = bank[:, 0:nh * 72].rearrange("p (h e) -> p h e", e=72)
                P.op("dve", TT(st8[:, 16:16 + nh], v[:, :, 64], sinkexp[:, h0:h0 + nh], ALU.add), reads=[btr], writes=[t_st8b])
                P.op("dve", RCP(rall[:, 16:16 + nh], st8[:, 16:16 + nh]), reads=[t_st8b], writes=[t_rallb])
                P.op("dve", TT(T_a[:, h0 * 64:(h0 + nh) * 64].rearrange("p (h d) -> p h d", d=64), v[:, :, 0:64],
                               rall[:, 16:16 + nh].unsqueeze(2).to_broadcast([128, nh, 64]), ALU.mult),
                     reads=[btr, t_rallb], writes=[t_a])

            def C3_sample_cache(i):
                par = i % 2
                qT_, tqT_ = qT2[par], t_qT2[par]
                P.op("dve", MSET(cva, 1.0), writes=[t_cva])
                P.op("dve", MSET(PTz, 0.0), writes=[t_PTz])

                def load(sg):
                    P.dma("sp", ckf, ck[2 * sg:2 * sg + 2].rearrange("s k e -> k s e"), ds_ck, writes=[t_ckf])
                    P.dma("sp", cvf, cv[2 * sg:2 * sg + 2].rearrange("s k e -> k s e"), ds_cv, writes=[t_cvf])

                def cast(sg):
                    P.op("act", ACTF(ckd, ckf, AF.Copy), reads=[t_ckf], writes=[t_ckd])

                def castv(sg):
                    P.op("dve", CP(cva[:, :, :, 0:64], cvf.rearrange("p s (h d) -> p s h d", d=64)),
                         reads=[t_cvf], writes=[t_cva])

                def TR(b):
                    s_ = b % 2
                    fns = [TRP(tp[0:64, h * 128:(h + 1) * 128], ckd[:, s_, h * 64:(h + 1) * 64], ident) for h in range(4)]
                    P.group("pe", fns, reads=[t_ckd, t_const, t_const2], writes=[t_tp])
                    P.op("dve", CP(kcT2b[b % 2], tp[0:64, 0:512].rearrange("p (h t) -> p h t", t=128)), reads=[t_tp], writes=[t_kcT2b[b % 2]])

                def ST(b):
                    kc, kct = kcT2b[b % 2], t_kcT2b[b % 2]
                    fns = [MM(Bk[1][:, 0:128], ident, m_cache, True, False)]
                    for h in range(16):
                        fns.append(MM(Bk[1][:, h * 8:(h + 1) * 8], kc[:, h // 4, :], qT_[:, h, b * 8:(b + 1) * 8], False, h == 15))
                    P.group("pe", fns, reads=[kct, tqT_, t_const, t_const2], writes=[t_B[1]])
                    P.op("act", ACTF(PTz[:, :, b * 8:(b + 1) * 8], Bk[1][:, 0:128].rearrange("p (h i) -> p h i", i=8), AF.Exp, scale=0.125),
                         reads=[t_B[1]], writes=[t_PTz])

                def PV(b):
                    s_ = b % 2
                    fns = []
                    for h in range(16):
                        bank, btr, c0 = head_slot(h)
                        fns.append(MM(bank[:, c0:c0 + 72], PTz[:, h, :], cva[:, s_, h // 4, :], False, b == 15))
                    P.group("pe", fns, reads=[t_PTz, t_cva], writes=[t_B[2], t_B[3], t_B[4]])
                    P.op("act", MSET_ACT(PTz[:, :, b * 8:(b + 1) * 8]), reads=[], writes=[t_PTz])

                load(0); cast(0); castv(0)
                TR(0)
                for b in range(16):
                    ST(b)
                    if b % 2 == 0:
                        TR(b + 1)
                    PV(b)
                    if b % 2 == 1 and b + 1 < 16:
                        load((b + 1) // 2); cast((b + 1) // 2); castv((b + 1) // 2)
                        TR(b + 1)
                for bnk in range(3):
                    nh = 7 if bnk < 2 else 2
                    evac(Bk[2 + bnk], t_B[2 + bnk], nh, 7 * bnk)

            def C3_fin(i):
                li = i
                norm_transpose(T_a, t_a, 8, mixT, li * 128, t_mixA[li], gao_col, c_off=0)

            A3(0)
            B3_chain(0, 0); B3_tr(0, 0); B3_chain(0, 1); B3_tr(0, 1)
            if nct > 1:
                A3(1)
            for i in range(nct):
                nxt = i + 1 < nct
                t = ctiles[i]
                C3_ST(i, 0)
                C3_ST(i, 1)
                if nxt:
                    B3_chain(i + 1, 0)
                C3_PV(i, 0)
                C3_ST(i, 2)
                C3_PV(i, 1)
                C3_ST(i, 3)
                if nxt:
                    B3_tr(i + 1, 0)
                    B3_chain(i + 1, 1)
                C3_PV(i, 2)
                C3_PV(i, 3)
                if nxt:
                    B3_tr(i + 1, 1)
                    if i + 2 < nct:
                        A3(i + 2)
                if t == 9:
                    C3_sample_cache(i)
                C3_fin(i)
            release_slab()
            release_slab()

            MARKS.append((gi, 3, len(P.ops['pe'])))
            if stop_after == 3:
                raise _Stop()
            P.dma("sp", gg_bc, bcv_d[144:1168].partition_broadcast(128), ds_gg, writes=[t_gg])
            s0, s0_tr = next_slab()
            s1, s1_tr = next_slab()
            for t in ctiles:
                li = li_of[t]
                for hq, (sl, sl_tr) in enumerate(((s0, s0_tr), (s1, s1_tr))):
                    dense_B(xnT, xcol[t], t_xnT[xblk[t]], sl, sl_tr, 16, mm[hq], t_mm[hq])
                for hq in range(2):
                    P.op("act", ACTF(T_g[:, hq * 512:(hq + 1) * 512], mm[hq][:, :], AF.Gelu_apprx_tanh), reads=[t_mm[hq]], writes=[t_g])
                ssq = st8[:, 0:1]
                P.op("dve", MSET(ssq, 0.0), writes=[t_st8])
                P.op("act", ACTF(xsb[:, 0:1024], T_g, AF.Square, accum=ssq), reads=[t_g], writes=[t_xsb, t_st8])
                rstd_from_ssq(ssq, rall[:, 0:1], 1024.0, 1)
                P.op("dve", STT(g_n[:, li, :], T_g, rall[:, 0:1], gg_bc, ALU.mult, ALU.mult), reads=[t_g, t_rall, t_gg], writes=[t_gn[li]])
                if t == 9:
                    P.op("dve", STT(T_a, T_g, rall[:, 0:1], gg_bc, ALU.mult, ALU.mult), reads=[t_g, t_rall, t_gg], writes=[t_a])
                    P.dma("sp", sgv_o, T_a, ds_gout, reads=[t_a])
            release_slab()
            release_slab()

            MARKS.append((gi, 4, len(P.ops['pe'])))
            if stop_after == 4:
                raise _Stop()
            s0, s0_tr = next_slab()
            s1, s1_tr = next_slab()

            def A5(i):
                t = ctiles[i]
                for hq, (sl, sl_tr) in enumerate(((s0, s0_tr), (s1, s1_tr))):
                    dense_B(xnT, xcol[t], t_xnT[xblk[t]], sl, sl_tr, 16, mm[hq], t_mm[hq])

            A5(0)
            for i, t in enumerate(ctiles):
                li = li_of[t]
                kk = 1 if t == 9 else 0
                for hq in range(2):
                    P.op("act", ACTF(T_g[:, hq * 512:(hq + 1) * 512], mm[hq][:, :], AF.Gelu_apprx_tanh), reads=[t_mm[hq]], writes=[t_g])
                for hq in range(2):
                    fns = [MM(stp[hq][:, j * 128:(j + 1) * 128], wTm[:, kk, hq * 4 + j, :],
                              g_n[:, li, (hq * 4 + j) * 128:(hq * 4 + j + 1) * 128], True, True) for j in range(4)]
                    P.group("pe", fns, reads=[t_gn[li], t_wTm], writes=[t_st[hq]])
                if i + 1 < nct:
                    A5(i + 1)
                for hd in range(8):
                    P.op("dve", STT(T_a[:, hd * 128:(hd + 1) * 128], stp[hd // 4][:, (hd % 4) * 128:(hd % 4 + 1) * 128],
                                    bcol[kk][:, hd:hd + 1], T_g[:, hd * 128:(hd + 1) * 128], ALU.add, ALU.mult),
                         reads=[t_st[hd // 4], t_g, t_const, t_const2], writes=[t_a])
                norm_transpose(T_a, t_a, 8, mixT, li * 128, t_mixS[li], gso_col, c_off=8)
            release_slab()
            release_slab()

            MARKS.append((gi, 5, len(P.ops['pe'])))
            if stop_after == 5:
                raise _Stop()
            if debug and gi == 0:
                P.dma("sp", dbg_mix, mixT[:, :, :], ds_dbg, reads=t_mixA[0:nct] + t_mixS[0:nct])
            P.barrier(engines=("pe", "act", "dve", "sp"))
            for s in range(4):
                slab, slab_tr = next_slab()
                for t in ctiles:
                    li = li_of[t]
                    bk = li % 2
                    fns = [MM(mm[bk][:, :], mixT[:, c, li * 128:(li + 1) * 128], slab[:, c, :], c == 0, c == 15) for c in range(16)]
                    P.group("pe", fns, reads=[t_mixA[li], t_mixS[li], slab_tr], writes=[t_mm[bk]])
                    P.dma("sp", xr[bk], xs[t][:, s * 512:(s + 1) * 512], ds_xr[bk], writes=[t_xr[bk]])
                    P.op("dve", TT(hbuf[:, li, s * 512:(s + 1) * 512], mm[bk][:, :], xr[bk], ALU.add),
                         reads=[t_mm[bk], t_xr[bk]], writes=[t_h[li][s]])
                    if s == 3 and not debug:
                        hn_part1(li)
                        if li > 0:
                            hn_part2(li - 1)
                if s == 3 and not debug:
                    hn_part2(nct - 1)
                release_slab()
            if debug and gi == 0:
                P.dma("sp", dbg_h, hbuf[:, :, :], ds_dbg2, reads=[x for l in t_h for x in l])
            if debug:
                for t in ctiles:
                    norm_transpose_h(li_of[t])

            MARKS.append((gi, 6, len(P.ops['pe'])))
            if stop_after == 6:
                raise _Stop()
            ntok = nct * 128
            t_act.w = None
            t_act.r = [P.last["pe"]] + [tok for tr in (t_mixA + t_mixS) for tok in tr.r]
            banks = [mm[0], mm[1], Bk[0], Bk[1], Bk[2], Bk[3], Bk[4], tpF]
            t_banks = [t_mm[0], t_mm[1]] + t_B + [t_tp]
            chunks = [(0, ntok)] if ntok <= 512 else [(0, ntok // 2), (ntok // 2, ntok)]
            nck = len(chunks)
            wd_cnt = 0
            for pi, (pc0, pc1) in enumerate(FFN_PARTS):
                nch = pc1 - pc0
                for j in range(nch // 4):
                    slabG, slabG_tr = next_slab()
                    slabU, slabU_tr = next_slab()
                    for fc in range(4):
                        fcl = 4 * j + fc
                        st_ = fcl % 2
                        for ci, (ca, cb) in enumerate(chunks):
                            bg = (st_ * nck + ci) * 2
                            bu_ = bg + 1
                            n = cb - ca
                            fg = [MM(banks[bg][:, 0:n], slabG[:, c, fc * 128:(fc + 1) * 128], xnT[:, c, ca:cb], c == 0, c == 15) for c in range(16)]
                            P.group("pe", fg, reads=[slabG_tr] + t_xnT[0:nct], writes=[t_banks[bg]])
                            fu = [MM(banks[bu_][:, 0:n], slabU[:, c, fc * 128:(fc + 1) * 128], xnT[:, c, ca:cb], c == 0, c == 15) for c in range(16)]
                            P.group("pe", fu, reads=[slabU_tr] + t_xnT[0:nct], writes=[t_banks[bu_]])
                            si = st_ * 2 + ci
                            sg_ap = S[:, 2048 + 640 * st_:2048 + 640 * st_ + n] if nck == 1 else sgt[si][:, 0:n]
                            P.op("act", ACTF(sg_ap, banks[bg][:, 0:n], AF.Silu), reads=[t_banks[bg]], writes=[t_sgt[si]])
                            P.op("dve", TT(mixT[:, fcl, ca:cb], sg_ap, banks[bu_][:, 0:n], ALU.mult),
                                 reads=[t_sgt[si], t_banks[bu_]], writes=[t_act])
                    release_slab()
                    release_slab()
                last_part = pi == len(FFN_PARTS) - 1
                if last_part and next_kv is not None and not debug:
                    S1_tile(next_kv[0], 0)
                    S1_tile(next_kv[1], 1)
                for s in range(4):
                    slab, slab_tr = next_slab()
                    for t in ctiles:
                        li = li_of[t]
                        bi_ = wd_cnt % 8
                        wd_cnt += 1
                        bk = li % 2
                        fns = [MM(banks[bi_][:, :], mixT[:, c, li * 128:(li + 1) * 128], slab[:, c, :], c == 0, c == nch - 1) for c in range(nch)]
                        P.group("pe", fns, reads=[t_act, slab_tr], writes=[t_banks[bi_]])
                        if not last_part:
                            P.op("dve", TT(hbuf[:, li, s * 512:(s + 1) * 512], banks[bi_][:, :], hbuf[:, li, s * 512:(s + 1) * 512], ALU.add),
                                 reads=[t_banks[bi_]], writes=[t_h[li][s]])
                        else:
                            P.op("dve", TT(yb[bk], banks[bi_][:, :], hbuf[:, li, s * 512:(s + 1) * 512], ALU.add),
                                 reads=[t_banks[bi_], t_h[li][s]], writes=[t_yb[bk]])
                            P.dma("sp", y_o[t - 1][:, s * 512:(s + 1) * 512], yb[bk], ds_yb[bk], reads=[t_yb[bk]])
                    release_slab()

        xsbs = [xsb, X[:, 0:1024].bitcast(BF16)]
        t_xsbs = [t_xsb, t_xsb2]

        def hn_part1(li):
            b = li % 2
            ssq = st8[:, 0:1]
            src = hbuf[:, li, :]
            P.op("dve", MSET(ssq, 0.0), writes=[t_st8])
            P.op("act", ACTF(xsbs[b][:, :], src, AF.Square, accum=ssq), reads=t_h[li], writes=[t_xsbs[b], t_st8])
            rstd_from_ssq(ssq, rall[:, 0:1], float(D), 1)
            P.op("act", ACTF(xsbs[b][:, :], src, AF.Copy, scale=rall[:, 0:1]), reads=t_h[li] + [t_rall], writes=[t_xsbs[b]])

        def hn_part2(li):
            b = li % 2
            for h0 in (0, 8):
                tpb, tpt = next_tp(True)
                fns = [TRP(tpb[:, (c - h0) * 128:(c - h0 + 1) * 128], xsbs[b][:, c * 128:(c + 1) * 128], ident) for c in range(h0, h0 + 8)]
                P.group("pe", fns, reads=[t_xsbs[b], t_const, t_const2], writes=[tpt])
                P.op("dve", TT(xnT[:, h0:h0 + 8, li * 128:(li + 1) * 128], tpb[:, :].rearrange("p (c t) -> p c t", t=128),
                               gf_col[:, h0:h0 + 8].unsqueeze(2).to_broadcast([128, 8, 128]), ALU.mult),
                     reads=[tpt, t_const, t_const2], writes=[t_xnT[li]])

        def norm_transpose_h(li):
            hn_part1(li)
            hn_part2(li)

        def MSET_ACT(ap):
            return _est(lambda e: e.activation(out=ap, in_=ap, func=AF.Copy, scale=0.0), 0.25)


        try:
            if stop_after == 0:
                raise _Stop()
            for gi, (kv_tiles, ctiles) in enumerate(GROUPS):
                if gi > 0:
                    P.barrier(extra=[d.last for d in (ds_yb + ds_xr + ds_xt) if d.last is not None])
                nxt = GROUPS[gi + 1][0] if gi + 1 < len(GROUPS) else None
                run_group(gi, kv_tiles, ctiles, next_kv=nxt, s1_done=(2 if (gi > 0 and not debug) else 0))
                if stop_after == 7:
                    raise _Stop()
        except _Stop:
            pass

        final_deps = [d.last for d in (ds_out, ds_gout, ds_yb[0], ds_yb[1], ds_dbg, ds_dbg2) if d.last is not None]
        P.final_wait("sp", (lambda e: e.nop()), final_deps)
        P.finalize()
        LAST_PROG.clear()
        LAST_PROG.append(P)

        with nc.Block() as block:
            @block.tensor
            def _(e):
                P.replay("pe", e, esem)

            @block.scalar
            def _(e):
                P.replay("act", e, esem)

            @block.vector
            def _(e):
                P.replay("dve", e, esem)

            @block.gpsimd
            def _(e):
                P.replay("pool", e, esem)

            @block.sync
            def _(e):
                P.replay("sp", e, esem)
    return nc


def _rope_tables():
    half = 32
    inv = 10000.0 ** (-np.arange(half, dtype=np.float64) / float(half))
    out = np.zeros((8, 128, NT, 128), np.float32)
    for c in range(8):
        hf = c % 2
        for t in range(NT):
            if t == 9:
                pos = (16384 + (np.arange(128) % 8)).astype(np.float64)
            else:
                pos = (hf * 1024 + (t - 1) * 128 + np.arange(128)).astype(np.float64)
            ang = pos[:, None] * inv[None, :]
            cos = np.cos(ang).astype(np.float32)
            sin = np.sin(ang).astype(np.float32)
            out[c, :, t, 0:32] = cos
            out[c, :, t, 32:64] = cos
            out[c, :, t, 64:96] = -sin
            out[c, :, t, 96:128] = sin
    return out


def _masks(first_valid):
    j = np.arange(128)[:, None]
    i = np.arange(128)[None, :]
    m_cur = np.where(j <= i, 0.0, NEG).astype(np.float32)
    m_prev = np.where(j > i, 0.0, NEG).astype(np.float32)
    m_first = m_prev if first_valid else np.full((128, 128), NEG, np.float32)
    bj, jj = j // 8, j % 8
    bi, ii = i // 8, i % 8
    m_news = np.where((bj == bi) & (jj <= ii), 0.0, NEG).astype(np.float32)
    icol = (np.arange(128) % 8)[None, :]
    m_cache = np.where(j > icol, 0.0, NEG).astype(np.float32)
    ident = np.eye(128, dtype=np.float32)
    return np.concatenate([np.tile(m_cur, (1, 4)), np.tile(m_prev, (1, 4)), np.tile(m_first, (1, 4)),
                           np.tile(m_news, (1, 4)), m_cache, ident], axis=1)


_NC_CACHE = {}


def kernel(x_prompt, x_sample, cache_k_win, cache_v_win, attn_norm, w_in, q_norm, k_norm,
           sinks, sg_norm, sg_w, sg_b, attn_out_norm, sg_out_norm, w_o, ffn_norm,
           w_gate, w_up, w_down):
    f = lambda a: np.ascontiguousarray(np.asarray(a, dtype=np.float32))
    x_prompt, x_sample = f(x_prompt), f(x_sample)
    ck_all = f(cache_k_win)[0].reshape(128, 128, 256)
    cv_all = f(cache_v_win)[0].reshape(128, 128, 256)
    w_in_, w_o_, w_g_, w_u_, w_d_ = f(w_in)[0], f(w_o)[0], f(w_gate)[0], f(w_up)[0], f(w_down)[0]
    colf = lambda v, n: f(v)[0].reshape(n, 128).T
    sgb = f(sg_b)[0]
    bcol = sgb.T
    bcol_s = np.tile(sgb[:, :8].T, (16, 1))
    cols = np.ascontiguousarray(np.concatenate(
        [colf(attn_norm, 16), colf(ffn_norm, 16), colf(attn_out_norm, 8), colf(sg_out_norm, 8), bcol, bcol_s], axis=1))
    bcv = np.ascontiguousarray(np.concatenate([f(q_norm)[0], f(k_norm)[0], f(sinks)[0], f(sg_norm)[0]]))
    sgw = f(sg_w)[0]
    wT = np.ascontiguousarray(sgw.transpose(2, 0, 1)).reshape(128, 1024)
    wT_s = np.zeros((128, 8, 128), np.float32)
    blk = sgw[:, :8, :8].transpose(2, 0, 1)
    for b in range(16):
        wT_s[b * 8:(b + 1) * 8, :, b * 8:(b + 1) * 8] = blk
    wt = np.ascontiguousarray(np.concatenate([wT, wT_s.reshape(128, 1024)], axis=1))
    jj = np.arange(128)[:, None]
    ii = np.arange(128)[None, :]
    cmask = (jj <= ii).astype(np.float32)
    cmask_s = ((jj // 8 == ii // 8) & (jj % 8 <= ii % 8)).astype(np.float32)
    cm = np.ascontiguousarray(np.concatenate([cmask, cmask_s], axis=1))
    rope = _rope_tables()

    in_maps = []
    for c in range(8):
        b, hf = c // 2, c % 2
        xs = np.zeros((NT, 128, D), np.float32)
        if hf == 1:
            xs[0] = x_prompt[b, 896:1024]
        xs[1:9] = x_prompt[b, hf * 1024:(hf + 1) * 1024].reshape(8, 128, D)
        xs[9] = x_sample[16 * c:16 * c + 16].reshape(128, D)
        in_maps.append({
            "xs": xs, "ck": np.ascontiguousarray(ck_all[16 * c:16 * c + 16]), "cv": np.ascontiguousarray(cv_all[16 * c:16 * c + 16]),
            "w_in": w_in_, "w_o": w_o_, "w_gate": w_g_, "w_up": w_u_, "w_down": w_d_,
            "cols": cols, "bcv": bcv, "cs": np.ascontiguousarray(rope[c]), "mk": np.ascontiguousarray(_masks(hf == 1)),
            "wt": wt, "cm": cm,
        })
    if "nc" not in _NC_CACHE:
        _NC_CACHE["nc"] = build_program()
    res = run_bass_kernel_spmd(_NC_CACHE["nc"], in_maps, core_ids=list(range(8)))
    R = res.results
    y_prompt = np.zeros((4, 2048, D), np.float32)
    y_sample = np.zeros((128, 8, D), np.float32)
    kwp = np.zeros((1, 4, 128, 4, 64), np.float32)
    vwp = np.zeros((1, 4, 128, 4, 64), np.float32)
    kws = np.zeros((1, 128, 128, 4, 64), np.float32)
    vws = np.zeros((1, 128, 128, 4, 64), np.float32)
    sgv = np.zeros((1, 128, 8, 1024), np.float32)
    for c in range(8):
        b, hf = c // 2, c % 2
        y = np.asarray(R[c]["y"])
        y_prompt[b, hf * 1024:(hf + 1) * 1024] = y[0:8].reshape(1024, D)
        y_sample[16 * c:16 * c + 16] = y[8].reshape(16, 8, D)
        if hf == 1:
            kwp[0, b] = np.asarray(R[c]["kwp"]).reshape(128, 4, 64)
            vwp[0, b] = np.asarray(R[c]["vwp"]).reshape(128, 4, 64)
        kws[0, 16 * c:16 * c + 16, 0:120] = np.asarray(R[c]["kws"]).reshape(16, 120, 4, 64)
        vws[0, 16 * c:16 * c + 16, 0:120] = np.asarray(R[c]["vws"]).reshape(16, 120, 4, 64)
        kws[0, 16 * c:16 * c + 16, 120:128] = np.asarray(R[c]["knew"]).reshape(16, 8, 4, 64)
        vws[0, 16 * c:16 * c + 16, 120:128] = np.asarray(R[c]["vnew"]).reshape(16, 8, 4, 64)
        sgv[0, 16 * c:16 * c + 16] = np.asarray(R[c]["sgv"]).reshape(16, 8, 1024)
    return (y_prompt, y_sample, kwp, vwp, kws, vws, sgv)
```

```python
import contextlib
import numpy as np
import concourse.bass as bass
import concourse.mybir as mybir
from concourse.bass_utils import run_bass_kernel_spmd

F32 = mybir.dt.float32
BF16 = mybir.dt.bfloat16
AF = mybir.ActivationFunctionType
ALU = mybir.AluOpType
AX = mybir.AxisListType

D = 2048
DC = 16
NT = 10
DFF = 5632
EPS = 1e-6
NEG = -240000.0
GROUPS = [([0, 1, 2, 3, 4], [1, 2, 3, 4]), ([5, 6, 7, 8, 9], [5, 6, 7, 8, 9])]
NG = 5
FFN_PARTS = [(0, 16), (16, 32), (32, 44)]


class Tr:
    def __init__(self):
        self.w = None
        self.r = []


class DSem:
    def __init__(self, sem):
        self.sem = sem
        self.n = 0
        self.last = None

    def next(self):
        self.n += 16
        return ("dma", self.sem, self.n)


class Node:
    __slots__ = ("eng", "fns", "deps", "sig", "est", "tbl", "idx", "cnt", "fin", "dma", "tail")

    def __init__(self, eng, fns, deps, sig, est, tbl, idx):
        self.eng, self.fns, self.deps, self.sig, self.est, self.tbl, self.idx = eng, fns, deps, sig, est, tbl, idx
        self.cnt = None
        self.fin = None
        self.dma = None


SCHED = True
SCHED_WINDOW = 32
SCHED_SLACK = 0.3


class Prog:
    ENG = ("pe", "act", "dve", "pool", "sp")

    def __init__(self):
        self.nodes = []
        self.bar = {e: [] for e in self.ENG}
        self.last = {e: None for e in self.ENG}
        self.ops = {e: [] for e in self.ENG}

    def _deps(self, reads, writes, extra):
        deps = []
        for t in reads:
            if t.w is not None:
                deps.append(t.w)
        for t in writes:
            if t.w is not None:
                deps.append(t.w)
            deps.extend(t.r)
        deps.extend([d for d in extra if d is not None])
        return deps

    def _finish(self, tok, reads, writes):
        for t in writes:
            t.w = tok
            t.r = []
        for t in reads:
            if t not in writes:
                t.r.append(tok)

    def _node(self, eng, fns, deps, sig):
        est = sum(getattr(f, "est", 0.1) for f in fns)
        tbl = getattr(fns[0], "tbl", None)
        n = Node(eng, fns, deps, sig, est, tbl, len(self.nodes))
        self.nodes.append(n)
        self.ops[eng].extend(fns)
        return n

    def op(self, eng, fn, reads=(), writes=(), extra=()):
        return self.group(eng, [fn], reads, writes, extra)

    def group(self, eng, fns, reads=(), writes=(), extra=()):
        deps = self._deps(reads, writes, extra) + self.bar[eng]
        self.bar[eng] = []
        n = self._node(eng, list(fns), deps, True)
        tok = ("n", n)
        self._finish(tok, reads, writes)
        self.last[eng] = tok
        return tok

    def dma(self, eng, out, in_, ds, reads=(), writes=(), extra=(), nbytes=65536):
        deps = self._deps(reads, writes, extra) + self.bar[eng]
        self.bar[eng] = []
        tok0 = ds.next()
        sem = ds.sem

        def fn(e, out=out, in_=in_, sem=sem):
            return e.dma_start(out=out, in_=in_).then_inc(sem, 16)

        fn.est = 1.4 if eng == "pool" else 0.1
        n = self._node(eng, [fn], deps, False)
        n.dma = nbytes
        tok = ("dma", tok0[1], tok0[2], n)
        ds.last = tok
        self._finish(tok, reads, writes)
        return tok

    def barrier(self, engines=("pe", "act", "dve", "sp"), extra=()):
        toks = [self.last[e] for e in ("pe", "act", "dve") if self.last[e] is not None] + list(extra)
        for e in engines:
            self.bar[e] = list(toks)

    def final_wait(self, eng, fn, deps):
        n = self._node(eng, [fn], list(deps), False)
        return n

    def schedule(self):
        per = {e: [n for n in self.nodes if n.eng == e] for e in self.ENG}
        if not SCHED:
            return per
        for n in self.nodes:
            n.tail = 0.0
        for n in reversed(self.nodes):
            w = n.est + n.tail + (n.dma / 300e3 + 2.0 if n.dma is not None else 0.0)
            for d in n.deps:
                nd = d[1] if d[0] == "n" else (d[3] if len(d) > 3 else None)
                if nd is not None and w > nd.tail:
                    nd.tail = w
        pos = {e: 0 for e in self.ENG}
        pend = {e: list(per[e]) for e in self.ENG}
        free = {e: 0.0 for e in self.ENG}
        cur_tbl = [None]
        dma_free = [0.0]
        out = {e: [] for e in self.ENG}
        remaining = len(self.nodes)

        def dep_fin(d):
            nd = d[1] if d[0] == "n" else (d[3] if len(d) > 3 else None)
            if nd is None:
                return 0.0
            return nd.fin

        while remaining:
            best = None
            for e in self.ENG:
                lst = pend[e]
                if not lst:
                    continue
                W = SCHED_WINDOW if e in ("pe", "act", "dve") else 1
                seen = 0
                cands = []
                for n in lst:
                    if seen >= W:
                        break
                    seen += 1
                    ok = True
                    t = free[e]
                    for d in n.deps:
                        f = dep_fin(d)
                        if f is None:
                            ok = False
                            break
                        lat = 0.15 if (d[0] == "n" and d[1].eng != e) else 0.05
                        if f + lat > t:
                            t = f + lat
                    if not ok:
                        continue
                    if e == "act" and n.tbl is not None and n.tbl != cur_tbl[0]:
                        t += 1.3
                    cands.append((t, n))
                if not cands:
                    continue
                tmin = min(c[0] for c in cands)
                t, n = max((c for c in cands if c[0] <= tmin + SCHED_SLACK), key=lambda c: (c[1].tail, -c[1].idx))
                key = (t, n.idx)
                if best is None or key < best[0]:
                    best = (key, e, n)
            key, e, n = best
            t = key[0]
            if e == "act" and n.tbl is not None:
                cur_tbl[0] = n.tbl
            end = t + n.est + 0.08
            free[e] = end
            if n.dma is not None:
                st = max(end, dma_free[0])
                dma_free[0] = st + n.dma / 300e3
                n.fin = dma_free[0] + 2.0
            else:
                n.fin = end
            pend[e].remove(n)
            out[e].append(n)
            remaining -= 1
        self.makespan = max(free.values())
        return out

    def finalize(self):
        order = self.schedule()
        for e in self.ENG:
            c = 0
            for n in order[e]:
                if n.sig:
                    c += 1
                    n.cnt = c
        self.order = order

    def replay(self, name, e, sems):
        waited = {}
        for n in self.order[name]:
            for d in n.deps:
                if d[0] == "dma":
                    key = ("dma", id(d[1]))
                    if waited.get(key, 0) >= d[2]:
                        continue
                    e.wait_ge(d[1], d[2])
                    waited[key] = d[2]
                else:
                    src = d[1]
                    if waited.get(src.eng, 0) >= src.cnt:
                        continue
                    e.wait_ge(sems[src.eng], src.cnt)
                    waited[src.eng] = src.cnt
            ins = None
            for fn in n.fns:
                ins = fn(e)
            if n.sig:
                ins.then_inc(sems[name], 1)


def _free(ap):
    k = 1
    for d in ap.shape[1:]:
        k *= int(d)
    return k


def _est(fn, v, tbl=None):
    fn.est = v
    fn.tbl = tbl
    return fn


def MM(out, lhsT, rhs, start, stop):
    return _est(lambda e: e.matmul(out, lhsT=lhsT, rhs=rhs, start=start, stop=stop, skip_group_check=True),
                max(_free(rhs), 64) / 2400.0 + 0.015)


def TRP(out, in_, ident):
    return _est(lambda e: e.transpose(out=out, in_=in_, identity=ident), 0.075)


def ACTF(out, in_, func, scale=None, bias=None, accum=None):
    kw = {}
    if scale is not None:
        kw["scale"] = scale
    if bias is not None:
        kw["bias"] = bias
    if accum is not None:
        kw["accum_out"] = accum
    tbl = {AF.Exp: "exp", AF.Ln: "exp", AF.Gelu_apprx_tanh: "gelu", AF.Silu: "silu"}.get(func)
    return _est(lambda e: e.activation(out=out, in_=in_, func=func, **kw), 0.2 + _free(out) / 1050.0, tbl)


def TT(out, in0, in1, op):
    return _est(lambda e: e.tensor_tensor(out=out, in0=in0, in1=in1, op=op), 0.12 + _free(out) / 930.0)


def TS(out, in0, s1, op0, s2=None, op1=None):
    if op1 is None:
        return lambda e: e.tensor_scalar(out=out, in0=in0, scalar1=s1, scalar2=None, op0=op0)
    return lambda e: e.tensor_scalar(out=out, in0=in0, scalar1=s1, scalar2=s2, op0=op0, op1=op1)


def STT(out, in0, scalar, in1, op0, op1):
    return _est(lambda e: e.scalar_tensor_tensor(out=out, in0=in0, scalar=scalar, in1=in1, op0=op0, op1=op1), 0.12 + _free(out) / 930.0)


def RSUM(out, in_):
    return _est(lambda e: e.reduce_sum(out=out, in_=in_, axis=AX.X), 0.12 + _free(in_) / 930.0)


def RCP(out, in_):
    return _est(lambda e: e.reciprocal(out=out, in_=in_), 0.2)


def CP(out, in_):
    return _est(lambda e: e.tensor_copy(out=out, in_=in_), 0.12 + _free(out) / 1800.0)


def MSET(ap, v):
    return _est(lambda e: e.memset(ap, v), 0.05 + _free(ap) / 4000.0)


class _Stop(Exception):
    pass


MARKS = []
LAST_PROG = []


def build_program(debug=False, stop_after=None):
    nc = bass.Bass("TRN2", target_bir_lowering=False)
    dt_in = lambda name, shape: nc.dram_tensor(name, shape, F32, kind="ExternalInput").ap()
    dt_out = lambda name, shape: nc.dram_tensor(name, shape, F32, kind="ExternalOutput").ap()

    xs = dt_in("xs", [NT, 128, D])
    ck = dt_in("ck", [16, 128, 256])
    cv = dt_in("cv", [16, 128, 256])
    w_in = dt_in("w_in", [D, 3584])
    w_o = dt_in("w_o", [D, D])
    w_gate = dt_in("w_gate", [D, DFF])
    w_up = dt_in("w_up", [D, DFF])
    w_down = dt_in("w_down", [DFF, D])
    cols_d = dt_in("cols", [128, 64])
    bcv_d = dt_in("bcv", [1168])
    cs_d = dt_in("cs", [128, NT, 128])
    mk_d = dt_in("mk", [128, 2304])
    wt_d = dt_in("wt", [128, 2048])
    cm_d = dt_in("cm", [128, 256])

    y_o = dt_out("y", [9, 128, D])
    kwp_o = dt_out("kwp", [128, 256])
    vwp_o = dt_out("vwp", [128, 256])
    kws_o = dt_out("kws", [16, 120, 256])
    vws_o = dt_out("vws", [16, 120, 256])
    knew_o = dt_out("knew", [128, 256])
    vnew_o = dt_out("vnew", [128, 256])
    sgv_o = dt_out("sgv", [128, 1024])
    if debug:
        dbg_mix = nc.dram_tensor("dbg_mix", [128, 16, NG * 128], BF16, kind="ExternalOutput").ap()
        dbg_h = dt_out("dbg_h", [128, NG, D])

    P = Prog()
    es = contextlib.ExitStack()
    with es:
        def sb(name, shape, dt):
            return es.enter_context(nc.sbuf_tensor("sb_" + name, shape, dt))

        def ps(name, shape, dt):
            return es.enter_context(nc.psum_tensor("ps_" + name, shape, dt))

        def sem(name):
            return es.enter_context(nc.semaphore(name))

        NSLOT = 4
        ring = [sb(f"ring{i}", [128, 16, 512], BF16) for i in range(NSLOT)]
        xnT = sb("xnT", [128, 16, NG * 128], BF16)
        mixT = sb("mixT", [128, 16, NG * 128], BF16)
        Hreg = sb("Hreg", [128, NG * D], F32)
        hbuf = Hreg[:, :].rearrange("p (t d) -> p t d", d=D)
        Hb = Hreg[:, :].bitcast(BF16)
        kT2 = sb("kT2", [64, 4, 6, 128], BF16)
        Vaug = sb("Vaug", [128, 6, 4, 72], BF16)
        g_n = Hb[:, 0:NG * 1024].rearrange("p (t e) -> p t e", e=1024)
        X = sb("X", [128, 4096], F32)
        xsb = sb("xsb", [128, D], BF16)
        o0 = NG * 1024
        qb2 = [Hb[:, o0 + 1024 * i:o0 + 1024 * (i + 1)] for i in range(2)]
        qT2 = [Hb[0:64, o0 + 2048 + 2048 * i:o0 + 2048 + 2048 * (i + 1)].rearrange("p (h t) -> p h t", t=128) for i in range(2)]
        PT2 = [Hb[:, o0 + 6144 + 1024 * i:o0 + 6144 + 1024 * (i + 1)].rearrange("p (b e) -> p b e", e=512) for i in range(2)]
        kbd = Hb[:, o0 + 8192:o0 + 8448]
        assert (o0 + 8448) // 2 <= 7168
        S = sb("S", [128, 3328], F32)
        Sb = S[:, :].bitcast(BF16)
        PTz = Sb[:, 0:2048].rearrange("p (h t) -> p h t", t=128)
        ckd = Sb[:, 2048:2560].rearrange("p (s e) -> p s e", e=256)
        cva = Sb[:, 2560:3136].rearrange("p (s h e) -> p s h e", h=4, e=72)
        kcT2 = Sb[0:64, 3136:3648].rearrange("p (h t) -> p h t", t=128)
        kcT2b = [kcT2, Sb[0:64, 5696:6208].rearrange("p (h t) -> p h t", t=128)]
        ckf = S[:, 1824:2336].rearrange("p (s e) -> p s e", e=256)
        cvf = S[:, 2336:2848].rearrange("p (s e) -> p s e", e=256)
        cols = sb("cols", [128, 64], F32)
        bc = sb("bc", [128, 144], F32)
        cs = sb("cs", [128, NT, 128], F32)
        mk = sb("mk", [128, 2304], BF16)
        wTm = sb("wTm", [128, 2, 8, 128], BF16)
        st8 = sb("st8", [128, 64], F32)
        rall = sb("rall", [128, 64], F32)
        sinkexp = sb("sinkexp", [128, 16], F32)
        kfo = sb("kfo", [128, 2, 256], F32)
        vfo = sb("vfo", [128, 2, 256], F32)
        kfw = sb("kfw", [128, 256], F32)

        mm = [ps(f"mm{i}", [128, 512], F32) for i in range(2)]
        tp = ps("tp", [128, 1024], BF16)
        Bk = [ps(f"bk{i}", [128, 512], F32) for i in range(5)]
        stp = [Bk[0], Bk[1]]

        esem = {e: sem(f"s_{e}") for e in Prog.ENG}
        ds_ring = [[DSem(sem(f"d_ring{i}_{h}")) for h in range(2)] for i in range(NSLOT)]
        ds_xt = [DSem(sem(f"d_xt{i}")) for i in range(2)]
        ds_xr = [DSem(sem(f"d_xr{i}")) for i in range(2)]
        ds_yb = [DSem(sem(f"d_yb{i}")) for i in range(2)]
        ds_ck = DSem(sem("d_ck"))
        ds_cv = DSem(sem("d_cv"))
        ds_setup = DSem(sem("d_setup"))
        ds_setup2 = DSem(sem("d_setup2"))
        t_const2 = Tr()
        ds_out = DSem(sem("d_out"))
        ds_gout = DSem(sem("d_gout"))
        ds_gg = DSem(sem("d_gg"))
        t_gg = Tr()
        ds_dbg = DSem(sem("d_dbg"))
        ds_dbg2 = DSem(sem("d_dbg2"))

        xt = [X[:, 0:2048], X[:, 2048:4096]]
        T_sq = X[:, 0:512]
        T_qh = X[:, 512:1024]
        T_m1 = X[:, 1024:1536]
        T_m2 = X[:, 1536:2048]
        T_a = X[:, 2048:3072]
        T_g = X[:, 3072:4096]
        xr = [S[:, 0:512], S[:, 512:1024]]
        yb = [S[:, 1024:1536], S[:, 1536:2048]]
        sgt = [S[:, 2048 + 320 * i:2048 + 320 * (i + 1)] for i in range(4)]
        tpF = tp[:, :].bitcast(F32)
        ga_col = cols[:, 0:16]
        gf_col = cols[:, 16:32]
        gao_col = cols[:, 32:40]
        gso_col = cols[:, 40:48]
        bcol = [cols[:, 48:56], cols[:, 56:64]]
        gq_bc = bc[:, 0:64]
        gk_bc = bc[:, 64:128]
        sinks_bc = bc[:, 128:144]
        gg_bc = Hreg[:, 8192:9216]
        m_cur = mk[:, 0:512]
        m_prev = mk[:, 512:1024]
        m_first = mk[:, 1024:1536]
        m_news = mk[:, 1536:2048]
        m_cache = mk[:, 2048:2176]
        ident = mk[:, 2176:2304]

        t_stq, t_rq, t_stk, t_rk = Tr(), Tr(), Tr(), Tr()
        t_ksq, t_kqh, t_km1 = Tr(), Tr(), Tr()
        t_ring = [[Tr(), Tr()] for _ in range(NSLOT)]

        class SlabTr:
            def __init__(self, a, b, split):
                self.a, self.b, self.split = a, b, split

        def mm_group(fns, slab_tr, other_reads, writes):
            sp_ = slab_tr.split
            P.group("pe", fns[:sp_], reads=list(other_reads) + [slab_tr.a], writes=writes)
            return P.group("pe", fns[sp_:], reads=list(other_reads) + [slab_tr.b], writes=writes)

        t_xt = [Tr(), Tr()]
        t_xsb = Tr()
        t_xsb2 = Tr()
        t_tp = Tr()
        t_mm = [Tr(), Tr()]
        t_B = [Tr() for _ in range(5)]
        t_st = [t_B[0], t_B[1]]
        t_xnT = [Tr() for _ in range(NG)]
        t_mixA = [Tr() for _ in range(NG)]
        t_mixS = [Tr() for _ in range(NG)]
        t_act = Tr()
        t_h = [[Tr() for _ in range(4)] for _ in range(NG)]
        t_kT2 = [Tr() for _ in range(6)]
        t_V = [Tr() for _ in range(6)]
        t_gn = [Tr() for _ in range(NG)]
        t_sq, t_qh, t_m1, t_m2, t_a, t_g = Tr(), Tr(), Tr(), Tr(), Tr(), Tr()
        t_kbd, t_PTz = Tr(), Tr()
        t_qb2, t_qT2 = [Tr(), Tr()], [Tr(), Tr()]
        t_PT2 = [[Tr(), Tr()], [Tr(), Tr()]]
        t_ckf, t_cvf, t_ckd, t_cva, t_kcT2 = Tr(), Tr(), Tr(), Tr(), Tr()
        t_kcT2b = [t_kcT2, Tr()]
        t_xr, t_yb, t_sgt = [Tr(), Tr()], [Tr(), Tr()], [Tr() for _ in range(4)]
        t_ovb = [Tr(), Tr(), Tr()]
        t_const = Tr()
        t_wTm = Tr()
        t_st8 = Tr()
        t_st8b = Tr()
        t_rallb = Tr()
        t_rall = Tr()
        t_kfw = Tr()
        t_kfo, t_vfo = [Tr(), Tr()], [Tr(), Tr()]

        slab_list = []
        w_in_v = w_in.rearrange("(c p) e -> p c e", p=128)
        w_o_v = w_o.rearrange("(c p) e -> p c e", p=128)
        w_g_v = w_gate.rearrange("(c p) e -> p c e", p=128)
        w_u_v = w_up.rearrange("(c p) e -> p c e", p=128)
        w_d_v = w_down.rearrange("(c p) e -> p c e", p=128)

        def full_slab(src_v, c0):
            return [(lambda r, h=h: r[:, 8 * h:8 * h + 8, :], src_v[:, 8 * h:8 * h + 8, c0:c0 + 512]) for h in range(2)]

        for _g in range(len(GROUPS)):
            for c0 in (1024, 0, 512, 2560, 3072, 1536, 2048):
                slab_list.append(full_slab(w_in_v, c0))
            for s in range(4):
                slab_list.append(full_slab(w_o_v, s * 512))
            for (pc0, pc1) in FFN_PARTS:
                for j in range(pc0 // 4, pc1 // 4):
                    slab_list.append(full_slab(w_g_v, 512 * j))
                    slab_list.append(full_slab(w_u_v, 512 * j))
                nch = pc1 - pc0
                for s in range(4):
                    hh = nch // 2
                    slab_list.append([
                        (lambda r, hh=hh: r[:, 0:hh, :], w_d_v[:, pc0:pc0 + hh, s * 512:(s + 1) * 512]),
                        (lambda r, hh=hh, nch=nch: r[:, hh:nch, :], w_d_v[:, pc0 + hh:pc1, s * 512:(s + 1) * 512]),
                    ])
        slab_state = {"loaded": 0, "used": 0}
        slot_split = {}
        slab_split = [8 if len(e_) == 2 and e_[0][1].shape[1] == 8 else e_[0][1].shape[1] for e_ in slab_list]
        MARKS.clear()

        def load_next_slab():
            n = slab_state["loaded"]
            if n >= len(slab_list):
                return
            slot = n % NSLOT
            same = slot_split.get(slot, slab_split[n]) == slab_split[n]
            slot_split[slot] = slab_split[n]
            for h, (dst_fn, src) in enumerate(slab_list[n]):
                wr = [t_ring[slot][h]] if same else [t_ring[slot][0], t_ring[slot][1]]
                P.dma("pool", dst_fn(ring[slot]), src, ds_ring[slot][h], writes=wr, nbytes=2 * 1024 * 1024)
            slab_state["loaded"] = n + 1

        def next_slab():
            n = slab_state["used"]
            slab_state["used"] = n + 1
            assert n < slab_state["loaded"]
            return ring[n % NSLOT], SlabTr(t_ring[n % NSLOT][0], t_ring[n % NSLOT][1], slab_split[n])

        def release_slab():
            load_next_slab()

        P.dma("sp", cols[:, :], cols_d, ds_setup, writes=[t_const])
        P.dma("sp", bc[:, :], bcv_d[0:144].partition_broadcast(128), ds_setup, writes=[t_const])
        P.dma("sp", cs[:, :, :], cs_d, ds_setup, writes=[t_const])
        P.dma("sp", X[:, 0:2048], wt_d, ds_setup, writes=[t_const])
        P.dma("sp", X[:, 2048:2304], cm_d, ds_setup, writes=[t_const])
        P.dma("pool", mk[:, :], mk_d, ds_setup2, writes=[t_const2])
        P.dma("sp", kws_o, ck[:, 8:128, :], ds_out)
        P.dma("sp", vws_o, cv[:, 8:128, :], ds_out)
        for _ in range(NSLOT):
            load_next_slab()

        P.op("dve", MSET(Vaug[:, :, :, :], 1.0), writes=t_V)
        for k in range(2):
            P.op("dve", TT(wTm[:, k, :, :], X[:, k * 1024:(k + 1) * 1024].rearrange("p (h i) -> p h i", i=128),
                           X[:, 2048 + 128 * k:2048 + 128 * (k + 1)].unsqueeze(1).to_broadcast([128, 8, 128]), ALU.mult),
                 reads=[t_const], writes=[t_wTm] + ([t_xt[0], t_xt[1]] if k == 1 else []))
        P.op("act", ACTF(sinkexp[:, :], sinks_bc, AF.Exp), reads=[t_const, t_const2])

        def rstd_from_ssq(ssq_ap, out_ap, n, width, st_tr=None, r_tr=None):
            st_tr = st_tr or t_st8
            r_tr = r_tr or t_rall
            P.op("act", ACTF(ssq_ap, ssq_ap, AF.Ln, scale=1.0 / n, bias=EPS), reads=[st_tr], writes=[st_tr])
            return P.op("act", ACTF(out_ap, ssq_ap, AF.Exp, scale=-0.5), reads=[st_tr], writes=[r_tr])

        tp_pool = [(tp, t_tp), (Bk[2][:, :].bitcast(BF16), t_B[2]), (Bk[3][:, :].bitcast(BF16), t_B[3]), (Bk[4][:, :].bitcast(BF16), t_B[4])]
        tp_rr = [0]

        def next_tp(rotate):
            if not rotate:
                return tp, t_tp
            tp_rr[0] = (tp_rr[0] + 1) % len(tp_pool)
            return tp_pool[tp_rr[0]]

        def norm_transpose(src_ap, src_tr, nchunks, dstT, dst_col0, dst_tr, gcol, c_off=0, rotate=False):
            W = nchunks * 128
            ssq = st8[:, 0:1]
            P.op("dve", MSET(ssq, 0.0), writes=[t_st8])
            P.op("act", ACTF(xsb[:, 0:W], src_ap, AF.Square, accum=ssq), reads=[src_tr], writes=[t_xsb, t_st8])
            rstd_from_ssq(ssq, rall[:, 0:1], float(W), 1)
            P.op("act", ACTF(xsb[:, 0:W], src_ap, AF.Copy, scale=rall[:, 0:1]), reads=[src_tr, t_rall], writes=[t_xsb])
            for h0 in range(0, nchunks, 8):
                tpb, tpt = next_tp(rotate)
                fns = [TRP(tpb[:, (c - h0) * 128:(c - h0 + 1) * 128], xsb[:, c * 128:(c + 1) * 128], ident)
                       for c in range(h0, h0 + 8)]
                P.group("pe", fns, reads=[t_xsb, t_const, t_const2], writes=[tpt])
                P.op("dve", TT(dstT[:, c_off + h0:c_off + h0 + 8, dst_col0:dst_col0 + 128],
                               tpb[:, :].rearrange("p (c t) -> p c t", t=128),
                               gcol[:, h0:h0 + 8].unsqueeze(2).to_broadcast([128, 8, 128]), ALU.mult),
                     reads=[tpt, t_const, t_const2], writes=[dst_tr])

        def dense_B(srcT, col0, src_tr, slab, slab_tr, nch, bank, bank_tr, ch0=0, ncols=512, sc0=0):
            fns = [MM(bank[:, 0:ncols], srcT[:, ch0 + c, col0:col0 + 128], slab[:, c, sc0:sc0 + ncols], c == 0, c == nch - 1)
                   for c in range(nch)]
            return mm_group(fns, slab_tr, [src_tr], [bank_tr])

        def qk_norm_rope(src_bank, src_tr, nh, gbc, t, out_ap, out_tr, TS_):
            W = nh * 64
            v3 = lambda ap: ap.rearrange("p (h d) -> p h d", d=64)
            sq, qh, m1 = TS_["sq"][:, 0:W], TS_["qh"][:, 0:W], TS_["m1"][:, 0:W]
            m2 = sq
            tsq, tqh, tm1, tst, tr_ = TS_["tsq"], TS_["tqh"], TS_["tm1"], TS_["tst"], TS_["tr"]
            stv, rv = TS_["st"][:, 0:nh], TS_["r"][:, 0:nh]
            P.op("act", ACTF(sq, src_bank[:, 0:W], AF.Square), reads=[src_tr], writes=[tsq])
            P.op("dve", RSUM(stv, v3(sq)), reads=[tsq], writes=[tst])
            rstd_from_ssq(stv, rv, 64.0, nh, tst, tr_)
            P.op("dve", TT(v3(qh), v3(src_bank[:, 0:W]), rv.unsqueeze(2).to_broadcast([128, nh, 64]), ALU.mult),
                 reads=[src_tr, tr_], writes=[tqh])
            P.op("dve", TT(v3(qh), v3(qh), gbc.unsqueeze(1).to_broadcast([128, nh, 64]), ALU.mult),
                 reads=[t_const, t_const2], writes=[tqh])
            csA = cs[:, t, 0:64].unsqueeze(1).to_broadcast([128, nh, 64])
            P.op("dve", TT(v3(m1), v3(qh), csA, ALU.mult), reads=[tqh, t_const, t_const2], writes=[tm1])
            P.op("dve", TT(v3(m2)[:, :, 0:32], v3(qh)[:, :, 32:64],
                           cs[:, t, 64:96].unsqueeze(1).to_broadcast([128, nh, 32]), ALU.mult),
                 reads=[tqh, t_const, t_const2], writes=[tsq])
            P.op("dve", TT(v3(m2)[:, :, 32:64], v3(qh)[:, :, 0:32],
                           cs[:, t, 96:128].unsqueeze(1).to_broadcast([128, nh, 32]), ALU.mult),
                 reads=[tqh], writes=[tsq])
            P.op("dve", TT(out_ap, m1, m2, ALU.add), reads=[tm1, tsq], writes=[out_tr])

        def head_slot(h):
            return Bk[2 + h // 7], t_B[2 + h // 7], (h % 7) * 72

        def S1_tile(t, i):
            b = i % 2
            P.dma("sp", xt[b], xs[t], ds_xt[b], writes=[t_xt[b]] + ([t_xsb2] if b == 0 else []), nbytes=1024 * 1024)
            norm_transpose(xt[b], t_xt[b], 16, xnT, i * 128, t_xnT[i], ga_col, rotate=True)

        def run_group(gi, kv_tiles, ctiles, next_kv=None, s1_done=0):
            nct = len(ctiles)
            xcol = {t: (i * 128) for i, t in enumerate(kv_tiles)}
            if kv_tiles[0] != ctiles[0]:
                xblk = {t: i for i, t in enumerate(kv_tiles)}
            else:
                xblk = {t: i for i, t in enumerate(kv_tiles)}
            li_of = {t: i for i, t in enumerate(ctiles)}

            QS = dict(sq=T_sq, qh=T_qh, m1=T_m1, tsq=t_sq, tqh=t_qh, tm1=t_m1,
                      st=st8[:, 32:40], r=rall[:, 32:40], tst=t_stq, tr=t_rq)
            KS = dict(sq=Hreg[:, 7168:7424], qh=Hreg[:, 7424:7680], m1=Hreg[:, 7680:7936], tsq=t_ksq, tqh=t_kqh, tm1=t_km1,
                      st=st8[:, 48:52], r=rall[:, 48:52], tst=t_stk, tr=t_rk)

            def S1(i):
                if i >= s1_done:
                    S1_tile(kv_tiles[i], i)

            slab, slab_tr = next_slab()

            def S2mm(i):
                t = kv_tiles[i]
                dense_B(xnT, xcol[t], t_xnT[xblk[t]], slab, slab_tr, 16, mm[i % 2], t_mm[i % 2])

            def S2post(i):
                t = kv_tiles[i]
                bk = i % 2
                if t == 8:
                    kf, kf_tr, vf, vf_tr = kfo[:, 0, :], t_kfo[0], vfo[:, 0, :], t_vfo[0]
                elif t == 9:
                    kf, kf_tr, vf, vf_tr = kfo[:, 1, :], t_kfo[1], vfo[:, 1, :], t_vfo[1]
                else:
                    kf, kf_tr, vf, vf_tr = kfw[:, :], t_kfw, None, None
                qk_norm_rope(mm[bk], t_mm[bk], 4, gk_bc, t, kf, kf_tr, KS)
                P.op("act", ACTF(kbd, kf, AF.Copy), reads=[kf_tr], writes=[t_kbd])
                P.op("act", ACTF(Vaug[:, t % 6, :, 0:64], mm[bk][:, 256:512].rearrange("p (h d) -> p h d", d=64), AF.Copy),
                     reads=[t_mm[bk]], writes=[t_V[t % 6]])
                if vf is not None:
                    P.op("act", ACTF(vf, mm[bk][:, 256:512], AF.Copy), reads=[t_mm[bk]], writes=[vf_tr])
                    if t == 8:
                        P.dma("sp", kwp_o, kf, ds_out, reads=[kf_tr])
                        P.dma("sp", vwp_o, vf, ds_out, reads=[vf_tr])
                    else:
                        P.dma("sp", knew_o, kf, ds_out, reads=[kf_tr])
                        P.dma("sp", vnew_o, vf, ds_out, reads=[vf_tr])
                fns = [TRP(tp[0:64, h * 128:(h + 1) * 128], kbd[:, h * 64:(h + 1) * 64], ident) for h in range(4)]
                P.group("pe", fns, reads=[t_kbd, t_const, t_const2], writes=[t_tp])
                P.op("dve", CP(kT2[:, :, t % 6, :], tp[0:64, 0:512].rearrange("p (h t) -> p h t", t=128)),
                     reads=[t_tp], writes=[t_kT2[t % 6]])

            nkv = len(kv_tiles)
            S1(0)
            if nkv > 1:
                S1(1)
            S2mm(0)
            for i in range(nkv):
                if i + 2 < nkv:
                    S1(i + 2)
                if i + 1 < nkv:
                    S2mm(i + 1)
                S2post(i)
            release_slab()

            MARKS.append((gi, 1, len(P.ops['pe'])))
            MARKS.append((gi, 2, len(P.ops['pe'])))
            if stop_after in (1, 2):
                raise _Stop()
            s0, s0_tr = next_slab()
            s1, s1_tr = next_slab()

            def A3(i):
                t = ctiles[i]
                for hq, (sl, sl_tr) in enumerate(((s0, s0_tr), (s1, s1_tr))):
                    dense_B(xnT, xcol[t], t_xnT[xblk[t]], sl, sl_tr, 16, mm[hq], t_mm[hq])

            def B3_chain(i, hq):
                t = ctiles[i]
                par = i % 2
                qk_norm_rope(mm[hq], t_mm[hq], 8, gq_bc, t, qb2[par][:, hq * 512:(hq + 1) * 512], t_qb2[par], QS)

            def B3_tr(i, hq):
                par = i % 2
                fns = [TRP(tp[0:64, c * 128:(c + 1) * 128], qb2[par][:, hq * 512 + c * 64:hq * 512 + (c + 1) * 64], ident) for c in range(8)]
                P.group("pe", fns, reads=[t_qb2[par], t_const, t_const2], writes=[t_tp])
                P.op("dve", CP(qT2[par][:, hq * 8:(hq + 1) * 8, :], tp[0:64, 0:1024].rearrange("p (c t) -> p c t", t=128)),
                     reads=[t_tp], writes=[t_qT2[par]])

            def blocks_of(t):
                if t == 9:
                    return [(9 % 6, m_news)]
                if t == 1:
                    return [(0, m_first), (1, m_cur)]
                return [((t - 1) % 6, m_prev), (t % 6, m_cur)]

            def C3_ST(i, kvh):
                t = ctiles[i]
                par = i % 2
                qT_, tqT_ = qT2[par], t_qT2[par]
                for bi, (slot, mask) in enumerate(blocks_of(t)):
                    bidx = (kvh % 2) * 2 + bi if t != 9 else 0
                    bank, btr = Bk[bidx], t_B[bidx]
                    fns = [MM(bank[:, :], ident, mask, True, False)]
                    for g in range(4):
                        h = 4 * kvh + g
                        fns.append(MM(bank[:, g * 128:(g + 1) * 128], kT2[:, kvh, slot, :], qT_[:, h, :], False, g == 3))
                    P.group("pe", fns, reads=[t_kT2[slot], tqT_, t_const, t_const2], writes=[btr])
                    pp = kvh % 2
                    P.op("act", ACTF(PT2[pp][:, bi, :], bank[:, :], AF.Exp, scale=0.125), reads=[btr], writes=[t_PT2[pp][bi]])

            def C3_PV(i, kvh):
                t = ctiles[i]
                blks = blocks_of(t)
                pp = kvh % 2
                fns = []
                wr = set()
                for g in range(4):
                    h = 4 * kvh + g
                    if t == 9:
                        bank, btr, c0 = head_slot(h)
                        first = (h % 7 == 0)
                    else:
                        bank, btr, c0 = Bk[4], t_B[4], g * 72
                        first = (g == 0)
                    wr.add(btr)
                    for bi, (slot, mask) in enumerate(blks):
                        last = (bi == len(blks) - 1) and (t != 9)
                        fns.append(MM(bank[:, c0:c0 + 72], PT2[pp][:, bi, g * 128:(g + 1) * 128], Vaug[:, slot, kvh, :],
                                      bi == 0 and first, last))
                P.group("pe", fns, reads=[t_PT2[pp][0], t_PT2[pp][1]] + [t_V[s_] for s_, _ in blks], writes=list(wr))
                if t != 9:
                    v = Bk[4][:, 0:288].rearrange("p (h e) -> p h e", e=72)
                    P.op("act", ACTF(T_g[:, kvh * 256:(kvh + 1) * 256].rearrange("p (h d) -> p h d", d=64), v[:, :, 0:64], AF.Copy),
                         reads=[t_B[4]], writes=[t_g])
                    P.op("act", ACTF(st8[:, 16 + 4 * kvh:20 + 4 * kvh], v[:, :, 64], AF.Copy), reads=[t_B[4]], writes=[t_st8b])
                    if kvh == 3:
                        P.op("dve", TT(st8[:, 16:32], st8[:, 16:32], sinkexp[:, 0:16], ALU.add), reads=[t_st8b], writes=[t_st8b])
                        P.op("dve", RCP(rall[:, 16:32], st8[:, 16:32]), reads=[t_st8b], writes=[t_rallb])
                        P.op("dve", TT(T_a.rearrange("p (h d) -> p h d", d=64), T_g.rearrange("p (h d) -> p h d", d=64),
                                       rall[:, 16:32].unsqueeze(2).to_broadcast([128, 16, 64]), ALU.mult),
                             reads=[t_g, t_rallb], writes=[t_a])

            def evac(bank, btr, nh, h0):
                v = bank[:, 0:nh * 72].rearrange("p (h e) -> p h e", e=72)
                P.op("dve", TT(st8[:, 16:16 + nh], v[:, :, 64], sinkexp[:, h0:h0 + nh], ALU.add), reads=[btr], writes=[t_st8b])
                P.op("dve", RCP(rall[:, 16:16 + nh], st8[:, 16:16 + nh]), reads=[t_st8b], writes=[t_rallb])
                P.op("dve", TT(T_a[:, h0 * 64:(h0 + nh) * 64].rearrange("p (h d) -> p h d", d=64), v[:, :, 0:64],
                               rall[:, 16:16 + nh].unsqueeze(2).to_broadcast([128, nh, 64]), ALU.mult),
                     reads=[btr, t_rallb], writes=[t_a])

            def C3_sample_cache(i):
                par = i % 2
                qT_, tqT_ = qT2[par], t_qT2[par]
                P.op("dve", MSET(cva, 1.0), writes=[t_cva])
                P.op("dve", MSET(PTz, 0.0), writes=[t_PTz])

                def load(sg):
                    P.dma("sp", ckf, ck[2 * sg:2 * sg + 2].rearrange("s k e -> k s e"), ds_ck, writes=[t_ckf])
                    P.dma("sp", cvf, cv[2 * sg:2 * sg + 2].rearrange("s k e -> k s e"), ds_cv, writes=[t_cvf])

                def cast(sg):
                    P.op("act", ACTF(ckd, ckf, AF.Copy), reads=[t_ckf], writes=[t_ckd])

                def castv(sg):
                    P.op("dve", CP(cva[:, :, :, 0:64], cvf.rearrange("p s (h d) -> p s h d", d=64)),
                         reads=[t_cvf], writes=[t_cva])

                def TR(b):
                    s_ = b % 2
                    fns = [TRP(tp[0:64, h * 128:(h + 1) * 128], ckd[:, s_, h * 64:(h + 1) * 64], ident) for h in range(4)]
                    P.group("pe", fns, reads=[t_ckd, t_const, t_const2], writes=[t_tp])
                    P.op("dve", CP(kcT2b[b % 2], tp[0:64, 0:512].rearrange("p (h t) -> p h t", t=128)), reads=[t_tp], writes=[t_kcT2b[b % 2]])

                def ST(b):
                    kc, kct = kcT2b[b % 2], t_kcT2b[b % 2]
                    fns = [MM(Bk[1][:, 0:128], ident, m_cache, True, False)]
                    for h in range(16):
                        fns.append(MM(Bk[1][:, h * 8:(h + 1) * 8], kc[:, h // 4, :], qT_[:, h, b * 8:(b + 1) * 8], False, h == 15))
                    P.group("pe", fns, reads=[kct, tqT_, t_const, t_const2], writes=[t_B[1]])
                    P.op("act", ACTF(PTz[:, :, b * 8:(b + 1) * 8], Bk[1][:, 0:128].rearrange("p (h i) -> p h i", i=8), AF.Exp, scale=0.125),
                         reads=[t_B[1]], writes=[t_PTz])

                def PV(b):
                    s_ = b % 2
                    fns = []
                    for h in range(16):
                        bank, btr, c0 = head_slot(h)
                        fns.append(MM(bank[:, c0:c0 + 72], PTz[:, h, :], cva[:, s_, h // 4, :], False, b == 15))
                    P.group("pe", fns, reads=[t_PTz, t_cva], writes=[t_B[2], t_B[3], t_B[4]])
                    P.op("act", MSET_ACT(PTz[:, :, b * 8:(b + 1) * 8]), reads=[], writes=[t_PTz])

                load(0); cast(0); castv(0)
                TR(0)
                for b in range(16):
                    ST(b)
                    if b % 2 == 0:
                        TR(b + 1)
                    PV(b)
                    if b % 2 == 1 and b + 1 < 16:
                        load((b + 1) // 2); cast((b + 1) // 2); castv((b + 1) // 2)
                        TR(b + 1)
                for bnk in range(3):
                    nh = 7 if bnk < 2 else 2
                    evac(Bk[2 + bnk], t_B[2 + bnk], nh, 7 * bnk)

            def C3_fin(i):
                li = i
                norm_transpose(T_a, t_a, 8, mixT, li * 128, t_mixA[li], gao_col, c_off=0)

            A3(0)
            B3_chain(0, 0); B3_tr(0, 0); B3_chain(0, 1); B3_tr(0, 1)
            if nct > 1:
                A3(1)
            for i in range(nct):
                nxt = i + 1 < nct
                t = ctiles[i]
                C3_ST(i, 0)
                C3_ST(i, 1)
                if nxt:
                    B3_chain(i + 1, 0)
                C3_PV(i, 0)
                C3_ST(i, 2)
                C3_PV(i, 1)
                C3_ST(i, 3)
                if nxt:
                    B3_tr(i + 1, 0)
                    B3_chain(i + 1, 1)
                C3_PV(i, 2)
                C3_PV(i, 3)
                if nxt:
                    B3_tr(i + 1, 1)
                    if i + 2 < nct:
                        A3(i + 2)
                if t == 9:
                    C3_sample_cache(i)
                C3_fin(i)
            release_slab()
            release_slab()

            MARKS.append((gi, 3, len(P.ops['pe'])))
            if stop_after == 3:
                raise _Stop()
            P.dma("sp", gg_bc, bcv_d[144:1168].partition_broadcast(128), ds_gg, writes=[t_gg])
            s0, s0_tr = next_slab()
            s1, s1_tr = next_slab()
            for t in ctiles:
                li = li_of[t]
                for hq, (sl, sl_tr) in enumerate(((s0, s0_tr), (s1, s1_tr))):
                    dense_B(xnT, xcol[t], t_xnT[xblk[t]], sl, sl_tr, 16, mm[hq], t_mm[hq])
                for hq in range(2):
                    P.op("act", ACTF(T_g[:, hq * 512:(hq + 1) * 512], mm[hq][:, :], AF.Gelu_apprx_tanh), reads=[t_mm[hq]], writes=[t_g])
                ssq = st8[:, 0:1]
                P.op("dve", MSET(ssq, 0.0), writes=[t_st8])
                P.op("act", ACTF(xsb[:, 0:1024], T_g, AF.Square, accum=ssq), reads=[t_g], writes=[t_xsb, t_st8])
                rstd_from_ssq(ssq, rall[:, 0:1], 1024.0, 1)
                P.op("dve", STT(g_n[:, li, :], T_g, rall[:, 0:1], gg_bc, ALU.mult, ALU.mult), reads=[t_g, t_rall, t_gg], writes=[t_gn[li]])
                if t == 9:
                    P.op("dve", STT(T_a, T_g, rall[:, 0:1], gg_bc, ALU.mult, ALU.mult), reads=[t_g, t_rall, t_gg], writes=[t_a])
                    P.dma("sp", sgv_o, T_a, ds_gout, reads=[t_a])
            release_slab()
            release_slab()

            MARKS.append((gi, 4, len(P.ops['pe'])))
            if stop_after == 4:
                raise _Stop()
            s0, s0_tr = next_slab()
            s1, s1_tr = next_slab()

            def A5(i):
                t = ctiles[i]
                for hq, (sl, sl_tr) in enumerate(((s0, s0_tr), (s1, s1_tr))):
                    dense_B(xnT, xcol[t], t_xnT[xblk[t]], sl, sl_tr, 16, mm[hq], t_mm[hq])

            A5(0)
            for i, t in enumerate(ctiles):
                li = li_of[t]
                kk = 1 if t == 9 else 0
                for hq in range(2):
                    P.op("act", ACTF(T_g[:, hq * 512:(hq + 1) * 512], mm[hq][:, :], AF.Gelu_apprx_tanh), reads=[t_mm[hq]], writes=[t_g])
                for hq in range(2):
                    fns = [MM(stp[hq][:, j * 128:(j + 1) * 128], wTm[:, kk, hq * 4 + j, :],
                              g_n[:, li, (hq * 4 + j) * 128:(hq * 4 + j + 1) * 128], True, True) for j in range(4)]
                    P.group("pe", fns, reads=[t_gn[li], t_wTm], writes=[t_st[hq]])
                if i + 1 < nct:
                    A5(i + 1)
                for hd in range(8):
                    P.op("dve", STT(T_a[:, hd * 128:(hd + 1) * 128], stp[hd // 4][:, (hd % 4) * 128:(hd % 4 + 1) * 128],
                                    bcol[kk][:, hd:hd + 1], T_g[:, hd * 128:(hd + 1) * 128], ALU.add, ALU.mult),
                         reads=[t_st[hd // 4], t_g, t_const, t_const2], writes=[t_a])
                norm_transpose(T_a, t_a, 8, mixT, li * 128, t_mixS[li], gso_col, c_off=8)
            release_slab()
            release_slab()

            MARKS.append((gi, 5, len(P.ops['pe'])))
            if stop_after == 5:
                raise _Stop()
            if debug and gi == 0:
                P.dma("sp", dbg_mix, mixT[:, :, :], ds_dbg, reads=t_mixA[0:nct] + t_mixS[0:nct])
            P.barrier(engines=("pe", "act", "dve", "sp"))
            for s in range(4):
                slab, slab_tr = next_slab()
                for t in ctiles:
                    li = li_of[t]
                    bk = li % 2
                    fns = [MM(mm[bk][:, :], mixT[:, c, li * 128:(li + 1) * 128], slab[:, c, :], c == 0, c == 15) for c in range(16)]
                    mm_group(fns, slab_tr, [t_mixA[li], t_mixS[li]], [t_mm[bk]])
                    P.dma("sp", xr[bk], xs[t][:, s * 512:(s + 1) * 512], ds_xr[bk], writes=[t_xr[bk]])
                    P.op("dve", TT(hbuf[:, li, s * 512:(s + 1) * 512], mm[bk][:, :], xr[bk], ALU.add),
                         reads=[t_mm[bk], t_xr[bk]], writes=[t_h[li][s]])
                    if s == 3 and not debug:
                        hn_part1(li)
                        if li > 0:
                            hn_part2(li - 1)
                if s == 3 and not debug:
                    hn_part2(nct - 1)
                release_slab()
            if debug and gi == 0:
                P.dma("sp", dbg_h, hbuf[:, :, :], ds_dbg2, reads=[x for l in t_h for x in l])
            if debug:
                for t in ctiles:
                    norm_transpose_h(li_of[t])

            MARKS.append((gi, 6, len(P.ops['pe'])))
            if stop_after == 6:
                raise _Stop()
            ntok = nct * 128
            t_act.w = None
            t_act.r = [P.last["pe"]] + [tok for tr in (t_mixA + t_mixS) for tok in tr.r]
            banks = [mm[0], mm[1], Bk[0], Bk[1], Bk[2], Bk[3], Bk[4], tpF]
            t_banks = [t_mm[0], t_mm[1]] + t_B + [t_tp]
            chunks = [(0, ntok)] if ntok <= 512 else [(0, ntok // 2), (ntok // 2, ntok)]
            nck = len(chunks)
            wd_cnt = 0
            for pi, (pc0, pc1) in enumerate(FFN_PARTS):
                nch = pc1 - pc0
                for j in range(nch // 4):
                    slabG, slabG_tr = next_slab()
                    slabU, slabU_tr = next_slab()
                    for fc in range(4):
                        fcl = 4 * j + fc
                        st_ = fcl % 2
                        for ci, (ca, cb) in enumerate(chunks):
                            bg = (st_ * nck + ci) * 2
                            bu_ = bg + 1
                            n = cb - ca
                            fg = [MM(banks[bg][:, 0:n], slabG[:, c, fc * 128:(fc + 1) * 128], xnT[:, c, ca:cb], c == 0, c == 15) for c in range(16)]
                            mm_group(fg, slabG_tr, t_xnT[0:nct], [t_banks[bg]])
                            fu = [MM(banks[bu_][:, 0:n], slabU[:, c, fc * 128:(fc + 1) * 128], xnT[:, c, ca:cb], c == 0, c == 15) for c in range(16)]
                            mm_group(fu, slabU_tr, t_xnT[0:nct], [t_banks[bu_]])
                            si = st_ * 2 + ci
                            sg_ap = S[:, 2048 + 640 * st_:2048 + 640 * st_ + n] if nck == 1 else sgt[si][:, 0:n]
                            P.op("act", ACTF(sg_ap, banks[bg][:, 0:n], AF.Silu), reads=[t_banks[bg]], writes=[t_sgt[si]])
                            P.op("dve", TT(mixT[:, fcl, ca:cb], sg_ap, banks[bu_][:, 0:n], ALU.mult),
                                 reads=[t_sgt[si], t_banks[bu_]], writes=[t_act])
                    release_slab()
                    release_slab()
                last_part = pi == len(FFN_PARTS) - 1
                if last_part and next_kv is not None and not debug:
                    S1_tile(next_kv[0], 0)
                    S1_tile(next_kv[1], 1)
                for s in range(4):
                    slab, slab_tr = next_slab()
                    for t in ctiles:
                        li = li_of[t]
                        bi_ = wd_cnt % 8
                        wd_cnt += 1
                        bk = li % 2
                        fns = [MM(banks[bi_][:, :], mixT[:, c, li * 128:(li + 1) * 128], slab[:, c, :], c == 0, c == nch - 1) for c in range(nch)]
                        mm_group(fns, slab_tr, [t_act], [t_banks[bi_]])
                        if not last_part:
                            P.op("dve", TT(hbuf[:, li, s * 512:(s + 1) * 512], banks[bi_][:, :], hbuf[:, li, s * 512:(s + 1) * 512], ALU.add),
                                 reads=[t_banks[bi_]], writes=[t_h[li][s]])
                        else:
                            P.op("dve", TT(yb[bk], banks[bi_][:, :], hbuf[:, li, s * 512:(s + 1) * 512], ALU.add),
                                 reads=[t_banks[bi_], t_h[li][s]], writes=[t_yb[bk]])
                            P.dma("sp", y_o[t - 1][:, s * 512:(s + 1) * 512], yb[bk], ds_yb[bk], reads=[t_yb[bk]])
                    release_slab()

        xsbs = [xsb, X[:, 0:1024].bitcast(BF16)]
        t_xsbs = [t_xsb, t_xsb2]

        def hn_part1(li):
            b = li % 2
            ssq = st8[:, 0:1]
            src = hbuf[:, li, :]
            P.op("dve", MSET(ssq, 0.0), writes=[t_st8])
            P.op("act", ACTF(xsbs[b][:, :], src, AF.Square, accum=ssq), reads=t_h[li], writes=[t_xsbs[b], t_st8])
            rstd_from_ssq(ssq, rall[:, 0:1], float(D), 1)
            P.op("act", ACTF(xsbs[b][:, :], src, AF.Copy, scale=rall[:, 0:1]), reads=t_h[li] + [t_rall], writes=[t_xsbs[b]])

        def hn_part2(li):
            b = li % 2
            for h0 in (0, 8):
                tpb, tpt = next_tp(True)
                fns = [TRP(tpb[:, (c - h0) * 128:(c - h0 + 1) * 128], xsbs[b][:, c * 128:(c + 1) * 128], ident) for c in range(h0, h0 + 8)]
                P.group("pe", fns, reads=[t_xsbs[b], t_const, t_const2], writes=[tpt])
                P.op("dve", TT(xnT[:, h0:h0 + 8, li * 128:(li + 1) * 128], tpb[:, :].rearrange("p (c t) -> p c t", t=128),
                               gf_col[:, h0:h0 + 8].unsqueeze(2).to_broadcast([128, 8, 128]), ALU.mult),
                     reads=[tpt, t_const, t_const2], writes=[t_xnT[li]])

        def norm_transpose_h(li):
            hn_part1(li)
            hn_part2(li)

        def MSET_ACT(ap):
            return _est(lambda e: e.activation(out=ap, in_=ap, func=AF.Copy, scale=0.0), 0.25)


        try:
            if stop_after == 0:
                raise _Stop()
            for gi, (kv_tiles, ctiles) in enumerate(GROUPS):
                if gi > 0:
                    P.barrier(extra=[d.last for d in (ds_yb + ds_xr + ds_xt) if d.last is not None])
                nxt = GROUPS[gi + 1][0] if gi + 1 < len(GROUPS) else None
                run_group(gi, kv_tiles, ctiles, next_kv=nxt, s1_done=(2 if (gi > 0 and not debug) else 0))
                if stop_after == 7:
                    raise _Stop()
        except _Stop:
            pass

        final_deps = [d.last for d in (ds_out, ds_gout, ds_yb[0], ds_yb[1], ds_dbg, ds_dbg2) if d.last is not None]
        P.final_wait("sp", (lambda e: e.nop()), final_deps)
        P.finalize()
        LAST_PROG.clear()
        LAST_PROG.append(P)

        with nc.Block() as block:
            @block.tensor
            def _(e):
                P.replay("pe", e, esem)

            @block.scalar
            def _(e):
                P.replay("act", e, esem)

            @block.vector
            def _(e):
                P.replay("dve", e, esem)

            @block.gpsimd
            def _(e):
                P.replay("pool", e, esem)

            @block.sync
            def _(e):
                P.replay("sp", e, esem)
    return nc


def _rope_tables():
    half = 32
    inv = 10000.0 ** (-np.arange(half, dtype=np.float64) / float(half))
    out = np.zeros((8, 128, NT, 128), np.float32)
    for c in range(8):
        hf = c % 2
        for t in range(NT):
            if t == 9:
                pos = (16384 + (np.arange(128) % 8)).astype(np.float64)
            else:
                pos = (hf * 1024 + (t - 1) * 128 + np.arange(128)).astype(np.float64)
            ang = pos[:, None] * inv[None, :]
            cos = np.cos(ang).astype(np.float32)
            sin = np.sin(ang).astype(np.float32)
            out[c, :, t, 0:32] = cos
            out[c, :, t, 32:64] = cos
            out[c, :, t, 64:96] = -sin
            out[c, :, t, 96:128] = sin
    return out


def _masks(first_valid):
    j = np.arange(128)[:, None]
    i = np.arange(128)[None, :]
    m_cur = np.where(j <= i, 0.0, NEG).astype(np.float32)
    m_prev = np.where(j > i, 0.0, NEG).astype(np.float32)
    m_first = m_prev if first_valid else np.full((128, 128), NEG, np.float32)
    bj, jj = j // 8, j % 8
    bi, ii = i // 8, i % 8
    m_news = np.where((bj == bi) & (jj <= ii), 0.0, NEG).astype(np.float32)
    icol = (np.arange(128) % 8)[None, :]
    m_cache = np.where(j > icol, 0.0, NEG).astype(np.float32)
    ident = np.eye(128, dtype=np.float32)
    return np.concatenate([np.tile(m_cur, (1, 4)), np.tile(m_prev, (1, 4)), np.tile(m_first, (1, 4)),
                           np.tile(m_news, (1, 4)), m_cache, ident], axis=1)


_NC_CACHE = {}


def kernel(x_prompt, x_sample, cache_k_win, cache_v_win, attn_norm, w_in, q_norm, k_norm,
           sinks, sg_norm, sg_w, sg_b, attn_out_norm, sg_out_norm, w_o, ffn_norm,
           w_gate, w_up, w_down):
    f = lambda a: np.ascontiguousarray(np.asarray(a, dtype=np.float32))
    x_prompt, x_sample = f(x_prompt), f(x_sample)
    ck_all = f(cache_k_win)[0].reshape(128, 128, 256)
    cv_all = f(cache_v_win)[0].reshape(128, 128, 256)
    w_in_, w_o_, w_g_, w_u_, w_d_ = f(w_in)[0], f(w_o)[0], f(w_gate)[0], f(w_up)[0], f(w_down)[0]
    colf = lambda v, n: f(v)[0].reshape(n, 128).T
    sgb = f(sg_b)[0]
    bcol = sgb.T
    bcol_s = np.tile(sgb[:, :8].T, (16, 1))
    cols = np.ascontiguousarray(np.concatenate(
        [colf(attn_norm, 16), colf(ffn_norm, 16), colf(attn_out_norm, 8), colf(sg_out_norm, 8), bcol, bcol_s], axis=1))
    bcv = np.ascontiguousarray(np.concatenate([f(q_norm)[0], f(k_norm)[0], f(sinks)[0], f(sg_norm)[0]]))
    sgw = f(sg_w)[0]
    wT = np.ascontiguousarray(sgw.transpose(2, 0, 1)).reshape(128, 1024)
    wT_s = np.zeros((128, 8, 128), np.float32)
    blk = sgw[:, :8, :8].transpose(2, 0, 1)
    for b in range(16):
        wT_s[b * 8:(b + 1) * 8, :, b * 8:(b + 1) * 8] = blk
    wt = np.ascontiguousarray(np.concatenate([wT, wT_s.reshape(128, 1024)], axis=1))
    jj = np.arange(128)[:, None]
    ii = np.arange(128)[None, :]
    cmask = (jj <= ii).astype(np.float32)
    cmask_s = ((jj // 8 == ii // 8) & (jj % 8 <= ii % 8)).astype(np.float32)
    cm = np.ascontiguousarray(np.concatenate([cmask, cmask_s], axis=1))
    rope = _rope_tables()

    in_maps = []
    for c in range(8):
        b, hf = c // 2, c % 2
        xs = np.zeros((NT, 128, D), np.float32)
        if hf == 1:
            xs[0] = x_prompt[b, 896:1024]
        xs[1:9] = x_prompt[b, hf * 1024:(hf + 1) * 1024].reshape(8, 128, D)
        xs[9] = x_sample[16 * c:16 * c + 16].reshape(128, D)
        in_maps.append({
            "xs": xs, "ck": np.ascontiguousarray(ck_all[16 * c:16 * c + 16]), "cv": np.ascontiguousarray(cv_all[16 * c:16 * c + 16]),
            "w_in": w_in_, "w_o": w_o_, "w_gate": w_g_, "w_up": w_u_, "w_down": w_d_,
            "cols": cols, "bcv": bcv, "cs": np.ascontiguousarray(rope[c]), "mk": np.ascontiguousarray(_masks(hf == 1)),
            "wt": wt, "cm": cm,
        })
    if "nc" not in _NC_CACHE:
        _NC_CACHE["nc"] = build_program()
    res = run_bass_kernel_spmd(_NC_CACHE["nc"], in_maps, core_ids=list(range(8)))
    R = res.results
    y_prompt = np.zeros((4, 2048, D), np.float32)
    y_sample = np.zeros((128, 8, D), np.float32)
    kwp = np.zeros((1, 4, 128, 4, 64), np.float32)
    vwp = np.zeros((1, 4, 128, 4, 64), np.float32)
    kws = np.zeros((1, 128, 128, 4, 64), np.float32)
    vws = np.zeros((1, 128, 128, 4, 64), np.float32)
    sgv = np.zeros((1, 128, 8, 1024), np.float32)
    for c in range(8):
        b, hf = c // 2, c % 2
        y = np.asarray(R[c]["y"])
        y_prompt[b, hf * 1024:(hf + 1) * 1024] = y[0:8].reshape(1024, D)
        y_sample[16 * c:16 * c + 16] = y[8].reshape(16, 8, D)
        if hf == 1:
            kwp[0, b] = np.asarray(R[c]["kwp"]).reshape(128, 4, 64)
            vwp[0, b] = np.asarray(R[c]["vwp"]).reshape(128, 4, 64)
        kws[0, 16 * c:16 * c + 16, 0:120] = np.asarray(R[c]["kws"]).reshape(16, 120, 4, 64)
        vws[0, 16 * c:16 * c + 16, 0:120] = np.asarray(R[c]["vws"]).reshape(16, 120, 4, 64)
        kws[0, 16 * c:16 * c + 16, 120:128] = np.asarray(R[c]["knew"]).reshape(16, 8, 4, 64)
        vws[0, 16 * c:16 * c + 16, 120:128] = np.asarray(R[c]["vnew"]).reshape(16, 8, 4, 64)
        sgv[0, 16 * c:16 * c + 16] = np.asarray(R[c]["sgv"]).reshape(16, 8, 1024)
    return (y_prompt, y_sample, kwp, vwp, kws, vws, sgv)
```

```python
import contextlib
import numpy as np
import concourse.bass as bass
import concourse.mybir as mybir
from concourse.bass_utils import run_bass_kernel_spmd

F32 = mybir.dt.float32
BF16 = mybir.dt.bfloat16
AF = mybir.ActivationFunctionType
ALU = mybir.AluOpType
AX = mybir.AxisListType

D = 2048
DC = 16
NT = 10
DFF = 5632
EPS = 1e-6
NEG = -240000.0
GROUPS = [([0, 1, 2, 3, 4], [1, 2, 3, 4]), ([5, 6, 7, 8, 9], [5, 6, 7, 8, 9])]
NG = 5
FFN_PARTS = [(0, 16), (16, 32), (32, 44)]


class Tr:
    def __init__(self):
        self.w = None
        self.r = []


class DSem:
    def __init__(self, sem):
        self.sem = sem
        self.n = 0
        self.last = None

    def next(self):
        self.n += 16
        return ("dma", self.sem, self.n)


class Node:
    __slots__ = ("eng", "fns", "deps", "sig", "est", "tbl", "idx", "cnt", "fin", "dma", "tail")

    def __init__(self, eng, fns, deps, sig, est, tbl, idx):
        self.eng, self.fns, self.deps, self.sig, self.est, self.tbl, self.idx = eng, fns, deps, sig, est, tbl, idx
        self.cnt = None
        self.fin = None
        self.dma = None


SCHED = True
SCHED_WINDOW = 32
SCHED_SLACK = 0.3


class Prog:
    ENG = ("pe", "act", "dve", "pool", "sp")

    def __init__(self):
        self.nodes = []
        self.bar = {e: [] for e in self.ENG}
        self.last = {e: None for e in self.ENG}
        self.ops = {e: [] for e in self.ENG}

    def _deps(self, reads, writes, extra):
        deps = []
        for t in reads:
            if t.w is not None:
                deps.append(t.w)
        for t in writes:
            if t.w is not None:
                deps.append(t.w)
            deps.extend(t.r)
        deps.extend([d for d in extra if d is not None])
        return deps

    def _finish(self, tok, reads, writes):
        for t in writes:
            t.w = tok
            t.r = []
        for t in reads:
            if t not in writes:
                t.r.append(tok)

    def _node(self, eng, fns, deps, sig):
        est = sum(getattr(f, "est", 0.1) for f in fns)
        tbl = getattr(fns[0], "tbl", None)
        n = Node(eng, fns, deps, sig, est, tbl, len(self.nodes))
        self.nodes.append(n)
        self.ops[eng].extend(fns)
        return n

    def op(self, eng, fn, reads=(), writes=(), extra=()):
        return self.group(eng, [fn], reads, writes, extra)

    def group(self, eng, fns, reads=(), writes=(), extra=()):
        deps = self._deps(reads, writes, extra) + self.bar[eng]
        self.bar[eng] = []
        n = self._node(eng, list(fns), deps, True)
        tok = ("n", n)
        self._finish(tok, reads, writes)
        self.last[eng] = tok
        return tok

    def dma(self, eng, out, in_, ds, reads=(), writes=(), extra=(), nbytes=65536):
        deps = self._deps(reads, writes, extra) + self.bar[eng]
        self.bar[eng] = []
        tok0 = ds.next()
        sem = ds.sem

        def fn(e, out=out, in_=in_, sem=sem):
            return e.dma_start(out=out, in_=in_).then_inc(sem, 16)

        fn.est = 1.4 if eng == "pool" else 0.1
        n = self._node(eng, [fn], deps, False)
        n.dma = nbytes
        tok = ("dma", tok0[1], tok0[2], n)
        ds.last = tok
        self._finish(tok, reads, writes)
        return tok

    def barrier(self, engines=("pe", "act", "dve", "sp"), extra=()):
        toks = [self.last[e] for e in ("pe", "act", "dve") if self.last[e] is not None] + list(extra)
        for e in engines:
            self.bar[e] = list(toks)

    def final_wait(self, eng, fn, deps):
        n = self._node(eng, [fn], list(deps), False)
        return n

    def schedule(self):
        per = {e: [n for n in self.nodes if n.eng == e] for e in self.ENG}
        if not SCHED:
            return per
        for n in self.nodes:
            n.tail = 0.0
        for n in reversed(self.nodes):
            w = n.est + n.tail + (n.dma / 300e3 + 2.0 if n.dma is not None else 0.0)
            for d in n.deps:
                nd = d[1] if d[0] == "n" else (d[3] if len(d) > 3 else None)
                if nd is not None and w > nd.tail:
                    nd.tail = w
        pos = {e: 0 for e in self.ENG}
        pend = {e: list(per[e]) for e in self.ENG}
        free = {e: 0.0 for e in self.ENG}
        cur_tbl = [None]
        dma_free = [0.0]
        out = {e: [] for e in self.ENG}
        remaining = len(self.nodes)

        def dep_fin(d):
            nd = d[1] if d[0] == "n" else (d[3] if len(d) > 3 else None)
            if nd is None:
                return 0.0
            return nd.fin

        while remaining:
            best = None
            for e in self.ENG:
                lst = pend[e]
                if not lst:
                    continue
                W = SCHED_WINDOW if e in ("pe", "act", "dve") else 1
                seen = 0
                cands = []
                for n in lst:
                    if seen >= W:
                        break
                    seen += 1
                    ok = True
                    t = free[e]
                    for d in n.deps:
                        f = dep_fin(d)
                        if f is None:
                            ok = False
                            break
                        lat = 0.15 if (d[0] == "n" and d[1].eng != e) else 0.05
                        if f + lat > t:
                            t = f + lat
                    if not ok:
                        continue
                    if e == "act" and n.tbl is not None and n.tbl != cur_tbl[0]:
                        t += 1.3
                    cands.append((t, n))
                if not cands:
                    continue
                tmin = min(c[0] for c in cands)
                t, n = max((c for c in cands if c[0] <= tmin + SCHED_SLACK), key=lambda c: (c[1].tail, -c[1].idx))
                key = (t, n.idx)
                if best is None or key < best[0]:
                    best = (key, e, n)
            key, e, n = best
            t = key[0]
            if e == "act" and n.tbl is not None:
                cur_tbl[0] = n.tbl
            end = t + n.est + 0.08
            free[e] = end
            if n.dma is not None:
                st = max(end, dma_free[0])
                dma_free[0] = st + n.dma / 300e3
                n.fin = dma_free[0] + 2.0
            else:
                n.fin = end
            pend[e].remove(n)
            out[e].append(n)
            remaining -= 1
        self.makespan = max(free.values())
        return out

    def finalize(self):
        order = self.schedule()
        for e in self.ENG:
            c = 0
            for n in order[e]:
                if n.sig:
                    c += 1
                    n.cnt = c
        self.order = order

    def replay(self, name, e, sems):
        waited = {}
        for n in self.order[name]:
            for d in n.deps:
                if d[0] == "dma":
                    key = ("dma", id(d[1]))
                    if waited.get(key, 0) >= d[2]:
                        continue
                    e.wait_ge(d[1], d[2])
                    waited[key] = d[2]
                else:
                    src = d[1]
                    if waited.get(src.eng, 0) >= src.cnt:
                        continue
                    e.wait_ge(sems[src.eng], src.cnt)
                    waited[src.eng] = src.cnt
            ins = None
            for fn in n.fns:
                ins = fn(e)
            if n.sig:
                ins.then_inc(sems[name], 1)


def _free(ap):
    k = 1
    for d in ap.shape[1:]:
        k *= int(d)
    return k


def _est(fn, v, tbl=None):
    fn.est = v
    fn.tbl = tbl
    return fn


def MM(out, lhsT, rhs, start, stop):
    return _est(lambda e: e.matmul(out, lhsT=lhsT, rhs=rhs, start=start, stop=stop, skip_group_check=True),
                max(_free(rhs), 64) / 2400.0 + 0.015)


def TRP(out, in_, ident):
    return _est(lambda e: e.transpose(out=out, in_=in_, identity=ident), 0.075)


def ACTF(out, in_, func, scale=None, bias=None, accum=None):
    kw = {}
    if scale is not None:
        kw["scale"] = scale
    if bias is not None:
        kw["bias"] = bias
    if accum is not None:
        kw["accum_out"] = accum
    tbl = {AF.Exp: "exp", AF.Ln: "exp", AF.Gelu_apprx_tanh: "gelu", AF.Silu: "silu"}.get(func)
    return _est(lambda e: e.activation(out=out, in_=in_, func=func, **kw), 0.2 + _free(out) / 1050.0, tbl)


def TT(out, in0, in1, op):
    return _est(lambda e: e.tensor_tensor(out=out, in0=in0, in1=in1, op=op), 0.12 + _free(out) / 930.0)


def TS(out, in0, s1, op0, s2=None, op1=None):
    if op1 is None:
        return lambda e: e.tensor_scalar(out=out, in0=in0, scalar1=s1, scalar2=None, op0=op0)
    return lambda e: e.tensor_scalar(out=out, in0=in0, scalar1=s1, scalar2=s2, op0=op0, op1=op1)


def STT(out, in0, scalar, in1, op0, op1):
    return _est(lambda e: e.scalar_tensor_tensor(out=out, in0=in0, scalar=scalar, in1=in1, op0=op0, op1=op1), 0.12 + _free(out) / 930.0)


def RSUM(out, in_):
    return _est(lambda e: e.reduce_sum(out=out, in_=in_, axis=AX.X), 0.12 + _free(in_) / 930.0)


def RCP(out, in_):
    return _est(lambda e: e.reciprocal(out=out, in_=in_), 0.2)


def CP(out, in_):
    return _est(lambda e: e.tensor_copy(out=out, in_=in_), 0.12 + _free(out) / 1800.0)


def MSET(ap, v):
    return _est(lambda e: e.memset(ap, v), 0.05 + _free(ap) / 4000.0)


class _Stop(Exception):
    pass


MARKS = []
LAST_PROG = []


def build_program(debug=False, stop_after=None):
    nc = bass.Bass("TRN2", target_bir_lowering=False)
    dt_in = lambda name, shape: nc.dram_tensor(name, shape, F32, kind="ExternalInput").ap()
    dt_out = lambda name, shape: nc.dram_tensor(name, shape, F32, kind="ExternalOutput").ap()

    xs = dt_in("xs", [NT, 128, D])
    ck = dt_in("ck", [16, 128, 256])
    cv = dt_in("cv", [16, 128, 256])
    w_in = dt_in("w_in", [D, 3584])
    w_o = dt_in("w_o", [D, D])
    w_gate = dt_in("w_gate", [D, DFF])
    w_up = dt_in("w_up", [D, DFF])
    w_down = dt_in("w_down", [DFF, D])
    cols_d = dt_in("cols", [128, 64])
    bcv_d = dt_in("bcv", [1168])
    cs_d = dt_in("cs", [128, NT, 128])
    mk_d = dt_in("mk", [128, 2304])
    wt_d = dt_in("wt", [128, 2048])
    cm_d = dt_in("cm", [128, 256])

    y_o = dt_out("y", [9, 128, D])
    kwp_o = dt_out("kwp", [128, 256])
    vwp_o = dt_out("vwp", [128, 256])
    kws_o = dt_out("kws", [16, 120, 256])
    vws_o = dt_out("vws", [16, 120, 256])
    knew_o = dt_out("knew", [128, 256])
    vnew_o = dt_out("vnew", [128, 256])
    sgv_o = dt_out("sgv", [128, 1024])
    if debug:
        dbg_mix = nc.dram_tensor("dbg_mix", [128, 16, NG * 128], BF16, kind="ExternalOutput").ap()
        dbg_h = dt_out("dbg_h", [128, NG, D])

    P = Prog()
    es = contextlib.ExitStack()
    with es:
        def sb(name, shape, dt):
            return es.enter_context(nc.sbuf_tensor("sb_" + name, shape, dt))

        def ps(name, shape, dt):
            return es.enter_context(nc.psum_tensor("ps_" + name, shape, dt))

        def sem(name):
            return es.enter_context(nc.semaphore(name))

        NSLOT = 4
        ring = [sb(f"ring{i}", [128, 16, 512], BF16) for i in range(NSLOT)]
        xnT = sb("xnT", [128, 16, NG * 128], BF16)
        mixT = sb("mixT", [128, 16, NG * 128], BF16)
        Hreg = sb("Hreg", [128, NG * D], F32)
        hbuf = Hreg[:, :].rearrange("p (t d) -> p t d", d=D)
        Hb = Hreg[:, :].bitcast(BF16)
        kT2 = sb("kT2", [64, 4, 6, 128], BF16)
        Vaug = sb("Vaug", [128, 6, 4, 72], BF16)
        g_n = Hb[:, 0:NG * 1024].rearrange("p (t e) -> p t e", e=1024)
        X = sb("X", [128, 4096], F32)
        xsb = sb("xsb", [128, D], BF16)
        o0 = NG * 1024
        qb2 = [Hb[:, o0 + 1024 * i:o0 + 1024 * (i + 1)] for i in range(2)]
        qT2 = [Hb[0:64, o0 + 2048 + 2048 * i:o0 + 2048 + 2048 * (i + 1)].rearrange("p (h t) -> p h t", t=128) for i in range(2)]
        PT2 = [Hb[:, o0 + 6144 + 1024 * i:o0 + 6144 + 1024 * (i + 1)].rearrange("p (b e) -> p b e", e=512) for i in range(2)]
        kbd = Hb[:, o0 + 8192:o0 + 8448]
        assert (o0 + 8448) // 2 <= 7168
        S = sb("S", [128, 3328], F32)
        Sb = S[:, :].bitcast(BF16)
        PTz = Sb[:, 0:2048].rearrange("p (h t) -> p h t", t=128)
        ckd = Sb[:, 2048:2560].rearrange("p (s e) -> p s e", e=256)
        cva = Sb[:, 2560:3136].rearrange("p (s h e) -> p s h e", h=4, e=72)
        kcT2 = Sb[0:64, 3136:3648].rearrange("p (h t) -> p h t", t=128)
        kcT2b = [kcT2, Sb[0:64, 5696:6208].rearrange("p (h t) -> p h t", t=128)]
        ckf = S[:, 1824:2336].rearrange("p (s e) -> p s e", e=256)
        cvf = S[:, 2336:2848].rearrange("p (s e) -> p s e", e=256)
        cols = sb("cols", [128, 64], F32)
        bc = sb("bc", [128, 144], F32)
        cs = sb("cs", [128, NT, 128], F32)
        mk = sb("mk", [128, 2304], BF16)
        wTm = sb("wTm", [128, 2, 8, 128], BF16)
        st8 = sb("st8", [128, 64], F32)
        rall = sb("rall", [128, 64], F32)
        sinkexp = sb("sinkexp", [128, 16], F32)
        kfo = sb("kfo", [128, 2, 256], F32)
        vfo = sb("vfo", [128, 2, 256], F32)
        kfw = sb("kfw", [128, 256], F32)

        mm = [ps(f"mm{i}", [128, 512], F32) for i in range(2)]
        tp = ps("tp", [128, 1024], BF16)
        Bk = [ps(f"bk{i}", [128, 512], F32) for i in range(5)]
        stp = [Bk[0], Bk[1]]

        esem = {e: sem(f"s_{e}") for e in Prog.ENG}
        ds_ring = [[DSem(sem(f"d_ring{i}_{h}")) for h in range(2)] for i in range(NSLOT)]
        ds_xt = [DSem(sem(f"d_xt{i}")) for i in range(2)]
        ds_xr = [DSem(sem(f"d_xr{i}")) for i in range(2)]
        ds_yb = [DSem(sem(f"d_yb{i}")) for i in range(2)]
        ds_ck = DSem(sem("d_ck"))
        ds_cv = DSem(sem("d_cv"))
        ds_setup = DSem(sem("d_setup"))
        ds_setup2 = DSem(sem("d_setup2"))
        t_const2 = Tr()
        ds_out = DSem(sem("d_out"))
        ds_gout = DSem(sem("d_gout"))
        ds_gg = DSem(sem("d_gg"))
        t_gg = Tr()
        ds_dbg = DSem(sem("d_dbg"))
        ds_dbg2 = DSem(sem("d_dbg2"))

        xt = [X[:, 0:2048], X[:, 2048:4096]]
        T_sq = X[:, 0:512]
        T_qh = X[:, 512:1024]
        T_m1 = X[:, 1024:1536]
        T_m2 = X[:, 1536:2048]
        T_a = X[:, 2048:3072]
        T_g = X[:, 3072:4096]
        xr = [S[:, 0:512], S[:, 512:1024]]
        yb = [S[:, 1024:1536], S[:, 1536:2048]]
        sgt = [S[:, 2048 + 320 * i:2048 + 320 * (i + 1)] for i in range(4)]
        tpF = tp[:, :].bitcast(F32)
        ga_col = cols[:, 0:16]
        gf_col = cols[:, 16:32]
        gao_col = cols[:, 32:40]
        gso_col = cols[:, 40:48]
        bcol = [cols[:, 48:56], cols[:, 56:64]]
        gq_bc = bc[:, 0:64]
        gk_bc = bc[:, 64:128]
        sinks_bc = bc[:, 128:144]
        gg_bc = Hreg[:, 8192:9216]
        m_cur = mk[:, 0:512]
        m_prev = mk[:, 512:1024]
        m_first = mk[:, 1024:1536]
        m_news = mk[:, 1536:2048]
        m_cache = mk[:, 2048:2176]
        ident = mk[:, 2176:2304]

        t_stq, t_rq, t_stk, t_rk = Tr(), Tr(), Tr(), Tr()
        t_ksq, t_kqh, t_km1 = Tr(), Tr(), Tr()
        t_ring = [[Tr(), Tr()] for _ in range(NSLOT)]

        class SlabTr:
            def __init__(self, a, b, split):
                self.a, self.b, self.split = a, b, split

        def mm_group(fns, slab_tr, other_reads, writes):
            sp_ = slab_tr.split
            P.group("pe", fns[:sp_], reads=list(other_reads) + [slab_tr.a], writes=writes)
            return P.group("pe", fns[sp_:], reads=list(other_reads) + [slab_tr.b], writes=writes)

        t_xt = [Tr(), Tr()]
        t_xsb = Tr()
        t_xsb2 = Tr()
        t_tp = Tr()
        t_mm = [Tr(), Tr()]
        t_B = [Tr() for _ in range(5)]
        t_st = [t_B[0], t_B[1]]
        t_xnT = [Tr() for _ in range(NG)]
        t_mixA = [Tr() for _ in range(NG)]
        t_mixS = [Tr() for _ in range(NG)]
        t_act = Tr()
        t_h = [[Tr() for _ in range(4)] for _ in range(NG)]
        t_kT2 = [Tr() for _ in range(6)]
        t_V = [Tr() for _ in range(6)]
        t_gn = [Tr() for _ in range(NG)]
        t_sq, t_qh, t_m1, t_m2, t_a, t_g = Tr(), Tr(), Tr(), Tr(), Tr(), Tr()
        t_kbd, t_PTz = Tr(), Tr()
        t_qb2, t_qT2 = [Tr(), Tr()], [Tr(), Tr()]
        t_PT2 = [[Tr(), Tr()], [Tr(), Tr()]]
        t_ckf, t_cvf, t_ckd, t_cva, t_kcT2 = Tr(), Tr(), Tr(), Tr(), Tr()
        t_kcT2b = [t_kcT2, Tr()]
        t_xr, t_yb, t_sgt = [Tr(), Tr()], [Tr(), Tr()], [Tr() for _ in range(4)]
        t_ovb = [Tr(), Tr(), Tr()]
        t_const = Tr()
        t_wTm = Tr()
        t_st8 = Tr()
        t_st8b = Tr()
        t_rallb = Tr()
        t_rall = Tr()
        t_kfw = Tr()
        t_kfo, t_vfo = [Tr(), Tr()], [Tr(), Tr()]

        slab_list = []
        w_in_v = w_in.rearrange("(c p) e -> p c e", p=128)
        w_o_v = w_o.rearrange("(c p) e -> p c e", p=128)
        w_g_v = w_gate.rearrange("(c p) e -> p c e", p=128)
        w_u_v = w_up.rearrange("(c p) e -> p c e", p=128)
        w_d_v = w_down.rearrange("(c p) e -> p c e", p=128)

        def full_slab(src_v, c0):
            return [(lambda r, h=h: r[:, 8 * h:8 * h + 8, :], src_v[:, 8 * h:8 * h + 8, c0:c0 + 512]) for h in range(2)]

        for _g in range(len(GROUPS)):
            for c0 in (1024, 0, 512, 2560, 3072, 1536, 2048):
                slab_list.append(full_slab(w_in_v, c0))
            for s in range(4):
                slab_list.append(full_slab(w_o_v, s * 512))
            for (pc0, pc1) in FFN_PARTS:
                for j in range(pc0 // 4, pc1 // 4):
                    slab_list.append(full_slab(w_g_v, 512 * j))
                    slab_list.append(full_slab(w_u_v, 512 * j))
                nch = pc1 - pc0
                for s in range(4):
                    hh = nch // 2
                    slab_list.append([
                        (lambda r, hh=hh: r[:, 0:hh, :], w_d_v[:, pc0:pc0 + hh, s * 512:(s + 1) * 512]),
                        (lambda r, hh=hh, nch=nch: r[:, hh:nch, :], w_d_v[:, pc0 + hh:pc1, s * 512:(s + 1) * 512]),
                    ])
        slab_state = {"loaded": 0, "used": 0}
        slot_split = {}
        slab_split = [8 if len(e_) == 2 and e_[0][1].shape[1] == 8 else e_[0][1].shape[1] for e_ in slab_list]
        MARKS.clear()

        def load_next_slab(extra=()):
            n = slab_state["loaded"]
            if n >= len(slab_list):
                return
            slot = n % NSLOT
            same = slot_split.get(slot, slab_split[n]) == slab_split[n]
            slot_split[slot] = slab_split[n]
            for h, (dst_fn, src) in enumerate(slab_list[n]):
                wr = [t_ring[slot][h]] if same else [t_ring[slot][0], t_ring[slot][1]]
                P.dma("pool", dst_fn(ring[slot]), src, ds_ring[slot][h], writes=wr, extra=extra, nbytes=2 * 1024 * 1024)
            slab_state["loaded"] = n + 1

        def next_slab():
            n = slab_state["used"]
            slab_state["used"] = n + 1
            assert n < slab_state["loaded"]
            return ring[n % NSLOT], SlabTr(t_ring[n % NSLOT][0], t_ring[n % NSLOT][1], slab_split[n])

        def release_slab():
            load_next_slab()

        P.dma("sp", cols[:, :], cols_d, ds_setup, writes=[t_const])
        P.dma("sp", bc[:, :], bcv_d[0:144].partition_broadcast(128), ds_setup, writes=[t_const])
        P.dma("sp", cs[:, :, :], cs_d, ds_setup, writes=[t_const])
        P.dma("sp", X[:, 0:2048], wt_d, ds_setup, writes=[t_const])
        P.dma("sp", X[:, 2048:2304], cm_d, ds_setup, writes=[t_const])
        P.dma("pool", mk[:, :], mk_d, ds_setup2, writes=[t_const2])
        load_next_slab()

        P.op("dve", MSET(Vaug[:, :, :, :], 1.0), writes=t_V)
        for k in range(2):
            P.op("dve", TT(wTm[:, k, :, :], X[:, k * 1024:(k + 1) * 1024].rearrange("p (h i) -> p h i", i=128),
                           X[:, 2048 + 128 * k:2048 + 128 * (k + 1)].unsqueeze(1).to_broadcast([128, 8, 128]), ALU.mult),
                 reads=[t_const], writes=[t_wTm] + ([t_xt[0], t_xt[1]] if k == 1 else []))
        P.op("act", ACTF(sinkexp[:, :], sinks_bc, AF.Exp), reads=[t_const, t_const2])

        def rstd_from_ssq(ssq_ap, out_ap, n, width, st_tr=None, r_tr=None):
            st_tr = st_tr or t_st8
            r_tr = r_tr or t_rall
            P.op("act", ACTF(ssq_ap, ssq_ap, AF.Ln, scale=1.0 / n, bias=EPS), reads=[st_tr], writes=[st_tr])
            return P.op("act", ACTF(out_ap, ssq_ap, AF.Exp, scale=-0.5), reads=[st_tr], writes=[r_tr])

        tp_pool = [(tp, t_tp), (Bk[2][:, :].bitcast(BF16), t_B[2]), (Bk[3][:, :].bitcast(BF16), t_B[3]), (Bk[4][:, :].bitcast(BF16), t_B[4])]
        tp_rr = [0]

        def next_tp(rotate):
            if not rotate:
                return tp, t_tp
            tp_rr[0] = (tp_rr[0] + 1) % len(tp_pool)
            return tp_pool[tp_rr[0]]

        def norm_transpose(src_ap, src_tr, nchunks, dstT, dst_col0, dst_tr, gcol, c_off=0, rotate=False):
            W = nchunks * 128
            ssq = st8[:, 0:1]
            P.op("dve", MSET(ssq, 0.0), writes=[t_st8])
            P.op("act", ACTF(xsb[:, 0:W], src_ap, AF.Square, accum=ssq), reads=[src_tr], writes=[t_xsb, t_st8])
            rstd_from_ssq(ssq, rall[:, 0:1], float(W), 1)
            P.op("act", ACTF(xsb[:, 0:W], src_ap, AF.Copy, scale=rall[:, 0:1]), reads=[src_tr, t_rall], writes=[t_xsb])
            for h0 in range(0, nchunks, 8):
                tpb, tpt = next_tp(rotate)
                fns = [TRP(tpb[:, (c - h0) * 128:(c - h0 + 1) * 128], xsb[:, c * 128:(c + 1) * 128], ident)
                       for c in range(h0, h0 + 8)]
                P.group("pe", fns, reads=[t_xsb, t_const, t_const2], writes=[tpt])
                P.op("dve", TT(dstT[:, c_off + h0:c_off + h0 + 8, dst_col0:dst_col0 + 128],
                               tpb[:, :].rearrange("p (c t) -> p c t", t=128),
                               gcol[:, h0:h0 + 8].unsqueeze(2).to_broadcast([128, 8, 128]), ALU.mult),
                     reads=[tpt, t_const, t_const2], writes=[dst_tr])

        def dense_B(srcT, col0, src_tr, slab, slab_tr, nch, bank, bank_tr, ch0=0, ncols=512, sc0=0):
            fns = [MM(bank[:, 0:ncols], srcT[:, ch0 + c, col0:col0 + 128], slab[:, c, sc0:sc0 + ncols], c == 0, c == nch - 1)
                   for c in range(nch)]
            return mm_group(fns, slab_tr, [src_tr], [bank_tr])

        def qk_norm_rope(src_bank, src_tr, nh, gbc, t, out_ap, out_tr, TS_):
            W = nh * 64
            v3 = lambda ap: ap.rearrange("p (h d) -> p h d", d=64)
            sq, qh, m1 = TS_["sq"][:, 0:W], TS_["qh"][:, 0:W], TS_["m1"][:, 0:W]
            m2 = sq
            tsq, tqh, tm1, tst, tr_ = TS_["tsq"], TS_["tqh"], TS_["tm1"], TS_["tst"], TS_["tr"]
            stv, rv = TS_["st"][:, 0:nh], TS_["r"][:, 0:nh]
            P.op("act", ACTF(sq, src_bank[:, 0:W], AF.Square), reads=[src_tr], writes=[tsq])
            P.op("dve", RSUM(stv, v3(sq)), reads=[tsq], writes=[tst])
            rstd_from_ssq(stv, rv, 64.0, nh, tst, tr_)
            P.op("dve", TT(v3(qh), v3(src_bank[:, 0:W]), rv.unsqueeze(2).to_broadcast([128, nh, 64]), ALU.mult),
                 reads=[src_tr, tr_], writes=[tqh])
            P.op("dve", TT(v3(qh), v3(qh), gbc.unsqueeze(1).to_broadcast([128, nh, 64]), ALU.mult),
                 reads=[t_const, t_const2], writes=[tqh])
            csA = cs[:, t, 0:64].unsqueeze(1).to_broadcast([128, nh, 64])
            P.op("dve", TT(v3(m1), v3(qh), csA, ALU.mult), reads=[tqh, t_const, t_const2], writes=[tm1])
            P.op("dve", TT(v3(m2)[:, :, 0:32], v3(qh)[:, :, 32:64],
                           cs[:, t, 64:96].unsqueeze(1).to_broadcast([128, nh, 32]), ALU.mult),
                 reads=[tqh, t_const, t_const2], writes=[tsq])
            P.op("dve", TT(v3(m2)[:, :, 32:64], v3(qh)[:, :, 0:32],
                           cs[:, t, 96:128].unsqueeze(1).to_broadcast([128, nh, 32]), ALU.mult),
                 reads=[tqh], writes=[tsq])
            P.op("dve", TT(out_ap, m1, m2, ALU.add), reads=[tm1, tsq], writes=[out_tr])

        def head_slot(h):
            return Bk[2 + h // 7], t_B[2 + h // 7], (h % 7) * 72

        def S1_tile(t, i):
            b = i % 2
            P.dma("sp", xt[b], xs[t], ds_xt[b], writes=[t_xt[b]] + ([t_xsb2] if b == 0 else []), nbytes=1024 * 1024)
            norm_transpose(xt[b], t_xt[b], 16, xnT, i * 128, t_xnT[i], ga_col, rotate=True)

        def run_group(gi, kv_tiles, ctiles, next_kv=None, s1_done=0):
            nct = len(ctiles)
            xcol = {t: (i * 128) for i, t in enumerate(kv_tiles)}
            if kv_tiles[0] != ctiles[0]:
                xblk = {t: i for i, t in enumerate(kv_tiles)}
            else:
                xblk = {t: i for i, t in enumerate(kv_tiles)}
            li_of = {t: i for i, t in enumerate(ctiles)}

            QS = dict(sq=T_sq, qh=T_qh, m1=T_m1, tsq=t_sq, tqh=t_qh, tm1=t_m1,
                      st=st8[:, 32:40], r=rall[:, 32:40], tst=t_stq, tr=t_rq)
            KS = dict(sq=Hreg[:, 7168:7424], qh=Hreg[:, 7424:7680], m1=Hreg[:, 7680:7936], tsq=t_ksq, tqh=t_kqh, tm1=t_km1,
                      st=st8[:, 48:52], r=rall[:, 48:52], tst=t_stk, tr=t_rk)

            def S1(i):
                if i >= s1_done:
                    S1_tile(kv_tiles[i], i)

            slab, slab_tr = next_slab()

            def S2mm(i):
                t = kv_tiles[i]
                dense_B(xnT, xcol[t], t_xnT[xblk[t]], slab, slab_tr, 16, mm[i % 2], t_mm[i % 2])

            def S2post(i):
                t = kv_tiles[i]
                bk = i % 2
                if t == 8:
                    kf, kf_tr, vf, vf_tr = kfo[:, 0, :], t_kfo[0], vfo[:, 0, :], t_vfo[0]
                elif t == 9:
                    kf, kf_tr, vf, vf_tr = kfo[:, 1, :], t_kfo[1], vfo[:, 1, :], t_vfo[1]
                else:
                    kf, kf_tr, vf, vf_tr = kfw[:, :], t_kfw, None, None
                qk_norm_rope(mm[bk], t_mm[bk], 4, gk_bc, t, kf, kf_tr, KS)
                P.op("act", ACTF(kbd, kf, AF.Copy), reads=[kf_tr], writes=[t_kbd])
                P.op("act", ACTF(Vaug[:, t % 6, :, 0:64], mm[bk][:, 256:512].rearrange("p (h d) -> p h d", d=64), AF.Copy),
                     reads=[t_mm[bk]], writes=[t_V[t % 6]])
                if vf is not None:
                    P.op("act", ACTF(vf, mm[bk][:, 256:512], AF.Copy), reads=[t_mm[bk]], writes=[vf_tr])
                    if t == 8:
                        P.dma("sp", kwp_o, kf, ds_out, reads=[kf_tr])
                        P.dma("sp", vwp_o, vf, ds_out, reads=[vf_tr])
                    else:
                        P.dma("sp", knew_o, kf, ds_out, reads=[kf_tr])
                        P.dma("sp", vnew_o, vf, ds_out, reads=[vf_tr])
                fns = [TRP(tp[0:64, h * 128:(h + 1) * 128], kbd[:, h * 64:(h + 1) * 64], ident) for h in range(4)]
                P.group("pe", fns, reads=[t_kbd, t_const, t_const2], writes=[t_tp])
                P.op("dve", CP(kT2[:, :, t % 6, :], tp[0:64, 0:512].rearrange("p (h t) -> p h t", t=128)),
                     reads=[t_tp], writes=[t_kT2[t % 6]])

            nkv = len(kv_tiles)
            S1(0)
            if nkv > 1:
                S1(1)
            if gi == 0:
                for _ in range(NSLOT - 1):
                    load_next_slab(extra=[("dma", ds_xt[0].sem, 16)])
            S2mm(0)
            for i in range(nkv):
                if i + 2 < nkv:
                    S1(i + 2)
                if i + 1 < nkv:
                    S2mm(i + 1)
                S2post(i)
            release_slab()

            MARKS.append((gi, 1, len(P.ops['pe'])))
            MARKS.append((gi, 2, len(P.ops['pe'])))
            if stop_after in (1, 2):
                raise _Stop()
            s0, s0_tr = next_slab()
            s1, s1_tr = next_slab()

            def A3(i):
                t = ctiles[i]
                for hq, (sl, sl_tr) in enumerate(((s0, s0_tr), (s1, s1_tr))):
                    dense_B(xnT, xcol[t], t_xnT[xblk[t]], sl, sl_tr, 16, mm[hq], t_mm[hq])

            def B3_chain(i, hq):
                t = ctiles[i]
                par = i % 2
                qk_norm_rope(mm[hq], t_mm[hq], 8, gq_bc, t, qb2[par][:, hq * 512:(hq + 1) * 512], t_qb2[par], QS)

            def B3_tr(i, hq):
                par = i % 2
                fns = [TRP(tp[0:64, c * 128:(c + 1) * 128], qb2[par][:, hq * 512 + c * 64:hq * 512 + (c + 1) * 64], ident) for c in range(8)]
                P.group("pe", fns, reads=[t_qb2[par], t_const, t_const2], writes=[t_tp])
                P.op("dve", CP(qT2[par][:, hq * 8:(hq + 1) * 8, :], tp[0:64, 0:1024].rearrange("p (c t) -> p c t", t=128)),
                     reads=[t_tp], writes=[t_qT2[par]])

            def blocks_of(t):
                if t == 9:
                    return [(9 % 6, m_news)]
                if t == 1:
                    return [(0, m_first), (1, m_cur)]
                return [((t - 1) % 6, m_prev), (t % 6, m_cur)]

            def C3_ST(i, kvh):
                t = ctiles[i]
                par = i % 2
                qT_, tqT_ = qT2[par], t_qT2[par]
                for bi, (slot, mask) in enumerate(blocks_of(t)):
                    bidx = (kvh % 2) * 2 + bi if t != 9 else 0
                    bank, btr = Bk[bidx], t_B[bidx]
                    fns = [MM(bank[:, :], ident, mask, True, False)]
                    for g in range(4):
                        h = 4 * kvh + g
                        fns.append(MM(bank[:, g * 128:(g + 1) * 128], kT2[:, kvh, slot, :], qT_[:, h, :], False, g == 3))
                    P.group("pe", fns, reads=[t_kT2[slot], tqT_, t_const, t_const2], writes=[btr])
                    pp = kvh % 2
                    P.op("act", ACTF(PT2[pp][:, bi, :], bank[:, :], AF.Exp, scale=0.125), reads=[btr], writes=[t_PT2[pp][bi]])

            def C3_PV(i, kvh):
                t = ctiles[i]
                blks = blocks_of(t)
                pp = kvh % 2
                fns = []
                wr = set()
                for g in range(4):
                    h = 4 * kvh + g
                    if t == 9:
                        bank, btr, c0 = head_slot(h)
                        first = (h % 7 == 0)
                    else:
                        bank, btr, c0 = Bk[4], t_B[4], g * 72
                        first = (g == 0)
                    wr.add(btr)
                    for bi, (slot, mask) in enumerate(blks):
                        last = (bi == len(blks) - 1) and (t != 9)
                        fns.append(MM(bank[:, c0:c0 + 72], PT2[pp][:, bi, g * 128:(g + 1) * 128], Vaug[:, slot, kvh, :],
                                      bi == 0 and first, last))
                P.group("pe", fns, reads=[t_PT2[pp][0], t_PT2[pp][1]] + [t_V[s_] for s_, _ in blks], writes=list(wr))
                if t != 9:
                    v = Bk[4][:, 0:288].rearrange("p (h e) -> p h e", e=72)
                    P.op("act", ACTF(T_g[:, kvh * 256:(kvh + 1) * 256].rearrange("p (h d) -> p h d", d=64), v[:, :, 0:64], AF.Copy),
                         reads=[t_B[4]], writes=[t_g])
                    P.op("act", ACTF(st8[:, 16 + 4 * kvh:20 + 4 * kvh], v[:, :, 64], AF.Copy), reads=[t_B[4]], writes=[t_st8b])
                    if kvh == 3:
                        P.op("dve", TT(st8[:, 16:32], st8[:, 16:32], sinkexp[:, 0:16], ALU.add), reads=[t_st8b], writes=[t_st8b])
                        P.op("dve", RCP(rall[:, 16:32], st8[:, 16:32]), reads=[t_st8b], writes=[t_rallb])
                        P.op("dve", TT(T_a.rearrange("p (h d) -> p h d", d=64), T_g.rearrange("p (h d) -> p h d", d=64),
                                       rall[:, 16:32].unsqueeze(2).to_broadcast([128, 16, 64]), ALU.mult),
                             reads=[t_g, t_rallb], writes=[t_a])

            def evac(bank, btr, nh, h0):
                v = bank[:, 0:nh * 72].rearrange("p (h e) -> p h e", e=72)
                P.op("dve", TT(st8[:, 16:16 + nh], v[:, :, 64], sinkexp[:, h0:h0 + nh], ALU.add), reads=[btr], writes=[t_st8b])
                P.op("dve", RCP(rall[:, 16:16 + nh], st8[:, 16:16 + nh]), reads=[t_st8b], writes=[t_rallb])
                P.op("dve", TT(T_a[:, h0 * 64:(h0 + nh) * 64].rearrange("p (h d) -> p h d", d=64), v[:, :, 0:64],
                               rall[:, 16:16 + nh].unsqueeze(2).to_broadcast([128, nh, 64]), ALU.mult),
                     reads=[btr, t_rallb], writes=[t_a])

            def C3_sample_cache(i):
                par = i % 2
                qT_, tqT_ = qT2[par], t_qT2[par]
                P.op("dve", MSET(cva, 1.0), writes=[t_cva])
                P.op("dve", MSET(PTz, 0.0), writes=[t_PTz])

                def load(sg):
                    P.dma("sp", ckf, ck[2 * sg:2 * sg + 2].rearrange("s k e -> k s e"), ds_ck, writes=[t_ckf])
                    P.dma("sp", cvf, cv[2 * sg:2 * sg + 2].rearrange("s k e -> k s e"), ds_cv, writes=[t_cvf])

                def cast(sg):
                    P.op("act", ACTF(ckd, ckf, AF.Copy), reads=[t_ckf], writes=[t_ckd])

                def castv(sg):
                    P.op("dve", CP(cva[:, :, :, 0:64], cvf.rearrange("p s (h d) -> p s h d", d=64)),
                         reads=[t_cvf], writes=[t_cva])

                def TR(b):
                    s_ = b % 2
                    fns = [TRP(tp[0:64, h * 128:(h + 1) * 128], ckd[:, s_, h * 64:(h + 1) * 64], ident) for h in range(4)]
                    P.group("pe", fns, reads=[t_ckd, t_const, t_const2], writes=[t_tp])
                    P.op("dve", CP(kcT2b[b % 2], tp[0:64, 0:512].rearrange("p (h t) -> p h t", t=128)), reads=[t_tp], writes=[t_kcT2b[b % 2]])

                def ST(b):
                    kc, kct = kcT2b[b % 2], t_kcT2b[b % 2]
                    fns = [MM(Bk[1][:, 0:128], ident, m_cache, True, False)]
                    for h in range(16):
                        fns.append(MM(Bk[1][:, h * 8:(h + 1) * 8], kc[:, h // 4, :], qT_[:, h, b * 8:(b + 1) * 8], False, h == 15))
                    P.group("pe", fns, reads=[kct, tqT_, t_const, t_const2], writes=[t_B[1]])
                    P.op("act", ACTF(PTz[:, :, b * 8:(b + 1) * 8], Bk[1][:, 0:128].rearrange("p (h i) -> p h i", i=8), AF.Exp, scale=0.125),
                         reads=[t_B[1]], writes=[t_PTz])

                def PV(b):
                    s_ = b % 2
                    fns = []
                    for h in range(16):
                        bank, btr, c0 = head_slot(h)
                        fns.append(MM(bank[:, c0:c0 + 72], PTz[:, h, :], cva[:, s_, h // 4, :], False, b == 15))
                    P.group("pe", fns, reads=[t_PTz, t_cva], writes=[t_B[2], t_B[3], t_B[4]])
                    P.op("act", MSET_ACT(PTz[:, :, b * 8:(b + 1) * 8]), reads=[], writes=[t_PTz])

                load(0); cast(0); castv(0)
                TR(0)
                for b in range(16):
                    ST(b)
                    if b % 2 == 0:
                        TR(b + 1)
                    PV(b)
                    if b % 2 == 1 and b + 1 < 16:
                        load((b + 1) // 2); cast((b + 1) // 2); castv((b + 1) // 2)
                        TR(b + 1)
                for bnk in range(3):
                    nh = 7 if bnk < 2 else 2
                    evac(Bk[2 + bnk], t_B[2 + bnk], nh, 7 * bnk)

            def C3_fin(i):
                li = i
                norm_transpose(T_a, t_a, 8, mixT, li * 128, t_mixA[li], gao_col, c_off=0)

            A3(0)
            B3_chain(0, 0); B3_tr(0, 0); B3_chain(0, 1); B3_tr(0, 1)
            if nct > 1:
                A3(1)
            for i in range(nct):
                nxt = i + 1 < nct
                t = ctiles[i]
                C3_ST(i, 0)
                C3_ST(i, 1)
                if nxt:
                    B3_chain(i + 1, 0)
                C3_PV(i, 0)
                C3_ST(i, 2)
                C3_PV(i, 1)
                C3_ST(i, 3)
                if nxt:
                    B3_tr(i + 1, 0)
                    B3_chain(i + 1, 1)
                C3_PV(i, 2)
                C3_PV(i, 3)
                if nxt:
                    B3_tr(i + 1, 1)
                    if i + 2 < nct:
                        A3(i + 2)
                if t == 9:
                    C3_sample_cache(i)
                C3_fin(i)
            release_slab()
            release_slab()

            MARKS.append((gi, 3, len(P.ops['pe'])))
            if stop_after == 3:
                raise _Stop()
            P.dma("sp", gg_bc, bcv_d[144:1168].partition_broadcast(128), ds_gg, writes=[t_gg])
            s0, s0_tr = next_slab()
            s1, s1_tr = next_slab()
            for t in ctiles:
                li = li_of[t]
                for hq, (sl, sl_tr) in enumerate(((s0, s0_tr), (s1, s1_tr))):
                    dense_B(xnT, xcol[t], t_xnT[xblk[t]], sl, sl_tr, 16, mm[hq], t_mm[hq])
                for hq in range(2):
                    P.op("act", ACTF(T_g[:, hq * 512:(hq + 1) * 512], mm[hq][:, :], AF.Gelu_apprx_tanh), reads=[t_mm[hq]], writes=[t_g])
                ssq = st8[:, 0:1]
                P.op("dve", MSET(ssq, 0.0), writes=[t_st8])
                P.op("act", ACTF(xsb[:, 0:1024], T_g, AF.Square, accum=ssq), reads=[t_g], writes=[t_xsb, t_st8])
                rstd_from_ssq(ssq, rall[:, 0:1], 1024.0, 1)
                P.op("dve", STT(g_n[:, li, :], T_g, rall[:, 0:1], gg_bc, ALU.mult, ALU.mult), reads=[t_g, t_rall, t_gg], writes=[t_gn[li]])
                if t == 9:
                    P.op("dve", STT(T_a, T_g, rall[:, 0:1], gg_bc, ALU.mult, ALU.mult), reads=[t_g, t_rall, t_gg], writes=[t_a])
                    P.dma("sp", sgv_o, T_a, ds_gout, reads=[t_a])
            release_slab()
            release_slab()

            MARKS.append((gi, 4, len(P.ops['pe'])))
            if stop_after == 4:
                raise _Stop()
            s0, s0_tr = next_slab()
            s1, s1_tr = next_slab()

            def A5(i):
                t = ctiles[i]
                for hq, (sl, sl_tr) in enumerate(((s0, s0_tr), (s1, s1_tr))):
                    dense_B(xnT, xcol[t], t_xnT[xblk[t]], sl, sl_tr, 16, mm[hq], t_mm[hq])

            A5(0)
            for i, t in enumerate(ctiles):
                li = li_of[t]
                kk = 1 if t == 9 else 0
                for hq in range(2):
                    P.op("act", ACTF(T_g[:, hq * 512:(hq + 1) * 512], mm[hq][:, :], AF.Gelu_apprx_tanh), reads=[t_mm[hq]], writes=[t_g])
                for hq in range(2):
                    fns = [MM(stp[hq][:, j * 128:(j + 1) * 128], wTm[:, kk, hq * 4 + j, :],
                              g_n[:, li, (hq * 4 + j) * 128:(hq * 4 + j + 1) * 128], True, True) for j in range(4)]
                    P.group("pe", fns, reads=[t_gn[li], t_wTm], writes=[t_st[hq]])
                if i + 1 < nct:
                    A5(i + 1)
                for hd in range(8):
                    P.op("dve", STT(T_a[:, hd * 128:(hd + 1) * 128], stp[hd // 4][:, (hd % 4) * 128:(hd % 4 + 1) * 128],
                                    bcol[kk][:, hd:hd + 1], T_g[:, hd * 128:(hd + 1) * 128], ALU.add, ALU.mult),
                         reads=[t_st[hd // 4], t_g, t_const, t_const2], writes=[t_a])
                norm_transpose(T_a, t_a, 8, mixT, li * 128, t_mixS[li], gso_col, c_off=8)
            release_slab()
            release_slab()

            MARKS.append((gi, 5, len(P.ops['pe'])))
            if stop_after == 5:
                raise _Stop()
            if debug and gi == 0:
                P.dma("sp", dbg_mix, mixT[:, :, :], ds_dbg, reads=t_mixA[0:nct] + t_mixS[0:nct])
            if gi == 0:
                P.dma("sp", kws_o, ck[:, 8:128, :], ds_out)
                P.dma("sp", vws_o, cv[:, 8:128, :], ds_out)
            P.barrier(engines=("pe", "act", "dve", "sp"))
            for s in range(4):
                slab, slab_tr = next_slab()
                for t in ctiles:
                    li = li_of[t]
                    bk = li % 2
                    fns = [MM(mm[bk][:, :], mixT[:, c, li * 128:(li + 1) * 128], slab[:, c, :], c == 0, c == 15) for c in range(16)]
                    mm_group(fns, slab_tr, [t_mixA[li], t_mixS[li]], [t_mm[bk]])
                    P.dma("sp", xr[bk], xs[t][:, s * 512:(s + 1) * 512], ds_xr[bk], writes=[t_xr[bk]])
                    P.op("dve", TT(hbuf[:, li, s * 512:(s + 1) * 512], mm[bk][:, :], xr[bk], ALU.add),
                         reads=[t_mm[bk], t_xr[bk]], writes=[t_h[li][s]])
                    if s == 3 and not debug:
                        hn_part1(li)
                        if li > 0:
                            hn_part2(li - 1)
                if s == 3 and not debug:
                    hn_part2(nct - 1)
                release_slab()
            if debug and gi == 0:
                P.dma("sp", dbg_h, hbuf[:, :, :], ds_dbg2, reads=[x for l in t_h for x in l])
            if debug:
                for t in ctiles:
                    norm_transpose_h(li_of[t])

            MARKS.append((gi, 6, len(P.ops['pe'])))
            if stop_after == 6:
                raise _Stop()
            ntok = nct * 128
            t_act.w = None
            t_act.r = [P.last["pe"]] + [tok for tr in (t_mixA + t_mixS) for tok in tr.r]
            banks = [mm[0], mm[1], Bk[0], Bk[1], Bk[2], Bk[3], Bk[4], tpF]
            t_banks = [t_mm[0], t_mm[1]] + t_B + [t_tp]
            chunks = [(0, ntok)] if ntok <= 512 else [(0, ntok // 2), (ntok // 2, ntok)]
            nck = len(chunks)
            wd_cnt = 0
            for pi, (pc0, pc1) in enumerate(FFN_PARTS):
                nch = pc1 - pc0
                for j in range(nch // 4):
                    slabG, slabG_tr = next_slab()
                    slabU, slabU_tr = next_slab()
                    for fc in range(4):
                        fcl = 4 * j + fc
                        st_ = fcl % 2
                        for ci, (ca, cb) in enumerate(chunks):
                            bg = (st_ * nck + ci) * 2
                            bu_ = bg + 1
                            n = cb - ca
                            fg = [MM(banks[bg][:, 0:n], slabG[:, c, fc * 128:(fc + 1) * 128], xnT[:, c, ca:cb], c == 0, c == 15) for c in range(16)]
                            mm_group(fg, slabG_tr, t_xnT[0:nct], [t_banks[bg]])
                            fu = [MM(banks[bu_][:, 0:n], slabU[:, c, fc * 128:(fc + 1) * 128], xnT[:, c, ca:cb], c == 0, c == 15) for c in range(16)]
                            mm_group(fu, slabU_tr, t_xnT[0:nct], [t_banks[bu_]])
                            si = st_ * 2 + ci
                            sg_ap = S[:, 2048 + 640 * st_:2048 + 640 * st_ + n] if nck == 1 else sgt[si][:, 0:n]
                            P.op("act", ACTF(sg_ap, banks[bg][:, 0:n], AF.Silu), reads=[t_banks[bg]], writes=[t_sgt[si]])
                            P.op("dve", TT(mixT[:, fcl, ca:cb], sg_ap, banks[bu_][:, 0:n], ALU.mult),
                                 reads=[t_sgt[si], t_banks[bu_]], writes=[t_act])
                    release_slab()
                    release_slab()
                last_part = pi == len(FFN_PARTS) - 1
                if last_part and next_kv is not None and not debug:
                    S1_tile(next_kv[0], 0)
                    S1_tile(next_kv[1], 1)
                for s in range(4):
                    slab, slab_tr = next_slab()
                    for t in ctiles:
                        li = li_of[t]
                        bi_ = wd_cnt % 8
                        wd_cnt += 1
                        bk = li % 2
                        fns = [MM(banks[bi_][:, :], mixT[:, c, li * 128:(li + 1) * 128], slab[:, c, :], c == 0, c == nch - 1) for c in range(nch)]
                        mm_group(fns, slab_tr, [t_act], [t_banks[bi_]])
                        if not last_part:
                            P.op("dve", TT(hbuf[:, li, s * 512:(s + 1) * 512], banks[bi_][:, :], hbuf[:, li, s * 512:(s + 1) * 512], ALU.add),
                                 reads=[t_banks[bi_]], writes=[t_h[li][s]])
                        else:
                            P.op("dve", TT(yb[bk], banks[bi_][:, :], hbuf[:, li, s * 512:(s + 1) * 512], ALU.add),
                                 reads=[t_banks[bi_], t_h[li][s]], writes=[t_yb[bk]])
                            P.dma("sp", y_o[t - 1][:, s * 512:(s + 1) * 512], yb[bk], ds_yb[bk], reads=[t_yb[bk]])
                    release_slab()

        xsbs = [xsb, X[:, 0:1024].bitcast(BF16)]
        t_xsbs = [t_xsb, t_xsb2]

        def hn_part1(li):
            b = li % 2
            ssq = st8[:, 0:1]
            src = hbuf[:, li, :]
            P.op("dve", MSET(ssq, 0.0), writes=[t_st8])
            P.op("act", ACTF(xsbs[b][:, :], src, AF.Square, accum=ssq), reads=t_h[li], writes=[t_xsbs[b], t_st8])
            rstd_from_ssq(ssq, rall[:, 0:1], float(D), 1)
            P.op("act", ACTF(xsbs[b][:, :], src, AF.Copy, scale=rall[:, 0:1]), reads=t_h[li] + [t_rall], writes=[t_xsbs[b]])

        def hn_part2(li):
            b = li % 2
            for h0 in (0, 8):
                tpb, tpt = next_tp(True)
                fns = [TRP(tpb[:, (c - h0) * 128:(c - h0 + 1) * 128], xsbs[b][:, c * 128:(c + 1) * 128], ident) for c in range(h0, h0 + 8)]
                P.group("pe", fns, reads=[t_xsbs[b], t_const, t_const2], writes=[tpt])
                P.op("dve", TT(xnT[:, h0:h0 + 8, li * 128:(li + 1) * 128], tpb[:, :].rearrange("p (c t) -> p c t", t=128),
                               gf_col[:, h0:h0 + 8].unsqueeze(2).to_broadcast([128, 8, 128]), ALU.mult),
                     reads=[tpt, t_const, t_const2], writes=[t_xnT[li]])

        def norm_transpose_h(li):
            hn_part1(li)
            hn_part2(li)

        def MSET_ACT(ap):
            return _est(lambda e: e.activation(out=ap, in_=ap, func=AF.Copy, scale=0.0), 0.25)


        try:
            if stop_after == 0:
                raise _Stop()
            for gi, (kv_tiles, ctiles) in enumerate(GROUPS):
                if gi > 0:
                    P.barrier(extra=[d.last for d in (ds_yb + ds_xr + ds_xt) if d.last is not None])
                nxt = GROUPS[gi + 1][0] if gi + 1 < len(GROUPS) else None
                run_group(gi, kv_tiles, ctiles, next_kv=nxt, s1_done=(2 if (gi > 0 and not debug) else 0))
                if stop_after == 7:
                    raise _Stop()
        except _Stop:
            pass

        final_deps = [d.last for d in (ds_out, ds_gout, ds_yb[0], ds_yb[1], ds_dbg, ds_dbg2) if d.last is not None]
        P.final_wait("sp", (lambda e: e.nop()), final_deps)
        P.finalize()
        LAST_PROG.clear()
        LAST_PROG.append(P)

        with nc.Block() as block:
            @block.tensor
            def _(e):
                P.replay("pe", e, esem)

            @block.scalar
            def _(e):
                P.replay("act", e, esem)

            @block.vector
            def _(e):
                P.replay("dve", e, esem)

            @block.gpsimd
            def _(e):
                P.replay("pool", e, esem)

            @block.sync
            def _(e):
                P.replay("sp", e, esem)
    return nc


def _rope_tables():
    half = 32
    inv = 10000.0 ** (-np.arange(half, dtype=np.float64) / float(half))
    out = np.zeros((8, 128, NT, 128), np.float32)
    for c in range(8):
        hf = c % 2
        for t in range(NT):
            if t == 9:
                pos = (16384 + (np.arange(128) % 8)).astype(np.float64)
            else:
                pos = (hf * 1024 + (t - 1) * 128 + np.arange(128)).astype(np.float64)
            ang = pos[:, None] * inv[None, :]
            cos = np.cos(ang).astype(np.float32)
            sin = np.sin(ang).astype(np.float32)
            out[c, :, t, 0:32] = cos
            out[c, :, t, 32:64] = cos
            out[c, :, t, 64:96] = -sin
            out[c, :, t, 96:128] = sin
    return out


def _masks(first_valid):
    j = np.arange(128)[:, None]
    i = np.arange(128)[None, :]
    m_cur = np.where(j <= i, 0.0, NEG).astype(np.float32)
    m_prev = np.where(j > i, 0.0, NEG).astype(np.float32)
    m_first = m_prev if first_valid else np.full((128, 128), NEG, np.float32)
    bj, jj = j // 8, j % 8
    bi, ii = i // 8, i % 8
    m_news = np.where((bj == bi) & (jj <= ii), 0.0, NEG).astype(np.float32)
    icol = (np.arange(128) % 8)[None, :]
    m_cache = np.where(j > icol, 0.0, NEG).astype(np.float32)
    ident = np.eye(128, dtype=np.float32)
    return np.concatenate([np.tile(m_cur, (1, 4)), np.tile(m_prev, (1, 4)), np.tile(m_first, (1, 4)),
                           np.tile(m_news, (1, 4)), m_cache, ident], axis=1)


_NC_CACHE = {}


def kernel(x_prompt, x_sample, cache_k_win, cache_v_win, attn_norm, w_in, q_norm, k_norm,
           sinks, sg_norm, sg_w, sg_b, attn_out_norm, sg_out_norm, w_o, ffn_norm,
           w_gate, w_up, w_down):
    f = lambda a: np.ascontiguousarray(np.asarray(a, dtype=np.float32))
    x_prompt, x_sample = f(x_prompt), f(x_sample)
    ck_all = f(cache_k_win)[0].reshape(128, 128, 256)
    cv_all = f(cache_v_win)[0].reshape(128, 128, 256)
    w_in_, w_o_, w_g_, w_u_, w_d_ = f(w_in)[0], f(w_o)[0], f(w_gate)[0], f(w_up)[0], f(w_down)[0]
    colf = lambda v, n: f(v)[0].reshape(n, 128).T
    sgb = f(sg_b)[0]
    bcol = sgb.T
    bcol_s = np.tile(sgb[:, :8].T, (16, 1))
    cols = np.ascontiguousarray(np.concatenate(
        [colf(attn_norm, 16), colf(ffn_norm, 16), colf(attn_out_norm, 8), colf(sg_out_norm, 8), bcol, bcol_s], axis=1))
    bcv = np.ascontiguousarray(np.concatenate([f(q_norm)[0], f(k_norm)[0], f(sinks)[0], f(sg_norm)[0]]))
    sgw = f(sg_w)[0]
    wT = np.ascontiguousarray(sgw.transpose(2, 0, 1)).reshape(128, 1024)
    wT_s = np.zeros((128, 8, 128), np.float32)
    blk = sgw[:, :8, :8].transpose(2, 0, 1)
    for b in range(16):
        wT_s[b * 8:(b + 1) * 8, :, b * 8:(b + 1) * 8] = blk
    wt = np.ascontiguousarray(np.concatenate([wT, wT_s.reshape(128, 1024)], axis=1))
    jj = np.arange(128)[:, None]
    ii = np.arange(128)[None, :]
    cmask = (jj <= ii).astype(np.float32)
    cmask_s = ((jj // 8 == ii // 8) & (jj % 8 <= ii % 8)).astype(np.float32)
    cm = np.ascontiguousarray(np.concatenate([cmask, cmask_s], axis=1))
    rope = _rope_tables()

    in_maps = []
    for c in range(8):
        b, hf = c // 2, c % 2
        xs = np.zeros((NT, 128, D), np.float32)
        if hf == 1:
            xs[0] = x_prompt[b, 896:1024]
        xs[1:9] = x_prompt[b, hf * 1024:(hf + 1) * 1024].reshape(8, 128, D)
        xs[9] = x_sample[16 * c:16 * c + 16].reshape(128, D)
        in_maps.append({
            "xs": xs, "ck": np.ascontiguousarray(ck_all[16 * c:16 * c + 16]), "cv": np.ascontiguousarray(cv_all[16 * c:16 * c + 16]),
            "w_in": w_in_, "w_o": w_o_, "w_gate": w_g_, "w_up": w_u_, "w_down": w_d_,
            "cols": cols, "bcv": bcv, "cs": np.ascontiguousarray(rope[c]), "mk": np.ascontiguousarray(_masks(hf == 1)),
            "wt": wt, "cm": cm,
        })
    if "nc" not in _NC_CACHE:
        _NC_CACHE["nc"] = build_program()
    res = run_bass_kernel_spmd(_NC_CACHE["nc"], in_maps, core_ids=list(range(8)))
    R = res.results
    y_prompt = np.zeros((4, 2048, D), np.float32)
    y_sample = np.zeros((128, 8, D), np.float32)
    kwp = np.zeros((1, 4, 128, 4, 64), np.float32)
    vwp = np.zeros((1, 4, 128, 4, 64), np.float32)
    kws = np.zeros((1, 128, 128, 4, 64), np.float32)
    vws = np.zeros((1, 128, 128, 4, 64), np.float32)
    sgv = np.zeros((1, 128, 8, 1024), np.float32)
    for c in range(8):
        b, hf = c // 2, c % 2
        y = np.asarray(R[c]["y"])
        y_prompt[b, hf * 1024:(hf + 1) * 1024] = y[0:8].reshape(1024, D)
        y_sample[16 * c:16 * c + 16] = y[8].reshape(16, 8, D)
        if hf == 1:
            kwp[0, b] = np.asarray(R[c]["kwp"]).reshape(128, 4, 64)
            vwp[0, b] = np.asarray(R[c]["vwp"]).reshape(128, 4, 64)
        kws[0, 16 * c:16 * c + 16, 0:120] = np.asarray(R[c]["kws"]).reshape(16, 120, 4, 64)
        vws[0, 16 * c:16 * c + 16, 0:120] = np.asarray(R[c]["vws"]).reshape(16, 120, 4, 64)
        kws[0, 16 * c:16 * c + 16, 120:128] = np.asarray(R[c]["knew"]).reshape(16, 8, 4, 64)
        vws[0, 16 * c:16 * c + 16, 120:128] = np.asarray(R[c]["vnew"]).reshape(16, 8, 4, 64)
        sgv[0, 16 * c:16 * c + 16] = np.asarray(R[c]["sgv"]).reshape(16, 8, 1024)
    return (y_prompt, y_sample, kwp, vwp, kws, vws, sgv)
```
